# Optimizing a Trainium2 kernel written in Bass

```python
import math
import jax, jax.numpy as jnp
from jax import lax
import numpy as np

D_MODEL = 1024
BATCH = 2
SEQ = 8192
DEPTH = 1

D_MIX = D_MODEL
A_HEADS = 4
A_HEAD_DIM = 128
A_WIDTH = A_HEADS * A_HEAD_DIM
SGU_CHUNK = 128
B_HEADS = 4
B_HEAD_V = 128
B_HEAD_K = 64
B_WIDTH_V = B_HEADS * B_HEAD_V
B_WIDTH_K = B_HEADS * B_HEAD_K
GATE_RANK = 16
GATE_TAU = 16.0
GLA_CHUNK = 64
OFF_U = 0
OFF_V = OFF_U + A_WIDTH
OFF_Q = OFF_V + A_WIDTH
OFF_K = OFF_Q + B_WIDTH_K
OFF_BV = OFF_K + B_WIDTH_K
OFF_G = OFF_BV + B_WIDTH_V
OFF_LR = OFF_G + B_WIDTH_V
D_IN = OFF_LR + GATE_RANK
D_FF = 2816
CONV_K = 3
EPS = 1e-6

kernel_name = "hybrid_sgu_gla_convffn_block"


def rms_norm(x, g):
    xf = x.astype(jnp.float32)
    y = xf * lax.rsqrt(jnp.mean(xf * xf, axis=-1, keepdims=True) + EPS)
    return (y * g.astype(jnp.float32)).astype(x.dtype)


def layer_norm(x, g, b):
    xf = x.astype(jnp.float32)
    mu = jnp.mean(xf, axis=-1, keepdims=True)
    var = jnp.mean(jnp.square(xf - mu), axis=-1, keepdims=True)
    y = (xf - mu) * lax.rsqrt(var + EPS)
    return (y * g.astype(jnp.float32) + b.astype(jnp.float32)).astype(x.dtype)


def sgu_mixer(u, v, ln_g, ln_b, w_s, b_s):
    bsz, s, _ = v.shape
    v = layer_norm(v, ln_g, ln_b)
    vc = v.reshape(bsz, s // SGU_CHUNK, SGU_CHUNK, A_HEADS, A_HEAD_DIM)
    tril = jnp.tril(jnp.ones((SGU_CHUNK, SGU_CHUNK), dtype=w_s.dtype))
    w_m = w_s * tril[None]
    z = jnp.einsum('hts,bcshd->bcthd', w_m, vc)
    z = z + jnp.transpose(b_s)[None, None, :, :, None]
    return u * z.reshape(bsz, s, A_WIDTH)


def gla_mixer(q, k, v, gk):
    bsz, s, h, dk = q.shape
    dv = v.shape[-1]
    n = s // GLA_CHUNK
    q = q * (dk ** -0.5)

    def to_chunks(t):
        t = t.reshape(bsz, n, GLA_CHUNK, h, t.shape[-1])
        return jnp.moveaxis(t, 1, 0)

    qs, ks, vs, gs = to_chunks(q), to_chunks(k), to_chunks(v), to_chunks(gk.astype(jnp.float32))
    mask = jnp.tril(jnp.ones((GLA_CHUNK, GLA_CHUNK), dtype=bool))[None, :, :, None, None]

    def step(state, inp):
        qc, kc, vc, gc = inp
        bcum = jnp.cumsum(gc, axis=1)
        b_last = bcum[:, -1]
        o_inter = jnp.einsum('bihk,bhkv->bihv', qc * jnp.exp(bcum), state)
        diff = bcum[:, :, None] - bcum[:, None, :]
        decay = jnp.exp(jnp.where(mask, diff, -jnp.inf))
        attn = jnp.einsum('bihk,bjhk,bijhk->bhij', qc, kc, decay)
        o_intra = jnp.einsum('bhij,bjhv->bihv', attn, vc)
        k_dec = kc * jnp.exp(b_last[:, None] - bcum)
        state = jnp.exp(b_last)[..., None] * state + jnp.einsum('bjhk,bjhv->bhkv', k_dec, vc)
        return state, o_inter + o_intra

    s0 = jnp.zeros((bsz, h, dk, dv), dtype=jnp.float32)
    _, o = lax.scan(step, s0, (qs, ks, vs, gs))
    return jnp.moveaxis(o, 0, 1).reshape(bsz, s, h, dv)


def causal_dwconv(h, w, b):
    s = h.shape[1]
    hp = jnp.pad(h, ((0, 0), (CONV_K - 1, 0), (0, 0)))
    y = b
    for i in range(CONV_K):
        y = y + hp[:, i:i + s] * w[i]
    return y


def setup_inputs(seed: int = 0) -> dict:
    key = jax.random.key(seed)
    ks = jax.random.split(key, 20)
    L = DEPTH
    nrm = jax.random.normal
    f32 = jnp.float32

    def gain(k, n):
        return 1.0 + 0.02 * nrm(k, (L, n), f32)

    return {
        "x": nrm(ks[0], (BATCH, SEQ, D_MODEL), f32),
        "norm_mix_pre": gain(ks[1], D_MODEL),
        "w_in": nrm(ks[2], (L, D_MODEL, D_IN), f32) * D_MODEL ** -0.5,
        "sgu_ln_g": gain(ks[3], A_WIDTH),
        "sgu_ln_b": 0.02 * nrm(ks[4], (L, A_WIDTH), f32),
        "sgu_w_s": nrm(ks[5], (L, A_HEADS, SGU_CHUNK, SGU_CHUNK), f32) * SGU_CHUNK ** -0.5,
        "sgu_b": 1.0 + 0.02 * nrm(ks[6], (L, A_HEADS, SGU_CHUNK), f32),
        "gla_w_gk": nrm(ks[7], (L, GATE_RANK, B_WIDTH_K), f32) * GATE_RANK ** -0.5,
        "gla_b_gk": 0.01 * nrm(ks[8], (L, B_WIDTH_K), f32),
        "gla_norm_g": gain(ks[9], B_HEAD_V),
        "w_out": nrm(ks[10], (L, D_MIX, D_MODEL), f32) * D_MIX ** -0.5,
        "norm_mix_post": gain(ks[11], D_MODEL),
        "norm_ffn_pre": gain(ks[12], D_MODEL),
        "w_up": nrm(ks[13], (L, D_MODEL, 2 * D_FF), f32) * D_MODEL ** -0.5,
        "conv_w": nrm(ks[14], (L, CONV_K, 2 * D_FF), f32) * CONV_K ** -0.5,
        "conv_b": 0.02 * nrm(ks[15], (L, 2 * D_FF), f32),
        "w_down": nrm(ks[16], (L, D_FF, D_MODEL), f32) * D_FF ** -0.5,
        "norm_ffn_post": gain(ks[17], D_MODEL),
    }


def reference(x, norm_mix_pre, w_in, sgu_ln_g, sgu_ln_b, sgu_w_s, sgu_b, gla_w_gk, gla_b_gk,
              gla_norm_g, w_out, norm_mix_post, norm_ffn_pre, w_up, conv_w, conv_b, w_down,
              norm_ffn_post):
    bsz, s, _ = x.shape
    for l in range(DEPTH):
        h = rms_norm(x, norm_mix_pre[l])
        p = h @ w_in[l]
        u = jax.nn.gelu(p[..., OFF_U:OFF_V], approximate=False)
        v_a = jax.nn.gelu(p[..., OFF_V:OFF_Q], approximate=False)
        out_a = sgu_mixer(u, v_a, sgu_ln_g[l], sgu_ln_b[l], sgu_w_s[l], sgu_b[l])
        q = p[..., OFF_Q:OFF_K].reshape(bsz, s, B_HEADS, B_HEAD_K)
        k = p[..., OFF_K:OFF_BV].reshape(bsz, s, B_HEADS, B_HEAD_K)
        v_b = p[..., OFF_BV:OFF_G].reshape(bsz, s, B_HEADS, B_HEAD_V)
        g_out = p[..., OFF_G:OFF_LR].reshape(bsz, s, B_HEADS, B_HEAD_V)
        lr = p[..., OFF_LR:D_IN]
        gk = jax.nn.log_sigmoid((lr @ gla_w_gk[l] + gla_b_gk[l]).astype(jnp.float32)) / GATE_TAU
        gk = gk.reshape(bsz, s, B_HEADS, B_HEAD_K)
        o_b = gla_mixer(q, k, v_b, gk)
        o_b = rms_norm(o_b, gla_norm_g[l]) * jax.nn.silu(g_out.astype(jnp.float32))
        out_b = o_b.reshape(bsz, s, B_WIDTH_V).astype(x.dtype)
        mix = jnp.concatenate([out_a, out_b], axis=-1) @ w_out[l]
        x = x + rms_norm(mix, norm_mix_post[l])
        h = rms_norm(x, norm_ffn_pre[l])
        up = causal_dwconv(h @ w_up[l], conv_w[l], conv_b[l])
        ff = (jax.nn.gelu(up[..., :D_FF], approximate=True) * up[..., D_FF:]) @ w_down[l]
        x = x + rms_norm(ff, norm_ffn_post[l])
    return x
```

```python
import contextlib
import numpy as np
import concourse.bass as bass
import concourse.mybir as mybir
from concourse.bass_utils import run_bass_kernel_spmd

F32 = mybir.dt.float32
BF16 = mybir.dt.bfloat16
AF = mybir.ActivationFunctionType
ALU = mybir.AluOpType

D = 1024
DIN = 2576
DFF = 2816
NJ = DFF // 128
OFF_U, OFF_V, OFF_Q, OFF_K, OFF_BV, OFF_G, OFF_LR = 0, 512, 1024, 1280, 1536, 2048, 2560
EPS = 1e-6

ENGS = ["pe", "act", "dve", "pool", "sp"]
EPOCH = 16000
SAME_ENG_DIST = 2
BATCH_PREFIX = True
CONV_Q = "pool"
import os
LIMIT = [int(os.environ.get("MK_LIMIT", "100000000"))]


def _bank_of(res):
    if len(res) < 2 or res[0] != "p" or not res[1].isupper():
        return None
    for pre in ("pT", "pG", "pH"):
        if res.startswith(pre):
            return pre
    return res


class _Op:
    __slots__ = ("idx", "eng", "fn", "deps", "flag", "is_dma", "sem", "semval",
                 "eidx", "cnt", "waits")


class Prog:
    def __init__(self):
        self.ops = []
        self.last_w = {}
        self.readers = {}
        self.eng_n = {e: 0 for e in ENGS}
        self.dma_sems = {}
        self.bank_last = {}

    def add(self, eng, fn, r=(), w=(), dma=None, force=False):
        if len(self.ops) >= LIMIT[0] and not force:
            return None
        op = _Op()
        op.idx = len(self.ops)
        op.eng = eng
        op.fn = fn
        op.flag = False
        op.is_dma = dma is not None
        deps = set()
        for res in r:
            if res in self.last_w:
                deps.add(self.last_w[res])
        for res in w:
            if res in self.last_w:
                deps.add(self.last_w[res])
            for q in self.readers.get(res, ()):
                deps.add(q)
        for res in r:
            self.readers.setdefault(res, []).append(op.idx)
        for res in w:
            self.last_w[res] = op.idx
            self.readers[res] = []
        for res in list(r) + list(w):
            bk = _bank_of(res)
            if bk is None:
                continue
            lb = self.bank_last.setdefault(bk, {})
            for e2, oi in lb.items():
                if e2 != eng:
                    deps.add(oi)
            lb[eng] = op.idx
        deps.discard(op.idx)
        op.deps = sorted(deps)
        op.eidx = self.eng_n[eng]
        self.eng_n[eng] += 1
        op.sem = None
        op.semval = 0
        op.cnt = 0
        if op.is_dma:
            op.sem = dma
            self.dma_sems[dma] = self.dma_sems.get(dma, 0) + 16
            op.semval = self.dma_sems[dma]
        self.ops.append(op)
        return op

    def plan(self):
        ops = self.ops
        need = []
        for b in ops:
            nl = []
            for ai in b.deps:
                a = ops[ai]
                if a.is_dma:
                    nl.append(ai)
                elif a.eng == b.eng and not b.is_dma:
                    if a.eng == "pe":
                        continue
                    if b.eidx - a.eidx <= SAME_ENG_DIST:
                        a.flag = True
                        nl.append(ai)
                else:
                    a.flag = True
                    nl.append(ai)
            need.append(nl)
        cnt = {e: 0 for e in ENGS}
        for o in ops:
            if o.flag and not o.is_dma:
                cnt[o.eng] += 1
                o.cnt = cnt[o.eng]
        sem_names = []
        for e in ENGS:
            for k in range(max(1, (cnt[e] + EPOCH - 1) // EPOCH)):
                sem_names.append(("E", e, k))
        for d in self.dma_sems:
            sem_names.append(("D", d))
        waited = {e: {} for e in ENGS}
        for b, nl in zip(ops, need):
            ws = {}
            for ai in nl:
                a = ops[ai]
                if a.is_dma:
                    key = ("D", a.sem)
                    val = a.semval
                else:
                    c = a.cnt
                    key = ("E", a.eng, (c - 1) // EPOCH)
                    val = (c - 1) % EPOCH + 1
                if ws.get(key, 0) < val:
                    ws[key] = val
            out = []
            wd = waited[b.eng]
            for key, val in ws.items():
                if wd.get(key, 0) >= val:
                    continue
                wd[key] = val
                out.append((key, val))
            b.waits = out
        return sem_names

    def run_engine(self, eng, handle, sems):
        for o in self.ops:
            if o.eng != eng:
                continue
            for key, val in o.waits:
                handle.wait_ge(sems[key], val)
            ins = o.fn(handle)
            if o.is_dma:
                ins.then_inc(sems[("D", o.sem)], 16)
            elif o.flag:
                c = o.cnt
                ins.then_inc(sems[("E", o.eng, (c - 1) // EPOCH)], 1)


_BLK = [0]


def build_block(nc, prog):
    sem_names = prog.plan()
    _BLK[0] += 1
    blk = _BLK[0]
    with contextlib.ExitStack() as st:
        sems = {}
        for i, key in enumerate(sem_names):
            sems[key] = st.enter_context(nc.semaphore("sem%d_%d" % (blk, i)))
        block = st.enter_context(nc.Block())

        @block.tensor
        def _(e):
            prog.run_engine("pe", e, sems)

        @block.scalar
        def _(e):
            prog.run_engine("act", e, sems)

        @block.vector
        def _(e):
            prog.run_engine("dve", e, sems)

        @block.gpsimd
        def _(e):
            prog.run_engine("pool", e, sems)

        @block.sync
        def _(e):
            prog.run_engine("sp", e, sems)


def _prenorm_a(P, x_ap, x_res, gb, st, hb, sfx="", hb_res=None):
    hr = hb_res or ("hb" + sfx)
    P.add("act", lambda e: e.activation(out=hb[:, :], in_=x_ap, func=AF.Square, accum_out=st[:, 0:1]),
          r=[x_res], w=[hr, "st0" + sfx])
    P.add("act", lambda e: e.activation(out=st[:, 1:2], in_=st[:, 0:1], func=AF.Ln, scale=1.0 / D, bias=EPS),
          r=["st0" + sfx], w=["st1" + sfx])
    P.add("act", lambda e: e.activation(out=st[:, 2:3], in_=st[:, 1:2], func=AF.Exp, scale=-0.5),
          r=["st1" + sfx], w=["st2" + sfx])
    P.add("dve", lambda e: e.scalar_tensor_tensor(out=hb[:, :], in0=x_ap, scalar=st[:, 2:3], in1=gb[:, :],
                                                  op0=ALU.mult, op1=ALU.mult),
          r=[x_res, "st2" + sfx, "gb"], w=[hr])


def _prenorm_a_nog(P, x_ap, x_res, st, hb, sfx="", hb_res=None):
    hr = hb_res or ("hb" + sfx)
    P.add("act", lambda e: e.activation(out=hb[:, :], in_=x_ap, func=AF.Square, accum_out=st[:, 0:1]),
          r=[x_res], w=[hr, "st0" + sfx])
    P.add("act", lambda e: e.activation(out=st[:, 1:2], in_=st[:, 0:1], func=AF.Ln, scale=1.0 / D, bias=EPS),
          r=["st0" + sfx], w=["st1" + sfx])
    P.add("act", lambda e: e.activation(out=st[:, 2:3], in_=st[:, 1:2], func=AF.Exp, scale=-0.5),
          r=["st1" + sfx], w=["st2" + sfx])
    P.add("dve", lambda e: e.tensor_scalar(out=hb[:, :], in0=x_ap, scalar1=st[:, 2:3], scalar2=None, op0=ALU.mult),
          r=[x_res, "st2" + sfx], w=[hr])


def _prenorm_b(P, hb, pT, ident, dst3, dst_res, sfx="", evac_eng="act", hb_res=None, pres="pT"):
    hr = hb_res or ("hb" + sfx)
    for kc in range(8):
        P.add("pe", lambda e, kc=kc: e.transpose(pT[:, kc * 128:(kc + 1) * 128], hb[:, kc * 128:(kc + 1) * 128], ident[:, :]),
              r=[hr, "ident"], w=[pres + "%d" % kc])
    if evac_eng == "act":
        P.add("act", lambda e: e.activation(out=dst3, in_=pT[:, :].rearrange("p (k c) -> p k c", k=8), func=AF.Copy),
              r=[pres + "%d" % k for k in range(8)], w=[dst_res])
    else:
        P.add("dve", lambda e: e.tensor_copy(out=dst3, in_=pT[:, :].rearrange("p (k c) -> p k c", k=8)),
              r=[pres + "%d" % k for k in range(8)], w=[dst_res])


def _prenorm(P, tag, x_ap, x_res, gb, st, hb, junk, pT, ident, dst3, dst_res):
    _prenorm_a(P, x_ap, x_res, gb, st, hb)
    _prenorm_b(P, hb, pT, ident, dst3, dst_res)


def _consts(P, ident, dmaq="sp"):
    P.add("pool", lambda e: e.memset(ident[:, :], 0.0), w=["ident"])
    P.add("pool", lambda e: e.affine_select(out=ident[:, :], in_=ident[:, :], compare_op=ALU.not_equal, fill=1.0,
                                            base=0, pattern=[[-1, 128]], channel_multiplier=1),
          r=["ident"], w=["ident"])


def _phase_M(nc, dr, x1all, NOT, dbg=None):
    NPT = 3 * NOT
    NT = NPT + NOT
    NG = NT // 4
    P = Prog()
    with contextlib.ExitStack() as st_:
        def sb(name, shape, dt):
            return st_.enter_context(nc.sbuf_tensor("sb_" + name, shape, dt))

        def ps(name, shape, dt):
            return st_.enter_context(nc.psum_tensor("ps_" + name, shape, dt))

        win = sb("win", [128, 8, DIN], BF16)
        wout = sb("wout", [128, 8, D], BF16)
        NXS = 2
        xs = [sb("xs%d" % i, [128, D], F32) for i in range(NXS)]
        hb = sb("hb", [128, D], BF16)
        junk = sb("junk", [128, D], BF16)
        hT = sb("hT", [128, 8, 512], BF16)
        uT = sb("uT", [128, 4, 512], BF16)
        qT = sb("qT", [128, 2, 512], F32)
        kT = sb("kT", [128, 2, 512], F32)
        lrT = sb("lrT", [32, 512], F32)
        vav = sb("vav", [128, 1024], F32)
        va = vav[:, 0:512]
        vtmp = vav[:, 512:1024]
        vng = sb("vng", [128, 512], BF16)
        vb = sb("vb", [128, 512], BF16)
        sgs = sb("sgs", [128, 1024], F32)
        sg = sgs[:, 0:512]
        sgg = sgs[:, 512:1024]
        el = sb("el", [128, 256], F32)
        lg = sb("lg", [128, 256], F32)
        Ep = sb("Ep", [128, 2, 128], F32)
        Em = sb("Em", [128, 2, 128], F32)
        qd = sb("qd", [128, 4, 128], BF16)
        kd = sb("kd", [128, 2, 128], BF16)
        kdec = sb("kdec", [128, 2, 128], BF16)
        kdecT = sb("kdecT", [128, 256], BF16)
        attnT = sb("attnT", [128, 4, 128], BF16)
        ob = sb("ob", [128, 512], BF16)
        mixT = sb("mixT", [128, 8, 128], BF16)
        t1 = sb("t1", [128, D], F32)
        wraw = t1[:, 0:512].rearrange("p (h c) -> p h c", h=4)
        wrb = junk[:, 0:512].rearrange("p (h c) -> p h c", h=4)
        S = sb("S", [128, 2, 128], F32)
        Sb = sb("Sb", [128, 2, 128], BF16)
        st = sb("st", [128, 16], F32)
        stq = sb("stq", [128, 8], F32)
        sth = [sb("sth%d" % i, [128, 4], F32) for i in range(2)]
        bst = sb("bst", [128, 6], F32)
        mv = sb("mv", [128, 2], F32)
        ident = sb("ident", [128, 128], BF16)
        maskU = sb("maskU", [128, 128], F32)
        Lcum = sb("Lcum", [128, 128], F32)
        Ep2 = sb("Ep2", [128, 2, 128], F32)
        Em2 = sb("Em2", [128, 2, 128], F32)
        WmT = sb("WmT", [128, 4, 128], BF16)
        onesb = sb("onesb", [128, 2], BF16)
        lhsT2 = sb("lhsT2", [2, 512], F32)
        rhs2 = sb("rhs2", [2, 512], F32)
        lnG = sb("lnG", [128, 512], F32)
        gng = sb("gng", [128, 512], F32)
        gpost = sb("gpost", [128, D], F32)
        gp8 = sb("gp8", [128, 8], F32)
        wgk = sb("wgk", [32, 256], F32)

        pP = ps("pP", [128, 1024], F32)
        pG = ps("pG", [128, 512], F32)
        pAO = ps("pAO", [128, 1024], F32)
        pA = pAO[:, 0:512]
        pO = pAO[:, 512:1024]
        pS = ps("pS", [128, 512], F32)
        pZ = ps("pZ", [128, 512], F32)
        pT = ps("pT", [128, 1024], BF16)

        x1flat = x1all[:, :, :].rearrange("p s d -> p (s d)")

        def area_res(lo, hi):
            return ["x1_%d" % k for k in range(lo // D, (hi - 1) // D + 1)]

        def cast(eng, out_ap, in_ap, r, w):
            if eng == "act":
                P.add("act", lambda e: e.activation(out=out_ap, in_=in_ap, func=AF.Copy), r=r, w=w)
            else:
                P.add(eng, lambda e: e.tensor_copy(out=out_ap, in_=in_ap), r=r, w=w)

        HSPLIT = OFF_K
        AW = DIN - HSPLIT

        def win_res(kc, c0):
            part = "hi" if c0 >= HSPLIT else "lo"
            return ["win%d.%s.%s" % (kc, part, e) for e in ("act", "dve", "pool")]

        P.add("sp", lambda e: e.dma_start(out=gp8[:, :], in_=dr["gpre8"][:, :]), w=["gp8"], dma="c3")

        def cast_scaled(eng, out_ap, in_ap, sc, r, w):
            if eng == "act":
                P.add("act", lambda e: e.activation(out=out_ap, in_=in_ap, func=AF.Copy, scale=sc), r=r, w=w)
            else:
                P.add(eng, lambda e: e.tensor_scalar(out=out_ap, in0=in_ap, scalar1=sc, scalar2=None, op0=ALU.mult), r=r, w=w)

        def load_win_part(part, cb0, cb1, do_dma=True, do_cast=True):
            w_ = cb1 - cb0
            th = w_ // 3
            spl = [(0, th, "act"), (th, 2 * th, "dve"), (2 * th, w_, "pool")]
            for kc in range(8 if do_dma else 0):
                lo = kc * AW
                ar = area_res(lo, lo + w_)
                P.add("sp", lambda e, kc=kc, lo=lo, w_=w_, cb0=cb0, cb1=cb1: e.dma_start(out=x1flat[:, lo:lo + w_], in_=dr["w_in_r"][:, kc, cb0:cb1]),
                      w=ar, dma="stgA%d" % kc)
            for kc in range(8 if do_cast else 0):
                lo = kc * AW
                ar = area_res(lo, lo + w_)
                for (c0, c1, eng) in spl:
                    cast_scaled(eng, win[:, kc, cb0 + c0:cb0 + c1], x1flat[:, lo + c0:lo + c1], gp8[:, kc:kc + 1], ar + ["gp8"],
                                ["win%d.%s.%s" % (kc, part, eng)])

        load_win_part("hi", HSPLIT, DIN)
        _consts(P, ident)
        P.add("pool", lambda e: e.memset(maskU[:, :], 1.0), w=["maskU"])
        P.add("pool", lambda e: e.affine_select(out=maskU[:, :], in_=maskU[:, :], compare_op=ALU.is_ge, fill=0.0,
                                                base=0, pattern=[[1, 128]], channel_multiplier=-1),
              r=["maskU"], w=["maskU"])
        P.add("pool", lambda e: e.memset(Lcum[:, :], -1.0 / 16.0), w=["Lcum"])
        P.add("pool", lambda e: e.affine_select(out=Lcum[:, :], in_=Lcum[:, :], compare_op=ALU.is_ge, fill=0.0,
                                                base=0, pattern=[[1, 128]], channel_multiplier=-1),
              r=["Lcum"], w=["Lcum"])
        P.add("pool", lambda e: e.memset(onesb[:, :], 1.0), w=["onesb"])
        P.add("pool", lambda e: e.memset(lrT[:, :], 1.0), w=["lrT"])
        P.add("pool", lambda e: e.memset(S[:, :, :], 0.0), w=["S0", "S1"])
        P.add("pool", lambda e: e.memset(Sb[:, :, :], 0.0), w=["Sb"])
        P.add("pool", lambda e: e.memset(qd[:, :, :], 0.0), w=["qd"])
        P.add("pool", lambda e: e.memset(lhsT2[:, :], 1.0), w=["lhsT2"])
        P.add("sp", lambda e: e.dma_start(out=lhsT2[0:1, :], in_=dr["lnb4"][0:1, :]), w=["lhsT2"], dma="c0")
        P.add("sp", lambda e: e.dma_start(out=rhs2[1:2, :], in_=dr["sgu_b4"][0:1, :]), w=["rhs2r1"], dma="c1")
        P.add("sp", lambda e: e.dma_start(out=wraw[:, :, :], in_=dr["sgu_w"].rearrange("h t s -> t h s")),
              w=["t1"], dma="c2")
        P.add("sp", lambda e: e.dma_start(out=wgk[0:17, :], in_=dr["wgk17"][:, :]), w=["wgk"], dma="c4")
        P.add("sp", lambda e: e.dma_start(out=lnG[:, :], in_=dr["lng"][0:1, :].partition_broadcast(128)), w=["lnG"], dma="c5")
        P.add("sp", lambda e: e.dma_start(out=gng[:, :], in_=dr["gng4"][0:1, :].partition_broadcast(128)), w=["gng"], dma="c6")
        P.add("sp", lambda e: e.dma_start(out=gpost[:, :], in_=dr["gpost"][0:1, :].partition_broadcast(128)), w=["gpost"], dma="c7")
        for h in range(4):
            P.add("pool", lambda e, h=h: e.affine_select(out=wraw[:, h, :], in_=wraw[:, h, :], compare_op=ALU.is_ge, fill=0.0,
                                                         base=0, pattern=[[-1, 128]], channel_multiplier=1),
                  r=["t1"], w=["t1"])
        P.add("pool", lambda e: e.tensor_copy(out=wrb[:, :, :], in_=wraw[:, :, :]), r=["t1"], w=["junk"])
        for h in range(4):
            P.add("pe", lambda e, h=h: e.transpose(pT[:, h * 128:(h + 1) * 128], wrb[:, h, :], ident[:, :]),
                  r=["junk", "ident"], w=["pT%d" % h])
        P.add("act", lambda e: e.activation(out=WmT[:, :, :], in_=pT[:, 0:512].rearrange("p (h c) -> p h c", h=4), func=AF.Copy),
              r=["pT0", "pT1", "pT2", "pT3"], w=["WmT"])
        P.add("pe", lambda e: e.matmul(pG[0:1, 0:512], lhsT=onesb[:, 0:1], rhs=WmT[:, :, :].rearrange("p h c -> p (h c)"),
                                       start=True, stop=True),
              r=["onesb", "WmT"], w=["pG.g", "pG.c"])
        P.add("act", lambda e: e.activation(out=rhs2[0:1, :], in_=pG[0:1, 0:512], func=AF.Copy),
              r=["pG.g", "pG.c"], w=["rhs2r0"])
        wo_loaded = [0]
        wo_cast = [0]

        def _wout_load(kc):
            a = kc % 4
            lo = 8 * (DIN - OFF_K) + a * D
            ar = area_res(lo, lo + D)
            P.add("sp", lambda e: e.dma_start(out=x1flat[:, lo:lo + D], in_=dr["w_out_r"][:, kc, :]), w=ar, dma="stgB%d" % a)

        def load_wout(kc):
            while wo_loaded[0] <= kc:
                _wout_load(wo_loaded[0])
                wo_loaded[0] += 1
            a = kc % 4
            lo = 8 * (DIN - OFF_K) + a * D
            ar = area_res(lo, lo + D)
            cast("pool" if kc % 2 == 0 else "act", wout[:, kc, :], x1flat[:, lo:lo + D], ar, ["wout%d" % kc])
            wo_cast[0] = kc + 1
            while wo_loaded[0] < min(8, wo_cast[0] + 3):
                _wout_load(wo_loaded[0])
                wo_loaded[0] += 1

        for kc0 in range(3):
            _wout_load(kc0)
            wo_loaded[0] += 1

        pbank = [0]

        def next_bank():
            b = pbank[0]
            pbank[0] ^= 1
            return b

        win_all = ["win%d" % k for k in range(8)]

        def proj_fm(c0, M, evac):
            b = next_bank()
            for kc in range(8):
                P.add("pe", lambda e, kc=kc, b=b: e.matmul(pP[0:M, b * 512:(b + 1) * 512], lhsT=win[:, kc, c0:c0 + M],
                                                           rhs=hT[:, kc, :], start=(kc == 0), stop=(kc == 7)),
                      r=win_res(kc, c0) + ["hT0", "hT1", "hT2", "hT3"], w=["pP%d" % b])
            evac(b)

        def proj_tm(ti, c0, evac):
            b = next_bank()
            for kc in range(8):
                P.add("pe", lambda e, kc=kc, b=b: e.matmul(pP[:, b * 512:(b + 1) * 512], lhsT=hT[:, kc, ti * 128:(ti + 1) * 128],
                                                           rhs=win[:, kc, c0:c0 + 512], start=(kc == 0), stop=(kc == 7)),
                      r=win_res(kc, c0) + ["hT%d" % ti], w=["pP%d" % b])
            evac(b)

        xcnt = [0]
        pG_bf = pG[:, :].bitcast(BF16)
        hbp = [hb, junk]
        hbr = ["hb", "junk"]
        stp = [st, stq]
        pTb = [pT, pG_bf]
        pTres = ["pT", "pG.T"]
        vbs = [vb[:, :], attnT[:, :, :].rearrange("p h c -> p (h c)")]
        vbr = ["vb", "attnT"]
        t1f = t1
        kdall = mixT
        kdT_all = uT[:, 0:2, :].rearrange("p a c -> p (a c)")

        vb_all = qT[:, :, :].rearrange("p a c -> p (a c)").bitcast(BF16)

        vb_alt = [(vb[:, :], "vb"), (vng[:, :], "vng"), (attnT[:, :, :].rearrange("p h c -> p (h c)"), "attnT"), (ob[:, :], "ob")]

        def vbuf(g, ti):
            if g % 2 == 0:
                return vb_all[:, ti * 512:(ti + 1) * 512], "qT"
            return vb_alt[ti]

        def prefix_B(g):
            for ti in range(4):
                tau = 4 * g + ti
                sl = xcnt[0] % NXS
                xcnt[0] += 1
                par = ti % 2
                P.add("sp", lambda e, tau=tau, sl=sl: e.dma_start(out=xs[sl][:, :], in_=dr["xp"][tau * 128:(tau + 1) * 128, :]),
                      w=["xs%d" % sl], dma="xs%d" % sl)
                _prenorm_a_nog(P, xs[sl][:, :], "xs%d" % sl, stp[par], hbp[par], sfx="M%d" % par, hb_res=hbr[par])
                yield
                _prenorm_b(P, hbp[par], pTb[par], ident, hT[:, :, ti * 128:(ti + 1) * 128], "hT%d" % ti,
                           evac_eng=("act" if par == 0 else "dve"), hb_res=hbr[par], pres=pTres[par])
                yield
            yield ("wait", "k_read")
            for c in range(2):
                proj_fm(OFF_K + c * 128, 128,
                        lambda b, c=c: P.add("act", lambda e: e.activation(out=kT[:, c, :], in_=pP[:, b * 512:(b + 1) * 512], func=AF.Copy),
                                             r=["pP%d" % b], w=["kT"]))
                yield
            yield ("wait", "lr_read")
            proj_fm(OFF_LR, 16,
                    lambda b: P.add("dve", lambda e: e.tensor_copy(out=lrT[0:16, :], in_=pP[0:16, b * 512:(b + 1) * 512]),
                                    r=["pP%d" % b], w=["lrT"]))
            yield
            for ti in range(4):
                vv, vr = vbuf(g, ti)
                if ti % 2 == 0:
                    proj_tm(ti, OFF_BV,
                            lambda b, vv=vv, vr=vr: P.add("act", lambda e: e.activation(out=vv, in_=pP[:, b * 512:(b + 1) * 512], func=AF.Copy),
                                                          r=["pP%d" % b], w=[vr]))
                else:
                    proj_tm(ti, OFF_BV,
                            lambda b, vv=vv, vr=vr: P.add("dve", lambda e: e.tensor_copy(out=vv, in_=pP[:, b * 512:(b + 1) * 512]),
                                                          r=["pP%d" % b], w=[vr]))
                yield

        def prefix_A(g):
            bk = ["pA", "pA", "pO", "pO"]
            for ti in range(4):
                P.add("pe", lambda e, ti=ti: e.matmul(pAO[:, ti * 256:(ti + 1) * 256], lhsT=lrT[0:17, ti * 128:(ti + 1) * 128],
                                                      rhs=wgk[0:17, :], start=True, stop=True),
                      r=["lrT", "wgk"], w=[bk[ti]])
            yield ("signal", "lr_read")
            P.add("act", lambda e: e.activation(out=t1f[:, :], in_=pAO[:, :], func=AF.Exp, scale=-1.0), r=["pA", "pO"], w=["t1"])
            yield
            P.add("act", lambda e: e.activation(out=t1f[:, :], in_=t1f[:, :], func=AF.Ln, bias=1.0), r=["t1"], w=["t1"])
            yield
            for ti in range(4):
                for c in range(2):
                    o0 = ti * 256 + c * 128
                    P.add("pe", lambda e, o0=o0: e.matmul(pAO[:, o0:o0 + 128], lhsT=t1f[:, o0:o0 + 128], rhs=Lcum[:, :],
                                                          start=True, stop=True),
                          r=["t1", "Lcum"], w=[bk[ti]])
            yield
            P.add("act", lambda e: e.activation(out=vav[:, :], in_=pAO[:, :], func=AF.Exp), r=["pA", "pO"], w=["va", "vtmp"])
            yield
            P.add("act", lambda e: e.activation(out=sgs[:, :], in_=pAO[:, :], func=AF.Exp, scale=-1.0), r=["pA", "pO"], w=["sg", "sgg"])
            yield
            for ti in range(4):
                for c in range(2):
                    o0 = ti * 256 + c * 128
                    P.add("dve", lambda e, o0=o0, ti=ti, c=c: e.scalar_tensor_tensor(
                        out=kdall[:, ti * 2 + c, :], in0=kT[:, c, ti * 128:(ti + 1) * 128], scalar=vav[:, o0 + 127:o0 + 128],
                        in1=sgs[:, o0:o0 + 128], op0=ALU.mult, op1=ALU.mult),
                        r=["kT", "va", "vtmp", "sg", "sgg"], w=["mixTa", "mixTb"])
                yield
            yield ("signal", "k_read")
            for q in range(8):
                P.add("pe", lambda e, q=q: e.transpose(pT[:, q * 128:(q + 1) * 128], kdall[:, q, :], ident[:, :]),
                      r=["mixTa", "mixTb", "ident"], w=["pT%d" % q])
            P.add("act", lambda e: e.activation(out=kdT_all, in_=pT[:, :], func=AF.Copy),
                  r=["pT%d" % q for q in range(8)], w=["uT"])
            yield
            for ti in range(4):
                vv, vr = vbuf(g, ti)
                psb, psr = (pS, "pS") if ti % 2 == 0 else (pZ, "pZ")
                for h in range(4):
                    c = h // 2
                    o0 = ti * 256 + c * 128
                    P.add("pe", lambda e, h=h, o0=o0, vv=vv, psb=psb: e.matmul(psb[:, h * 128:(h + 1) * 128], lhsT=kdT_all[:, o0:o0 + 128],
                                                                                rhs=vv[:, h * 128:(h + 1) * 128], start=True, stop=True),
                          r=["uT", vr], w=[psr])
                yield
                for h in range(4):
                    c, r0 = h // 2, (h % 2) * 64
                    o0 = ti * 256 + c * 128
                    P.add("dve", lambda e, h=h, c=c, r0=r0, o0=o0, psb=psb: e.scalar_tensor_tensor(
                        out=S[r0:r0 + 64, c, :], in0=S[r0:r0 + 64, c, :], scalar=vav[r0:r0 + 64, o0 + 127:o0 + 128],
                        in1=psb[r0:r0 + 64, h * 128:(h + 1) * 128], op0=ALU.mult, op1=ALU.add),
                        r=["S%d" % c, "va", "vtmp", psr], w=["S%d" % c])
                yield
            P.add("pool", lambda e: e.tensor_copy(out=Sb[:, :, :], in_=S[:, :, :]), r=["S0", "S1"], w=["Sb"])
            yield

        CSLOT = 3072
        ccnt = [0]

        conv_pending = []

        def conv_bufs(k):
            lo = k * CSLOT
            ar = area_res(lo, lo + CSLOT)
            stage = x1flat[:, lo:lo + 2048]
            bfv = x1flat[:, lo + 2048:lo + 3072].bitcast(BF16)
            return ar, stage, bfv

        def conv_load(u):
            k = ccnt[0] % 3
            ccnt[0] += 1
            ar, stage, bfv = conv_bufs(k)
            if u < NJ:
                src = dr["w_up_r"][u, :, :, :].rearrange("p k c -> p (k c)")
            else:
                src = dr["w_down_r"][:, 2 * (u - NJ):2 * (u - NJ) + 2, :].rearrange("p j d -> p (j d)")
            P.add(CONV_Q, lambda e: e.dma_start(out=stage, in_=src), w=ar, dma="cvl%d" % k)
            conv_pending.append((u, k))

        def conv_finish():
            u, k = conv_pending.pop(0)
            ar, stage, bfv = conv_bufs(k)
            if u < NJ:
                dst = dr["scr_up"][u, :, :, :].rearrange("p k c -> p (k c)")
            else:
                dst = dr["scr_dn"][u - NJ, :, :, :].rearrange("p j d -> p (j d)")
            cast("act", bfv[:, 0:1024], stage[:, 0:1024], ar, ["cv%d.a" % k])
            cast("pool", bfv[:, 1024:2048], stage[:, 1024:2048], ar, ["cv%d.b" % k])
            yield
            P.add(CONV_Q, lambda e: e.dma_start(out=dst, in_=bfv), r=["cv%d.a" % k, "cv%d.b" % k] + ar, w=["scrw%d" % u], dma="cvs%d" % k)
            yield

        def conv_thread(units, last=False):
            for u in units:
                conv_load(u)
                yield
                if len(conv_pending) > 2:
                    for _ in conv_finish():
                        yield
            if last:
                while conv_pending:
                    for _ in conv_finish():
                        yield

        def run_sched(gens, pre=()):
            sig = set(pre)
            live = [[x, None] for x in gens if x is not None]
            while live:
                progressed = False
                for item in list(live):
                    x, wk = item
                    if wk is not None:
                        if wk not in sig:
                            continue
                        item[1] = None
                    progressed = True
                    try:
                        r = next(x)
                    except StopIteration:
                        live.remove(item)
                        continue
                    if isinstance(r, tuple):
                        if r[0] == "signal":
                            sig.add(r[1])
                        elif r[0] == "wait" and r[1] not in sig:
                            item[1] = r[1]
                if not progressed:
                    raise RuntimeError("op-thread deadlock: %s" % [i[1] for i in live])

        n_pg = (NPT // 4 - 1) if BATCH_PREFIX else 0
        n_pre = max(1, NPT // 4 - 1)
        per_g = (8 + n_pre - 1) // n_pre
        if n_pg > 0:
            for kc in range(0, min(8, per_g)):
                load_wout(kc)
            run_sched([prefix_B(0)], pre=("k_read", "lr_read"))
            load_win_part("lo", 0, HSPLIT, do_cast=False)
            n_units = NJ + NJ // 2
            n_cg = max(1, n_pg - 1)
            upg = (n_units + n_cg - 1) // n_cg
            for g in range(n_pg):
                for kc in range((g + 1) * per_g, min(8, (g + 2) * per_g)):
                    load_wout(kc)
                gg = g - 1 if n_pg > 1 else g
                units = list(range(gg * upg, min(n_units, (gg + 1) * upg))) if gg >= 0 else []
                run_sched([prefix_A(g), prefix_B(g + 1) if g + 1 < n_pg else None, conv_thread(units, last=(g == n_pg - 1))])
                if g == 0:
                    load_win_part("lo", 0, HSPLIT, do_dma=False)

        def group_head(g):
            if n_pg == 0 or g >= n_pg:
                for kc in range(max(g, n_pg + 1) * per_g if n_pg > 0 else g * per_g, min(8, (g + 1) * per_g)):
                    load_wout(kc)
            for ti in range(4):
                tau = 4 * g + ti
                sl = xcnt[0] % NXS
                xcnt[0] += 1
                par = ti % 2
                P.add("sp", lambda e, tau=tau, sl=sl: e.dma_start(out=xs[sl][:, :], in_=dr["xp"][tau * 128:(tau + 1) * 128, :]),
                      w=["xs%d" % sl], dma="xs%d" % sl)
                _prenorm_a_nog(P, xs[sl][:, :], "xs%d" % sl, sth[par], hb, sfx="H%d" % par, hb_res="hb")
                yield
                if ti == 3:
                    yield ("wait", "h_gla")
                    yield ("wait", "h_sgu")
                _prenorm_b(P, hb, pT, ident, hT[:, :, ti * 128:(ti + 1) * 128], "hT%d" % ti, hb_res="hb")
                yield
            yield ("wait", "kq_read")
            for c in range(2):
                proj_fm(OFF_K + c * 128, 128,
                        lambda b, c=c: P.add("act", lambda e: e.activation(out=kT[:, c, :], in_=pP[:, b * 512:(b + 1) * 512], func=AF.Copy),
                                             r=["pP%d" % b], w=["kT"]))
                yield
            yield ("wait", "lr_read")
            proj_fm(OFF_LR, 16,
                    lambda b: P.add("act", lambda e: e.activation(out=lrT[0:16, :], in_=pP[0:16, b * 512:(b + 1) * 512], func=AF.Copy),
                                    r=["pP%d" % b], w=["lrT"]))
            yield
            for c in range(2):
                proj_fm(OFF_Q + c * 128, 128,
                        lambda b, c=c: P.add("dve", lambda e: e.tensor_copy(out=qT[:, c, :], in_=pP[:, b * 512:(b + 1) * 512]),
                                             r=["pP%d" % b], w=["qT"]))
                yield
            yield ("wait", "u_read")
            for c in range(4):
                proj_fm(OFF_U + c * 128, 128,
                        lambda b, c=c: P.add("act", lambda e: e.activation(out=uT[:, c, :], in_=pP[:, b * 512:(b + 1) * 512], func=AF.Gelu),
                                             r=["pP%d" % b], w=["uT"]))
                yield

        ALLSIG = ("h_gla", "h_sgu", "kq_read", "lr_read", "u_read")
        pending_out = [None]
        gate_done = set()
        for g in range(NG):
            own = g >= NPT // 4
            full = [own or (g == NPT // 4 - 1 and ti == 3) for ti in range(4)]
            if g < n_pg:
                continue
            if g == n_pg:
                run_sched([group_head(g)], pre=ALLSIG)
            EpS = [(Ep, "Ep"), (Ep2, "Ep2")]
            EmS = [(Em, "Em"), (Em2, "Em2")]

            def tile_gate(ti, inline=False):
                gc0, gc1 = ti * 128, (ti + 1) * 128
                Ept, Epn = EpS[ti % 2]
                Emt, Emn = EmS[ti % 2]
                if not inline:
                    yield ("wait", "gate_free")
                P.add("pe", lambda e: e.matmul(pG[:, 0:256], lhsT=lrT[0:17, gc0:gc1], rhs=wgk[0:17, :], start=True, stop=True),
                      r=["lrT", "wgk"], w=["pG.g"])
                P.add("act", lambda e: e.activation(out=el[:, :], in_=pG[:, 0:256], func=AF.Exp, scale=-1.0), r=["pG.g"], w=["el"])
                yield ("signal", "lr_read")
                P.add("act", lambda e: e.activation(out=lg[:, :], in_=el[:, :], func=AF.Ln, bias=1.0), r=["el"], w=["lg"])
                yield
                for c in range(2):
                    P.add("pe", lambda e, c=c: e.matmul(pG[:, 256 + c * 128:256 + (c + 1) * 128], lhsT=lg[:, c * 128:(c + 1) * 128],
                                                        rhs=Lcum[:, :], start=True, stop=True),
                          r=["lg", "Lcum"], w=["pG.c"])
                P.add("act", lambda e: e.activation(out=Ept[:, :, :].rearrange("p c i -> p (c i)"), in_=pG[:, 256:512], func=AF.Exp),
                      r=["pG.c"], w=[Epn])
                P.add("act", lambda e: e.activation(out=Emt[:, :, :].rearrange("p c i -> p (c i)"), in_=pG[:, 256:512], func=AF.Exp, scale=-1.0),
                      r=["pG.c"], w=[Emn])
                gate_done.add((g, ti))
                yield

            def tile_gla(ti, fl):
                tc0, tc1 = ti * 128, (ti + 1) * 128
                proj_tm(ti, OFF_BV,
                        lambda b: P.add("dve", lambda e: e.tensor_copy(out=vb[:, :], in_=pP[:, b * 512:(b + 1) * 512]),
                                        r=["pP%d" % b], w=["vb"]))
                if fl:
                    proj_tm(ti, OFF_G,
                            lambda b: P.add("act", lambda e: e.activation(out=sg[:, :], in_=pP[:, b * 512:(b + 1) * 512], func=AF.Silu),
                                            r=["pP%d" % b], w=["sg"]))
                yield ("signal", "h_gla")
                Ept, Epn = EpS[ti % 2]
                Emt, Emn = EmS[ti % 2]
                if (g, ti) in gate_done:
                    yield ("signal", "lr_read")
                else:
                    for r_ in tile_gate(ti, inline=True):
                        yield r_
                yield ("signal", "gate_free")
                for c in range(2):
                    P.add("dve", lambda e, c=c: e.scalar_tensor_tensor(
                        out=kdec[:, c, :], in0=kT[:, c, tc0:tc1], scalar=Ept[:, c, 127:128], in1=Emt[:, c, :],
                        op0=ALU.mult, op1=ALU.mult), r=["kT", Epn, Emn], w=["kdec"])
                yield
                if fl:
                    for h in range(4):
                        c, r0 = h // 2, (h % 2) * 64
                        P.add("dve", lambda e, h=h, c=c, r0=r0: e.scalar_tensor_tensor(
                            out=qd[r0:r0 + 64, h, :], in0=qT[r0:r0 + 64, c, tc0:tc1], scalar=0.125, in1=Ept[r0:r0 + 64, c, :],
                            op0=ALU.mult, op1=ALU.mult), r=["qT", Epn], w=["qd"])
                    P.add("pool", lambda e: e.tensor_tensor(
                        out=kd[:, :, :], in0=kT[:, :, tc0:tc1], in1=Emt[:, :, :], op=ALU.mult), r=["kT", Emn], w=["kd"])
                yield ("signal", "kq_read")
                for c in range(2):
                    P.add("pe", lambda e, c=c: e.transpose(pT[:, c * 128:(c + 1) * 128], kdec[:, c, :], ident[:, :]),
                          r=["kdec", "ident"], w=["pT%d" % c])
                P.add("dve", lambda e: e.tensor_copy(out=kdecT[:, :], in_=pT[:, 0:256]), r=["pT0", "pT1"], w=["kdecT"])
                yield
                if fl:
                    for h in range(4):
                        c = h // 2
                        P.add("pe", lambda e, h=h, c=c: e.matmul(pA[:, h * 128:(h + 1) * 128], lhsT=kd[:, c, :],
                                                                 rhs=qd[:, h, :], start=True, stop=True),
                              r=["kd", "qd"], w=["pA"])
                    P.add("dve", lambda e: e.tensor_tensor(out=attnT[:, :, :], in0=pA[:, :].rearrange("p (h c) -> p h c", h=4),
                                                           in1=maskU[:, :].unsqueeze(1).to_broadcast([128, 4, 128]), op=ALU.mult),
                          r=["pA", "maskU"], w=["attnT"])
                    yield
                    for h in range(4):
                        c = h // 2
                        P.add("pe", lambda e, h=h: e.matmul(pO[:, h * 128:(h + 1) * 128], lhsT=attnT[:, h, :],
                                                            rhs=vb[:, h * 128:(h + 1) * 128], start=True, stop=False),
                              r=["attnT", "vb"], w=["pO"])
                        P.add("pe", lambda e, h=h, c=c: e.matmul(pO[:, h * 128:(h + 1) * 128], lhsT=qd[:, h, :],
                                                                 rhs=Sb[:, c, :], start=False, stop=True),
                              r=["qd", "Sb"], w=["pO"])
                    yield
                for h in range(4):
                    c = h // 2
                    P.add("pe", lambda e, h=h, c=c: e.matmul(pS[:, h * 128:(h + 1) * 128], lhsT=kdecT[:, c * 128:(c + 1) * 128],
                                                             rhs=vb[:, h * 128:(h + 1) * 128], start=True, stop=True),
                          r=["kdecT", "vb"], w=["pS"])
                yield
                for h in range(4):
                    c, r0 = h // 2, (h % 2) * 64
                    P.add("dve", lambda e, h=h, c=c, r0=r0: e.scalar_tensor_tensor(
                        out=S[r0:r0 + 64, c, :], in0=S[r0:r0 + 64, c, :], scalar=Ept[r0:r0 + 64, c, 127:128],
                        in1=pS[r0:r0 + 64, h * 128:(h + 1) * 128], op0=ALU.mult, op1=ALU.add),
                        r=["S%d" % c, Epn, "pS"], w=["S%d" % c])
                P.add("pool", lambda e: e.tensor_copy(out=Sb[:, :, :], in_=S[:, :, :]), r=["S0", "S1"], w=["Sb"])
                yield
                if not fl:
                    return
                for h in range(4):
                    P.add("act", lambda e, h=h: e.activation(out=ob[:, h * 128:(h + 1) * 128], in_=pO[:, h * 128:(h + 1) * 128],
                                                             func=AF.Square, accum_out=st[:, 4 + h:5 + h]),
                          r=["pO"], w=["ob", "st4"])
                yield
                P.add("act", lambda e: e.activation(out=st[:, 8:12], in_=st[:, 4:8], func=AF.Ln, scale=1.0 / 128, bias=EPS),
                      r=["st4"], w=["st8"])
                P.add("act", lambda e: e.activation(out=st[:, 12:16], in_=st[:, 8:12], func=AF.Exp, scale=-0.5),
                      r=["st8"], w=["st12"])
                P.add("pool", lambda e: e.tensor_tensor(out=sgg[:, :], in0=sg[:, :], in1=gng[:, :], op=ALU.mult),
                      r=["sg", "gng"], w=["sgg"])
                yield
                for h in range(4):
                    P.add("dve", lambda e, h=h: e.scalar_tensor_tensor(
                        out=ob[:, h * 128:(h + 1) * 128], in0=pO[:, h * 128:(h + 1) * 128], scalar=st[:, 12 + h:13 + h],
                        in1=sgg[:, h * 128:(h + 1) * 128], op0=ALU.mult, op1=ALU.mult),
                        r=["pO", "st12", "sgg"], w=["ob"])
                yield
                for h in range(4):
                    P.add("pe", lambda e, h=h: e.transpose(pT[:, 256 + h * 128:256 + (h + 1) * 128], ob[:, h * 128:(h + 1) * 128], ident[:, :]),
                          r=["ob", "ident"], w=["pT%d" % (2 + h)])
                P.add("act", lambda e: e.activation(out=mixT[:, 4:8, :], in_=pT[:, 256:768].rearrange("p (h c) -> p h c", h=4), func=AF.Copy),
                      r=["pT2", "pT3", "pT4", "pT5"], w=["mixTb"])
                yield

            def tile_sgu(ti):
                tc0, tc1 = ti * 128, (ti + 1) * 128
                proj_tm(ti, OFF_V,
                        lambda b: P.add("act", lambda e: e.activation(out=va[:, :], in_=pP[:, b * 512:(b + 1) * 512], func=AF.Gelu),
                                        r=["pP%d" % b], w=["va"]))
                yield ("signal", "h_sgu")
                P.add("dve", lambda e: e.bn_stats(out=bst[:, 0:6], in_=va[:, :]), r=["va"], w=["bst"])
                P.add("dve", lambda e: e.bn_aggr(out=mv[:, 0:2], in_=bst[:, 0:6]), r=["bst"], w=["mv"])
                yield
                P.add("act", lambda e: e.activation(out=st[:, 3:4], in_=mv[:, 1:2], func=AF.Ln, bias=EPS), r=["mv"], w=["st3a"])
                P.add("act", lambda e: e.activation(out=st[:, 3:4], in_=st[:, 3:4], func=AF.Exp, scale=-0.5), r=["st3a"], w=["st3"])
                P.add("dve", lambda e: e.scalar_tensor_tensor(out=vtmp[:, :], in0=va[:, :], scalar=mv[:, 0:1], in1=lnG[:, :],
                                                              op0=ALU.subtract, op1=ALU.mult),
                      r=["va", "mv", "lnG"], w=["vtmp"])
                yield
                P.add("act", lambda e: e.activation(out=vng[:, :], in_=vtmp[:, :], func=AF.Copy, scale=st[:, 3:4]),
                      r=["vtmp", "st3"], w=["vng"])
                yield
                for h in range(4):
                    P.add("pe", lambda e, h=h: e.matmul(pZ[:, h * 128:(h + 1) * 128], lhsT=vng[:, h * 128:(h + 1) * 128],
                                                        rhs=WmT[:, h, :], start=True, stop=False),
                          r=["vng", "WmT"], w=["pZ"])
                    P.add("pe", lambda e, h=h: e.matmul(pZ[:, h * 128:(h + 1) * 128], lhsT=lhsT2[0:2, h * 128:(h + 1) * 128],
                                                        rhs=rhs2[0:2, h * 128:(h + 1) * 128], start=False, stop=True),
                          r=["lhsT2", "rhs2r0", "rhs2r1"], w=["pZ"])
                yield
                P.add("dve", lambda e: e.tensor_tensor(
                    out=mixT[:, 0:4, :], in0=pZ[:, :].rearrange("p (h c) -> p h c", h=4), in1=uT[:, :, tc0:tc1], op=ALU.mult),
                    r=["pZ", "uT"], w=["mixTa"])
                yield ("signal", "u_read")

            def tile_out(ti, g=g):
                tau = 4 * g + ti
                for half in range(2):
                    for kc in range(8):
                        P.add("pe", lambda e, kc=kc, half=half: e.matmul(pP[:, half * 512:(half + 1) * 512], lhsT=mixT[:, kc, :],
                                                                         rhs=wout[:, kc, half * 512:(half + 1) * 512],
                                                                         start=(kc == 0), stop=(kc == 7)),
                              r=["mixTa", "mixTb", "wout%d" % kc], w=["pP%d" % half])
                P.add("act", lambda e: e.activation(out=junk[:, :], in_=pP[:, :], func=AF.Square, accum_out=st[:, 0:1]),
                      r=["pP0", "pP1"], w=["junk", "st0"])
                P.add("act", lambda e: e.activation(out=st[:, 1:2], in_=st[:, 0:1], func=AF.Ln, scale=1.0 / D, bias=EPS),
                      r=["st0"], w=["st1"])
                P.add("act", lambda e: e.activation(out=st[:, 2:3], in_=st[:, 1:2], func=AF.Exp, scale=-0.5),
                      r=["st1"], w=["st2"])
                P.add("dve", lambda e: e.scalar_tensor_tensor(out=t1[:, :], in0=pP[:, :], scalar=st[:, 2:3], in1=gpost[:, :],
                                                              op0=ALU.mult, op1=ALU.mult),
                      r=["pP0", "pP1", "st2", "gpost"], w=["t1"])
                yield
                slot = tau - (NPT - 1)
                P.add("sp", lambda e: e.dma_start(out=x1all[:, slot, :], in_=dr["xp"][tau * 128:(tau + 1) * 128, :]),
                      w=["x1_%d" % slot], dma="x1ld%d" % slot)
                P.add("pool", lambda e: e.tensor_tensor(out=x1all[:, slot, :], in0=x1all[:, slot, :], in1=t1[:, :], op=ALU.add),
                      r=["x1_%d" % slot, "t1"], w=["x1_%d" % slot])
                yield

            def run_many(gens):
                gens = [x for x in gens if x is not None]
                while gens:
                    for x in list(gens):
                        try:
                            next(x)
                        except StopIteration:
                            gens.remove(x)

            for ti in range(4):
                fl = full[ti]
                th = [pending_out[0], tile_gla(ti, fl), tile_sgu(ti) if fl else None]
                if ti + 1 <= 3:
                    th.append(tile_gate(ti + 1))
                if ti == 3 and g + 1 < NG:
                    th.append(group_head(g + 1))
                run_sched(th)
                pending_out[0] = tile_out(ti) if fl else None
        run_sched([pending_out[0]])
        if n_pg > 0:
            P.add("sp", lambda e: e.nop(), r=["scrw%d" % u for u in range(NJ + NJ // 2)], w=[], force=True)
        if dbg is not None:
            for slot in range(NOT + 1):
                P.add("sp", lambda e, slot=slot: e.dma_start(out=dbg[slot * 128:(slot + 1) * 128, :], in_=x1all[:, slot, :]),
                      r=["x1_%d" % slot], w=["dbg%d" % slot], dma="dbg", force=True)
            P.add("sp", lambda e: e.nop(), r=["dbg%d" % s for s in range(NOT + 1)], w=[], force=True)
        build_block(nc, P)


def _phase_F(nc, dr, x1all, NOT, out):
    NGF = NOT // 4
    scr = dr["scr_up"]
    LIMIT[0] = int(os.environ.get("MK_LIMIT_F", "100000000"))
    P = Prog()
    with contextlib.ExitStack() as st_:
        def sb(name, shape, dt):
            return st_.enter_context(nc.sbuf_tensor("sb_" + name, shape, dt))

        def ps(name, shape, dt):
            return st_.enter_context(nc.psum_tensor("ps_" + name, shape, dt))

        wdn = sb("wdn", [128, NJ, D], BF16)
        NWU = 3
        wu = [sb("wu%d" % i, [128, 8, 256], BF16) for i in range(NWU)]
        GT = sb("GT", [128, NJ, 512], BF16)
        h2T = [sb("h2T%d" % i, [128, 8, 514], BF16) for i in range(2)]
        NHB = 2
        hb = [sb("hbF%d" % i, [128, D], BF16) for i in range(NHB)]
        NY = 3
        ya = [sb("ya%d" % i, [128, 512], F32) for i in range(NY)]
        yb = [sb("yb%d" % i, [128, 512], F32) for i in range(NY)]
        ga = [sb("ga%d" % i, [128, 512], F32) for i in range(NY)]
        t1 = [sb("t1F%d" % i, [128, D], F32) for i in range(2)]
        junk = sb("junkF", [128, D], BF16)
        st = [sb("stF%d" % i, [128, 8], F32) for i in range(NHB)]
        st2 = [sb("stG%d" % i, [128, 8], F32) for i in range(2)]
        Hs = [sb("Hs%d" % i, [128, 2 * NJ, 2], F32) for i in range(2)]
        corr = [sb("corr%d" % i, [128, 8], F32) for i in range(NY)]
        ident = sb("identF", [128, 128], BF16)
        cw = sb("cw", [128, 2 * NJ, 3], F32)
        cb = sb("cb", [128, 2 * NJ], F32)
        gffn = sb("gffn", [128, D], F32)
        gb = sb("gbF", [128, D], F32)

        pU = [ps("pU%d" % i, [128, 1024], F32) for i in range(2)]
        pF = ps("pF", [128, 1024], F32)
        pH = ps("pH", [128, 512], F32)
        pT = ps("pTF", [128, 1024], BF16)

        _consts(P, ident)
        P.add("sp", lambda e: e.dma_start(out=cw[:, :, :], in_=dr["cw"][:, :, :]), w=["cw"], dma="f0")
        P.add("sp", lambda e: e.dma_start(out=cb[:, :], in_=dr["cb"][:, :]), w=["cb"], dma="f1")
        P.add("sp", lambda e: e.dma_start(out=gffn[:, :], in_=dr["gffn"][0:1, :].partition_broadcast(128)), w=["gffn"], dma="f2")
        P.add("sp", lambda e: e.dma_start(out=gb[:, :], in_=dr["g2"][0:1, :].partition_broadcast(128)), w=["gb"], dma="f3")

        hcnt = [0]

        def prenorm_a(slot):
            k = hcnt[0] % NHB
            hcnt[0] += 1
            _prenorm_a(P, x1all[:, slot, :], "x1_%d" % slot, gb, st[k], hb[k], sfx="F%d" % k)
            return k

        def prenorm_b(k, dst3, dst_res):
            _prenorm_b(P, hb[k], pT, ident, dst3, dst_res, sfx="F%d" % k)

        k = prenorm_a(0)
        prenorm_b(k, h2T[0][:, :, 2:130], "h2T0_0")
        P.add("pool", lambda e: e.tensor_copy(out=h2T[0][:, :, 0:2], in_=h2T[0][:, :, 128:130]), r=["h2T0_0"], w=["h2Th"])
        for ti in range(4):
            k = prenorm_a(1 + ti)
            prenorm_b(k, h2T[0][:, :, 2 + ti * 128:2 + (ti + 1) * 128], "h2T0_%d" % ti)
        def load_wdn(jp):
            P.add("sp", lambda e: e.dma_start(out=wdn[:, 2 * jp:2 * jp + 2, :], in_=dr["scr_dn"][jp, :, :, :]),
                  w=["wdn%d" % (2 * jp), "wdn%d" % (2 * jp + 1)], dma="wdn%d" % jp)
        wcnt = [0]
        ycnt = [0]
        ecnt = [0]
        for g in range(NGF):
            hp = g % 2
            hT_ = h2T[hp]
            h2res = ["h2T%d_%d" % (hp, t) for t in range(4)]
            for j in range(NJ):
                ws = wcnt[0] % NWU
                wcnt[0] += 1
                pu = pU[j % 2]
                pur = "pU%d" % (j % 2)
                wur = ["wu%d.a" % ws, "wu%d.b" % ws]
                P.add("sp", lambda e, j=j, ws=ws: e.dma_start(out=wu[ws][:, :, :], in_=scr[j, :, :, :]),
                      w=wur, dma="wu%d" % ws)
                if g == 0 and 2 <= j < 2 + NJ // 2:
                    load_wdn(j - 2)
                for half in range(2):
                    for kc in range(8):
                        P.add("pe", lambda e, kc=kc, half=half, ws=ws, pu=pu, hT_=hT_: e.matmul(
                            pu[:, half * 512:(half + 1) * 512], lhsT=wu[ws][:, kc, half * 128:(half + 1) * 128],
                            rhs=hT_[:, kc, 2:514], start=(kc == 0), stop=(kc == 7)),
                            r=wur + h2res, w=[pur + ".%d" % half])
                    if g == 0:
                        for kc in range(8):
                            P.add("pe", lambda e, kc=kc, half=half, ws=ws, hT_=hT_: e.matmul(
                                pH[:, half * 2:half * 2 + 2], lhsT=wu[ws][:, kc, half * 128:(half + 1) * 128],
                                rhs=hT_[:, kc, 0:2], start=(kc == 0), stop=(kc == 7)),
                                r=wur + ["h2Th"], w=["pH.%d" % half])
                ys = ycnt[0] % NY
                ycnt[0] += 1
                for half, y in ((0, ya[ys]), (1, yb[ys])):
                    ci = half * NJ + j
                    yr = "y%d_%d" % (half, ys)
                    src = pu[:, half * 512:(half + 1) * 512]
                    sr = pur + ".%d" % half
                    if g == 0:
                        hsrc = pH[:, half * 2:half * 2 + 2]
                        hres = "pH.%d" % half
                    else:
                        hsrc = Hs[g % 2][:, ci, :]
                        hres = "Hs%d" % (g % 2)
                    P.add("act", lambda e, y=y, src=src, ci=ci: e.activation(out=y[:, :], in_=src, func=AF.Identity,
                                                                              scale=cw[:, ci, 2:3], bias=cb[:, ci:ci + 1]),
                          r=[sr, "cw", "cb"], w=[yr])
                    if g < NGF - 1:
                        P.add("act", lambda e, src=src, ci=ci, g=g: e.activation(out=Hs[(g + 1) % 2][:, ci, :], in_=src[:, 510:512], func=AF.Copy),
                              r=[sr], w=["Hs%d" % ((g + 1) % 2)])
                    P.add("dve", lambda e, y=y, src=src, ci=ci: e.scalar_tensor_tensor(
                        out=y[:, 1:512], in0=src[:, 0:511], scalar=cw[:, ci, 1:2], in1=y[:, 1:512], op0=ALU.mult, op1=ALU.add),
                        r=[sr, "cw", yr], w=[yr])
                    P.add("dve", lambda e, y=y, src=src, ci=ci: e.scalar_tensor_tensor(
                        out=y[:, 2:512], in0=src[:, 0:510], scalar=cw[:, ci, 0:1], in1=y[:, 2:512], op0=ALU.mult, op1=ALU.add),
                        r=[sr, "cw", yr], w=[yr])
                    if g == 0:
                        P.add("dve", lambda e, y=y, hsrc=hsrc, ci=ci: e.scalar_tensor_tensor(
                            out=y[:, 0:1], in0=hsrc[:, 1:2], scalar=cw[:, ci, 1:2], in1=y[:, 0:1], op0=ALU.mult, op1=ALU.add),
                            r=[hres, "cw", yr], w=[yr])
                        P.add("dve", lambda e, y=y, hsrc=hsrc, ci=ci: e.scalar_tensor_tensor(
                            out=y[:, 0:2], in0=hsrc[:, 0:2], scalar=cw[:, ci, 0:1], in1=y[:, 0:2], op0=ALU.mult, op1=ALU.add),
                            r=[hres, "cw", yr], w=[yr])
                    else:
                        cc = corr[ys][:, half * 2:half * 2 + 2]
                        c2 = corr[ys][:, 4 + half:5 + half]
                        cr = "corr%d.%d" % (ys, half)
                        P.add("pool", lambda e, cc=cc, hsrc=hsrc, ci=ci: e.tensor_scalar(out=cc, in0=hsrc[:, 0:2], scalar1=cw[:, ci, 0:1], scalar2=None, op0=ALU.mult),
                              r=[hres, "cw"], w=[cr])
                        P.add("pool", lambda e, c2=c2, hsrc=hsrc, ci=ci: e.tensor_scalar(out=c2, in0=hsrc[:, 1:2], scalar1=cw[:, ci, 1:2], scalar2=None, op0=ALU.mult),
                              r=[hres, "cw"], w=[cr + "b"])
                        P.add("pool", lambda e, cc=cc, c2=c2: e.tensor_tensor(out=cc[:, 0:1], in0=cc[:, 0:1], in1=c2, op=ALU.add),
                              r=[cr, cr + "b"], w=[cr])
                        P.add("dve", lambda e, y=y, cc=cc: e.tensor_tensor(out=y[:, 0:2], in0=y[:, 0:2], in1=cc, op=ALU.add),
                              r=[cr, yr], w=[yr])
                P.add("act", lambda e, ys=ys: e.activation(out=ga[ys][:, :], in_=ya[ys][:, :], func=AF.Gelu_apprx_tanh),
                      r=["y0_%d" % ys], w=["ga%d" % ys])
                P.add("pool", lambda e, j=j, ys=ys: e.tensor_tensor(out=GT[:, j, :], in0=ga[ys][:, :], in1=yb[ys][:, :], op=ALU.mult),
                      r=["ga%d" % ys, "y1_%d" % ys], w=["GT%d" % j])
            for ti in range(4):
                slot = 1 + 4 * g + ti
                nk = None
                if g + 1 < NGF:
                    nk = prenorm_a(1 + 4 * (g + 1) + ti)
                pf, pfr = [(pF, "pF"), (pU[0], "pU0."), (pU[1], "pU1.")][ti % 3]
                for half in range(2):
                    for j in range(NJ):
                        P.add("pe", lambda e, j=j, half=half, ti=ti, pf=pf: e.matmul(
                            pf[:, half * 512:(half + 1) * 512], lhsT=GT[:, j, ti * 128:(ti + 1) * 128],
                            rhs=wdn[:, j, half * 512:(half + 1) * 512], start=(j == 0), stop=(j == NJ - 1)),
                            r=["GT%d" % j, "wdn%d" % j], w=[pfr + "%d" % half])
                if nk is not None:
                    prenorm_b(nk, h2T[1 - hp][:, :, 2 + ti * 128:2 + (ti + 1) * 128], "h2T%d_%d" % (1 - hp, ti))
                es = ecnt[0] % 2
                sg_ = st2[es]
                ecnt[0] += 1
                P.add("act", lambda e, sg_=sg_, pf=pf: e.activation(out=junk[:, :], in_=pf[:, :], func=AF.Square, accum_out=sg_[:, 0:1]),
                      r=[pfr + "0", pfr + "1"], w=["junk", "sg0_%d" % es])
                P.add("act", lambda e, sg_=sg_: e.activation(out=sg_[:, 1:2], in_=sg_[:, 0:1], func=AF.Ln, scale=1.0 / D, bias=EPS),
                      r=["sg0_%d" % es], w=["sg1_%d" % es])
                P.add("act", lambda e, sg_=sg_: e.activation(out=sg_[:, 2:3], in_=sg_[:, 1:2], func=AF.Exp, scale=-0.5),
                      r=["sg1_%d" % es], w=["sg2_%d" % es])
                P.add("dve", lambda e, sg_=sg_, es=es, pf=pf: e.scalar_tensor_tensor(out=t1[es][:, :], in0=pf[:, :], scalar=sg_[:, 2:3], in1=gffn[:, :],
                                                                                     op0=ALU.mult, op1=ALU.mult),
                      r=[pfr + "0", pfr + "1", "sg2_%d" % es, "gffn"], w=["t1_%d" % es])
                P.add("pool", lambda e, slot=slot, es=es: e.tensor_tensor(out=x1all[:, slot, :], in0=x1all[:, slot, :], in1=t1[es][:, :], op=ALU.add),
                      r=["x1_%d" % slot, "t1_%d" % es], w=["x1_%d" % slot])
                P.add("sp", lambda e, slot=slot: e.dma_start(out=out[(slot - 1) * 128:slot * 128, :], in_=x1all[:, slot, :]),
                      r=["x1_%d" % slot], w=["out%d" % slot], dma="out", force=True)
        fin = P.add("sp", lambda e: e.nop(), r=["out%d" % s for s in range(1, NOT + 1)], w=[], force=True)
        build_block(nc, P)


def build_program(NOT=16, phase="all"):
    NT = 4 * NOT
    nc = bass.Bass("TRN2", target_bir_lowering=False)
    dr = {}

    def din(name, shape):
        dr[name] = nc.dram_tensor(name, shape, F32, kind="ExternalInput").ap()

    din("xp", [NT * 128, D])
    din("w_in_r", [128, 8, DIN])
    din("w_out_r", [128, 8, D])
    din("w_up_r", [NJ, 128, 8, 256])
    din("w_down_r", [128, NJ, D])
    for nm in ("gpre", "gpost", "g2", "gffn"):
        din(nm, [1, D])
    for nm in ("lng", "lnb4", "sgu_b4", "gng4"):
        din(nm, [1, 512])
    din("sgu_w", [4, 128, 128])
    din("wgk17", [17, 256])
    din("cw", [128, 2 * NJ, 3])
    din("cb", [128, 2 * NJ])
    din("gpre8", [128, 8])
    if phase == "M":
        out = nc.dram_tensor("out", [(NOT + 1) * 128, D], F32, kind="ExternalOutput").ap()
    else:
        out = nc.dram_tensor("out", [NOT * 128, D], F32, kind="ExternalOutput").ap()
    dr["scr_up"] = nc.dram_tensor("wup_bf16_scr", [NJ, 128, 8, 256], BF16, kind="Internal").ap()
    dr["scr_dn"] = nc.dram_tensor("wdn_bf16_scr", [NJ // 2, 128, 2, D], BF16, kind="Internal").ap()
    with nc.sbuf_tensor("x1all", [128, max(NOT + 1, 15), D], F32) as x1all:
        _phase_M(nc, dr, x1all, NOT, dbg=out if phase == "M" else None)
        if phase != "M":
            _phase_F(nc, dr, x1all, NOT, out)
    return nc


def make_in_maps(inp, n_seg=4):
    x = np.asarray(inp["x"], dtype=np.float32)
    B, S, _ = x.shape
    seg = S // n_seg
    f = lambda a: np.ascontiguousarray(np.asarray(a, dtype=np.float32))
    w_in = f(inp["w_in"])[0]
    w_out = f(inp["w_out"])[0]
    w_up = f(inp["w_up"])[0]
    w_down = f(inp["w_down"])[0]
    shared = {
        "w_in_r": f(w_in.reshape(8, 128, DIN).transpose(1, 0, 2)),
        "w_out_r": f(w_out.reshape(8, 128, D).transpose(1, 0, 2)),
        "w_up_r": f(w_up.reshape(8, 128, 2, NJ, 128).transpose(3, 1, 0, 2, 4).reshape(NJ, 128, 8, 256)),
        "w_down_r": f(w_down.reshape(NJ, 128, D).transpose(1, 0, 2)),
        "gpre": f(inp["norm_mix_pre"]).reshape(1, D),
        "gpre8": f(f(inp["norm_mix_pre"]).reshape(8, 128).T),
        "gpost": f(inp["norm_mix_post"]).reshape(1, D),
        "g2": f(inp["norm_ffn_pre"]).reshape(1, D),
        "gffn": f(inp["norm_ffn_post"]).reshape(1, D),
        "lng": f(inp["sgu_ln_g"]).reshape(1, 512),
        "lnb4": f(inp["sgu_ln_b"]).reshape(1, 512),
        "sgu_b4": f(inp["sgu_b"]).reshape(1, 512),
        "gng4": f(np.tile(f(inp["gla_norm_g"]).reshape(128), 4)).reshape(1, 512),
        "sgu_w": f(inp["sgu_w_s"])[0],
        "wgk17": f(np.concatenate([f(inp["gla_w_gk"])[0], f(inp["gla_b_gk"]).reshape(1, 256)], axis=0)),
        "cw": f(f(inp["conv_w"])[0].reshape(3, 2 * NJ, 128).transpose(2, 1, 0)),
        "cb": f(f(inp["conv_b"])[0].reshape(2 * NJ, 128).transpose(1, 0)),
    }
    maps = []
    for b in range(B):
        for s in range(n_seg):
            xp = np.zeros((n_seg * seg, D), np.float32)
            xp[(n_seg - 1 - s) * seg:] = x[b, :(s + 1) * seg]
            m = dict(shared)
            m["xp"] = xp
            maps.append(m)
    return maps, B, S, seg


def kernel(**inputs):
    maps, B, S, seg = make_in_maps(inputs)
    nc = build_program(NOT=seg // 128)
    res = run_bass_kernel_spmd(nc, maps, core_ids=list(range(len(maps))))
    out = np.zeros((B, S, D), np.float32)
    i = 0
    for b in range(B):
        for s in range(S // seg):
            out[b, s * seg:(s + 1) * seg] = res.results[i]["out"]
            i += 1
    return out
```

```python
import contextlib
import numpy as np
import concourse.bass as bass
import concourse.mybir as mybir
from concourse.bass_utils import run_bass_kernel_spmd

F32 = mybir.dt.float32
BF16 = mybir.dt.bfloat16
AF = mybir.ActivationFunctionType
ALU = mybir.AluOpType

D = 1024
DIN = 2576
DFF = 2816
NJ = DFF // 128
OFF_U, OFF_V, OFF_Q, OFF_K, OFF_BV, OFF_G, OFF_LR = 0, 512, 1024, 1280, 1536, 2048, 2560
EPS = 1e-6

ENGS = ["pe", "act", "dve", "pool", "sp"]
EPOCH = 16000
SAME_ENG_DIST = 2
BATCH_PREFIX = True
CONV_Q = "pool"
import os
LIMIT = [int(os.environ.get("MK_LIMIT", "100000000"))]


def _bank_of(res):
    if len(res) < 2 or res[0] != "p" or not res[1].isupper():
        return None
    for pre in ("pT", "pG", "pH"):
        if res.startswith(pre):
            return pre
    return res


class _Op:
    __slots__ = ("idx", "eng", "fn", "deps", "flag", "is_dma", "sem", "semval",
                 "eidx", "cnt", "waits")


class Prog:
    def __init__(self):
        self.ops = []
        self.last_w = {}
        self.readers = {}
        self.eng_n = {e: 0 for e in ENGS}
        self.dma_sems = {}
        self.bank_last = {}

    def add(self, eng, fn, r=(), w=(), dma=None, force=False):
        if len(self.ops) >= LIMIT[0] and not force:
            return None
        op = _Op()
        op.idx = len(self.ops)
        op.eng = eng
        op.fn = fn
        op.flag = False
        op.is_dma = dma is not None
        deps = set()
        for res in r:
            if res in self.last_w:
                deps.add(self.last_w[res])
        for res in w:
            if res in self.last_w:
                deps.add(self.last_w[res])
            for q in self.readers.get(res, ()):
                deps.add(q)
        for res in r:
            self.readers.setdefault(res, []).append(op.idx)
        for res in w:
            self.last_w[res] = op.idx
            self.readers[res] = []
        for res in list(r) + list(w):
            bk = _bank_of(res)
            if bk is None:
                continue
            lb = self.bank_last.setdefault(bk, {})
            for e2, oi in lb.items():
                if e2 != eng:
                    deps.add(oi)
            lb[eng] = op.idx
        deps.discard(op.idx)
        op.deps = sorted(deps)
        op.eidx = self.eng_n[eng]
        self.eng_n[eng] += 1
        op.sem = None
        op.semval = 0
        op.cnt = 0
        if op.is_dma:
            op.sem = dma
            self.dma_sems[dma] = self.dma_sems.get(dma, 0) + 16
            op.semval = self.dma_sems[dma]
        self.ops.append(op)
        return op

    def plan(self):
        ops = self.ops
        need = []
        for b in ops:
            nl = []
            for ai in b.deps:
                a = ops[ai]
                if a.is_dma:
                    nl.append(ai)
                elif a.eng == b.eng and not b.is_dma:
                    if a.eng == "pe":
                        continue
                    if b.eidx - a.eidx <= SAME_ENG_DIST:
                        a.flag = True
                        nl.append(ai)
                else:
                    a.flag = True
                    nl.append(ai)
            need.append(nl)
        cnt = {e: 0 for e in ENGS}
        for o in ops:
            if o.flag and not o.is_dma:
                cnt[o.eng] += 1
                o.cnt = cnt[o.eng]
        sem_names = []
        for e in ENGS:
            for k in range(max(1, (cnt[e] + EPOCH - 1) // EPOCH)):
                sem_names.append(("E", e, k))
        for d in self.dma_sems:
            sem_names.append(("D", d))
        waited = {e: {} for e in ENGS}
        for b, nl in zip(ops, need):
            ws = {}
            for ai in nl:
                a = ops[ai]
                if a.is_dma:
                    key = ("D", a.sem)
                    val = a.semval
                else:
                    c = a.cnt
                    key = ("E", a.eng, (c - 1) // EPOCH)
                    val = (c - 1) % EPOCH + 1
                if ws.get(key, 0) < val:
                    ws[key] = val
            out = []
            wd = waited[b.eng]
            for key, val in ws.items():
                if wd.get(key, 0) >= val:
                    continue
                wd[key] = val
                out.append((key, val))
            b.waits = out
        return sem_names

    def run_engine(self, eng, handle, sems):
        for o in self.ops:
            if o.eng != eng:
                continue
            for key, val in o.waits:
                handle.wait_ge(sems[key], val)
            ins = o.fn(handle)
            if o.is_dma:
                ins.then_inc(sems[("D", o.sem)], 16)
            elif o.flag:
                c = o.cnt
                ins.then_inc(sems[("E", o.eng, (c - 1) // EPOCH)], 1)


_BLK = [0]


def build_block(nc, prog):
    sem_names = prog.plan()
    _BLK[0] += 1
    blk = _BLK[0]
    with contextlib.ExitStack() as st:
        sems = {}
        for i, key in enumerate(sem_names):
            sems[key] = st.enter_context(nc.semaphore("sem%d_%d" % (blk, i)))
        block = st.enter_context(nc.Block())

        @block.tensor
        def _(e):
            prog.run_engine("pe", e, sems)

        @block.scalar
        def _(e):
            prog.run_engine("act", e, sems)

        @block.vector
        def _(e):
            prog.run_engine("dve", e, sems)

        @block.gpsimd
        def _(e):
            prog.run_engine("pool", e, sems)

        @block.sync
        def _(e):
            prog.run_engine("sp", e, sems)


def _prenorm_a(P, x_ap, x_res, gb, st, hb, sfx="", hb_res=None):
    hr = hb_res or ("hb" + sfx)
    P.add("act", lambda e: e.activation(out=hb[:, :], in_=x_ap, func=AF.Square, accum_out=st[:, 0:1]),
          r=[x_res], w=[hr, "st0" + sfx])
    P.add("act", lambda e: e.activation(out=st[:, 1:2], in_=st[:, 0:1], func=AF.Ln, scale=1.0 / D, bias=EPS),
          r=["st0" + sfx], w=["st1" + sfx])
    P.add("act", lambda e: e.activation(out=st[:, 2:3], in_=st[:, 1:2], func=AF.Exp, scale=-0.5),
          r=["st1" + sfx], w=["st2" + sfx])
    P.add("dve", lambda e: e.scalar_tensor_tensor(out=hb[:, :], in0=x_ap, scalar=st[:, 2:3], in1=gb[:, :],
                                                  op0=ALU.mult, op1=ALU.mult),
          r=[x_res, "st2" + sfx, "gb"], w=[hr])


def _prenorm_a_nog(P, x_ap, x_res, st, hb, sfx="", hb_res=None):
    hr = hb_res or ("hb" + sfx)
    P.add("act", lambda e: e.activation(out=hb[:, :], in_=x_ap, func=AF.Square, accum_out=st[:, 0:1]),
          r=[x_res], w=[hr, "st0" + sfx])
    P.add("act", lambda e: e.activation(out=st[:, 1:2], in_=st[:, 0:1], func=AF.Ln, scale=1.0 / D, bias=EPS),
          r=["st0" + sfx], w=["st1" + sfx])
    P.add("act", lambda e: e.activation(out=st[:, 2:3], in_=st[:, 1:2], func=AF.Exp, scale=-0.5),
          r=["st1" + sfx], w=["st2" + sfx])
    P.add("dve", lambda e: e.tensor_scalar(out=hb[:, :], in0=x_ap, scalar1=st[:, 2:3], scalar2=None, op0=ALU.mult),
          r=[x_res, "st2" + sfx], w=[hr])


def _prenorm_b(P, hb, pT, ident, dst3, dst_res, sfx="", evac_eng="act", hb_res=None, pres="pT"):
    hr = hb_res or ("hb" + sfx)
    for kc in range(8):
        P.add("pe", lambda e, kc=kc: e.transpose(pT[:, kc * 128:(kc + 1) * 128], hb[:, kc * 128:(kc + 1) * 128], ident[:, :]),
              r=[hr, "ident"], w=[pres + "%d" % kc])
    if evac_eng == "act":
        P.add("act", lambda e: e.activation(out=dst3, in_=pT[:, :].rearrange("p (k c) -> p k c", k=8), func=AF.Copy),
              r=[pres + "%d" % k for k in range(8)], w=[dst_res])
    else:
        P.add("dve", lambda e: e.tensor_copy(out=dst3, in_=pT[:, :].rearrange("p (k c) -> p k c", k=8)),
              r=[pres + "%d" % k for k in range(8)], w=[dst_res])


def _prenorm(P, tag, x_ap, x_res, gb, st, hb, junk, pT, ident, dst3, dst_res):
    _prenorm_a(P, x_ap, x_res, gb, st, hb)
    _prenorm_b(P, hb, pT, ident, dst3, dst_res)


def _consts(P, ident, dmaq="sp"):
    P.add("pool", lambda e: e.memset(ident[:, :], 0.0), w=["ident"])
    P.add("pool", lambda e: e.affine_select(out=ident[:, :], in_=ident[:, :], compare_op=ALU.not_equal, fill=1.0,
                                            base=0, pattern=[[-1, 128]], channel_multiplier=1),
          r=["ident"], w=["ident"])


def _phase_M(nc, dr, x1all, NOT, dbg=None):
    NPT = 3 * NOT
    NT = NPT + NOT
    NG = NT // 4
    P = Prog()
    with contextlib.ExitStack() as st_:
        def sb(name, shape, dt):
            return st_.enter_context(nc.sbuf_tensor("sb_" + name, shape, dt))

        def ps(name, shape, dt):
            return st_.enter_context(nc.psum_tensor("ps_" + name, shape, dt))

        win = sb("win", [128, 8, DIN], BF16)
        wout = sb("wout", [128, 8, D], BF16)
        NXS = 3
        xs = [sb("xs%d" % i, [128, D], F32) for i in range(NXS)]
        hb = sb("hb", [128, D], BF16)
        junk = sb("junk", [128, D], BF16)
        hT = sb("hT", [128, 8, 512], BF16)
        uT = sb("uT", [128, 4, 512], BF16)
        qT = sb("qT", [128, 2, 512], F32)
        kT = sb("kT", [128, 2, 512], F32)
        lrT = sb("lrT", [32, 512], F32)
        vav = sb("vav", [128, 1024], F32)
        va = vav[:, 0:512]
        vtmp = vav[:, 512:1024]
        vng = sb("vng", [128, 512], BF16)
        vb = sb("vb", [128, 512], BF16)
        sgs = sb("sgs", [128, 1024], F32)
        sg = sgs[:, 0:512]
        sgg = sgs[:, 512:1024]
        el = sb("el", [128, 256], F32)
        lg = sb("lg", [128, 256], F32)
        Ep = sb("Ep", [128, 2, 128], F32)
        Em = sb("Em", [128, 2, 128], F32)
        qd = sb("qd", [128, 4, 128], BF16)
        kd = sb("kd", [128, 2, 128], BF16)
        kdec = sb("kdec", [128, 2, 128], BF16)
        kdecT = sb("kdecT", [128, 256], BF16)
        attnT = sb("attnT", [128, 4, 128], BF16)
        ob = sb("ob", [128, 512], BF16)
        mixT = sb("mixT", [128, 8, 128], BF16)
        t1 = sb("t1", [128, D], F32)
        wraw = t1[:, 0:512].rearrange("p (h c) -> p h c", h=4)
        wrb = junk[:, 0:512].rearrange("p (h c) -> p h c", h=4)
        S = sb("S", [128, 2, 128], F32)
        Sb = sb("Sb", [128, 2, 128], BF16)
        st = sb("st", [128, 16], F32)
        stq = sb("stq", [128, 8], F32)
        sth = [sb("sth%d" % i, [128, 4], F32) for i in range(2)]
        bst = sb("bst", [128, 6], F32)
        mv = sb("mv", [128, 2], F32)
        ident = sb("ident", [128, 128], BF16)
        maskU = sb("maskU", [128, 128], F32)
        Lcum = sb("Lcum", [128, 128], F32)
        Ep2 = sb("Ep2", [128, 2, 128], F32)
        Em2 = sb("Em2", [128, 2, 128], F32)
        WmT = sb("WmT", [128, 4, 128], BF16)
        onesb = sb("onesb", [128, 2], BF16)
        lhsT2 = sb("lhsT2", [2, 512], F32)
        rhs2 = sb("rhs2", [2, 512], F32)
        lnG = sb("lnG", [128, 512], F32)
        gng = sb("gng", [128, 512], F32)
        gpost = sb("gpost", [128, D], F32)
        gp8 = sb("gp8", [128, 8], F32)
        wgk = sb("wgk", [32, 256], F32)

        pP = ps("pP", [128, 1024], F32)
        pG = ps("pG", [128, 512], F32)
        pAO = ps("pAO", [128, 1024], F32)
        pA = pAO[:, 0:512]
        pO = pAO[:, 512:1024]
        pS = ps("pS", [128, 512], F32)
        pZ = ps("pZ", [128, 512], F32)
        pT = ps("pT", [128, 1024], BF16)

        x1flat = x1all[:, :, :].rearrange("p s d -> p (s d)")

        def area_res(lo, hi):
            return ["x1_%d" % k for k in range(lo // D, (hi - 1) // D + 1)]

        def cast(eng, out_ap, in_ap, r, w):
            if eng == "act":
                P.add("act", lambda e: e.activation(out=out_ap, in_=in_ap, func=AF.Copy), r=r, w=w)
            else:
                P.add(eng, lambda e: e.tensor_copy(out=out_ap, in_=in_ap), r=r, w=w)

        HSPLIT = OFF_K
        AW = DIN - HSPLIT

        def win_res(kc, c0):
            part = "hi" if c0 >= HSPLIT else "lo"
            return ["win%d.%s.%s" % (kc, part, e) for e in ("act", "dve")]

        P.add("sp", lambda e: e.dma_start(out=gp8[:, :], in_=dr["gpre8"][:, :]), w=["gp8"], dma="c3")

        def cast_scaled(eng, out_ap, in_ap, sc, r, w):
            if eng == "act":
                P.add("act", lambda e: e.activation(out=out_ap, in_=in_ap, func=AF.Copy, scale=sc), r=r, w=w)
            else:
                P.add(eng, lambda e: e.tensor_scalar(out=out_ap, in0=in_ap, scalar1=sc, scalar2=None, op0=ALU.mult), r=r, w=w)

        def load_win_part(part, cb0, cb1, do_dma=True, do_cast=True):
            w_ = cb1 - cb0
            th = w_ // 2
            spl = [(0, th, "act"), (th, w_, "dve")]
            for kc in range(8 if do_dma else 0):
                lo = kc * AW
                ar = area_res(lo, lo + w_)
                P.add("sp", lambda e, kc=kc, lo=lo, w_=w_, cb0=cb0, cb1=cb1: e.dma_start(out=x1flat[:, lo:lo + w_], in_=dr["w_in_r"][:, kc, cb0:cb1]),
                      w=ar, dma="stgA%d" % kc)
            for kc in range(8 if do_cast else 0):
                lo = kc * AW
                ar = area_res(lo, lo + w_)
                for (c0, c1, eng) in spl:
                    cast_scaled(eng, win[:, kc, cb0 + c0:cb0 + c1], x1flat[:, lo + c0:lo + c1], gp8[:, kc:kc + 1], ar + ["gp8"],
                                ["win%d.%s.%s" % (kc, part, eng)])

        load_win_part("hi", HSPLIT, DIN)
        _consts(P, ident)
        P.add("pool", lambda e: e.memset(maskU[:, :], 1.0), w=["maskU"])
        P.add("pool", lambda e: e.affine_select(out=maskU[:, :], in_=maskU[:, :], compare_op=ALU.is_ge, fill=0.0,
                                                base=0, pattern=[[1, 128]], channel_multiplier=-1),
              r=["maskU"], w=["maskU"])
        P.add("pool", lambda e: e.memset(Lcum[:, :], -1.0 / 16.0), w=["Lcum"])
        P.add("pool", lambda e: e.affine_select(out=Lcum[:, :], in_=Lcum[:, :], compare_op=ALU.is_ge, fill=0.0,
                                                base=0, pattern=[[1, 128]], channel_multiplier=-1),
              r=["Lcum"], w=["Lcum"])
        P.add("pool", lambda e: e.memset(onesb[:, :], 1.0), w=["onesb"])
        P.add("pool", lambda e: e.memset(lrT[:, :], 1.0), w=["lrT"])
        P.add("pool", lambda e: e.memset(S[:, :, :], 0.0), w=["S0", "S1"])
        P.add("pool", lambda e: e.memset(Sb[:, :, :], 0.0), w=["Sb"])
        P.add("pool", lambda e: e.memset(qd[:, :, :], 0.0), w=["qd"])
        P.add("pool", lambda e: e.memset(lhsT2[:, :], 1.0), w=["lhsT2"])
        P.add("sp", lambda e: e.dma_start(out=lhsT2[0:1, :], in_=dr["lnb4"][0:1, :]), w=["lhsT2"], dma="c0")
        P.add("sp", lambda e: e.dma_start(out=rhs2[1:2, :], in_=dr["sgu_b4"][0:1, :]), w=["rhs2r1"], dma="c1")
        P.add("sp", lambda e: e.dma_start(out=wraw[:, :, :], in_=dr["sgu_w"].rearrange("h t s -> t h s")),
              w=["t1"], dma="c2")
        P.add("sp", lambda e: e.dma_start(out=wgk[0:17, :], in_=dr["wgk17"][:, :]), w=["wgk"], dma="c4")
        P.add("sp", lambda e: e.dma_start(out=lnG[:, :], in_=dr["lng"][0:1, :].partition_broadcast(128)), w=["lnG"], dma="c5")
        P.add("sp", lambda e: e.dma_start(out=gng[:, :], in_=dr["gng4"][0:1, :].partition_broadcast(128)), w=["gng"], dma="c6")
        P.add("sp", lambda e: e.dma_start(out=gpost[:, :], in_=dr["gpost"][0:1, :].partition_broadcast(128)), w=["gpost"], dma="c7")
        for h in range(4):
            P.add("pool", lambda e, h=h: e.affine_select(out=wraw[:, h, :], in_=wraw[:, h, :], compare_op=ALU.is_ge, fill=0.0,
                                                         base=0, pattern=[[-1, 128]], channel_multiplier=1),
                  r=["t1"], w=["t1"])
        P.add("pool", lambda e: e.tensor_copy(out=wrb[:, :, :], in_=wraw[:, :, :]), r=["t1"], w=["junk"])
        for h in range(4):
            P.add("pe", lambda e, h=h: e.transpose(pT[:, h * 128:(h + 1) * 128], wrb[:, h, :], ident[:, :]),
                  r=["junk", "ident"], w=["pT%d" % h])
        P.add("act", lambda e: e.activation(out=WmT[:, :, :], in_=pT[:, 0:512].rearrange("p (h c) -> p h c", h=4), func=AF.Copy),
              r=["pT0", "pT1", "pT2", "pT3"], w=["WmT"])
        P.add("pe", lambda e: e.matmul(pG[0:1, 0:512], lhsT=onesb[:, 0:1], rhs=WmT[:, :, :].rearrange("p h c -> p (h c)"),
                                       start=True, stop=True),
              r=["onesb", "WmT"], w=["pG.g", "pG.c"])
        P.add("act", lambda e: e.activation(out=rhs2[0:1, :], in_=pG[0:1, 0:512], func=AF.Copy),
              r=["pG.g", "pG.c"], w=["rhs2r0"])
        wo_loaded = [0]
        wo_cast = [0]

        def _wout_load(kc):
            a = kc % 4
            lo = 8 * (DIN - OFF_K) + a * D
            ar = area_res(lo, lo + D)
            P.add("sp", lambda e: e.dma_start(out=x1flat[:, lo:lo + D], in_=dr["w_out_r"][:, kc, :]), w=ar, dma="stgB%d" % a)

        def load_wout(kc):
            while wo_loaded[0] <= kc:
                _wout_load(wo_loaded[0])
                wo_loaded[0] += 1
            a = kc % 4
            lo = 8 * (DIN - OFF_K) + a * D
            ar = area_res(lo, lo + D)
            cast("pool" if kc % 2 == 0 else "act", wout[:, kc, :], x1flat[:, lo:lo + D], ar, ["wout%d" % kc])
            wo_cast[0] = kc + 1
            while wo_loaded[0] < min(8, wo_cast[0] + 3):
                _wout_load(wo_loaded[0])
                wo_loaded[0] += 1

        for kc0 in range(3):
            _wout_load(kc0)
            wo_loaded[0] += 1

        pbank = [0]

        def next_bank():
            b = pbank[0]
            pbank[0] ^= 1
            return b

        win_all = ["win%d" % k for k in range(8)]

        def proj_fm(c0, M, evac):
            b = next_bank()
            for kc in range(8):
                P.add("pe", lambda e, kc=kc, b=b: e.matmul(pP[0:M, b * 512:(b + 1) * 512], lhsT=win[:, kc, c0:c0 + M],
                                                           rhs=hT[:, kc, :], start=(kc == 0), stop=(kc == 7)),
                      r=win_res(kc, c0) + ["hT0", "hT1", "hT2", "hT3"], w=["pP%d" % b])
            evac(b)

        def proj_tm(ti, c0, evac):
            b = next_bank()
            for kc in range(8):
                P.add("pe", lambda e, kc=kc, b=b: e.matmul(pP[:, b * 512:(b + 1) * 512], lhsT=hT[:, kc, ti * 128:(ti + 1) * 128],
                                                           rhs=win[:, kc, c0:c0 + 512], start=(kc == 0), stop=(kc == 7)),
                      r=win_res(kc, c0) + ["hT%d" % ti], w=["pP%d" % b])
            evac(b)

        xcnt = [0]
        pG_bf = pG[:, :].bitcast(BF16)
        hbp = [hb, junk]
        hbr = ["hb", "junk"]
        stp = [st, stq]
        pTb = [pT, pG_bf]
        pTres = ["pT", "pG.T"]
        vbs = [vb[:, :], attnT[:, :, :].rearrange("p h c -> p (h c)")]
        vbr = ["vb", "attnT"]
        t1f = t1
        kdall = mixT
        kdT_all = uT[:, 0:2, :].rearrange("p a c -> p (a c)")

        vb_all = qT[:, :, :].rearrange("p a c -> p (a c)").bitcast(BF16)

        vb_alt = [(vb[:, :], "vb"), (vng[:, :], "vng"), (attnT[:, :, :].rearrange("p h c -> p (h c)"), "attnT"), (ob[:, :], "ob")]

        def vbuf(g, ti):
            if g % 2 == 0:
                return vb_all[:, ti * 512:(ti + 1) * 512], "qT"
            return vb_alt[ti]

        def prefix_B(g):
            for ti in range(4):
                tau = 4 * g + ti
                sl = xcnt[0] % NXS
                xcnt[0] += 1
                par = ti % 2
                P.add("sp", lambda e, tau=tau, sl=sl: e.dma_start(out=xs[sl][:, :], in_=dr["xp"][tau * 128:(tau + 1) * 128, :]),
                      w=["xs%d" % sl], dma="xs%d" % sl)
                _prenorm_a_nog(P, xs[sl][:, :], "xs%d" % sl, stp[par], hbp[par], sfx="M%d" % par, hb_res=hbr[par])
                yield
                _prenorm_b(P, hbp[par], pTb[par], ident, hT[:, :, ti * 128:(ti + 1) * 128], "hT%d" % ti,
                           evac_eng=("act" if par == 0 else "dve"), hb_res=hbr[par], pres=pTres[par])
                yield
            yield ("wait", "k_read")
            for c in range(2):
                proj_fm(OFF_K + c * 128, 128,
                        lambda b, c=c: P.add("act", lambda e: e.activation(out=kT[:, c, :], in_=pP[:, b * 512:(b + 1) * 512], func=AF.Copy),
                                             r=["pP%d" % b], w=["kT"]))
                yield
            yield ("wait", "lr_read")
            proj_fm(OFF_LR, 16,
                    lambda b: P.add("dve", lambda e: e.tensor_copy(out=lrT[0:16, :], in_=pP[0:16, b * 512:(b + 1) * 512]),
                                    r=["pP%d" % b], w=["lrT"]))
            yield
            for ti in range(4):
                vv, vr = vbuf(g, ti)
                if ti % 2 == 0:
                    proj_tm(ti, OFF_BV,
                            lambda b, vv=vv, vr=vr: P.add("act", lambda e: e.activation(out=vv, in_=pP[:, b * 512:(b + 1) * 512], func=AF.Copy),
                                                          r=["pP%d" % b], w=[vr]))
                else:
                    proj_tm(ti, OFF_BV,
                            lambda b, vv=vv, vr=vr: P.add("dve", lambda e: e.tensor_copy(out=vv, in_=pP[:, b * 512:(b + 1) * 512]),
                                                          r=["pP%d" % b], w=[vr]))
                yield

        def prefix_A(g):
            bk = ["pA", "pA", "pO", "pO"]
            for ti in range(4):
                P.add("pe", lambda e, ti=ti: e.matmul(pAO[:, ti * 256:(ti + 1) * 256], lhsT=lrT[0:17, ti * 128:(ti + 1) * 128],
                                                      rhs=wgk[0:17, :], start=True, stop=True),
                      r=["lrT", "wgk"], w=[bk[ti]])
            yield ("signal", "lr_read")
            P.add("act", lambda e: e.activation(out=t1f[:, :], in_=pAO[:, :], func=AF.Exp, scale=-1.0), r=["pA", "pO"], w=["t1"])
            yield
            P.add("act", lambda e: e.activation(out=t1f[:, :], in_=t1f[:, :], func=AF.Ln, bias=1.0), r=["t1"], w=["t1"])
            yield
            for ti in range(4):
                for c in range(2):
                    o0 = ti * 256 + c * 128
                    P.add("pe", lambda e, o0=o0: e.matmul(pAO[:, o0:o0 + 128], lhsT=t1f[:, o0:o0 + 128], rhs=Lcum[:, :],
                                                          start=True, stop=True),
                          r=["t1", "Lcum"], w=[bk[ti]])
            yield
            P.add("act", lambda e: e.activation(out=vav[:, :], in_=pAO[:, :], func=AF.Exp), r=["pA", "pO"], w=["va", "vtmp"])
            yield
            P.add("act", lambda e: e.activation(out=sgs[:, :], in_=pAO[:, :], func=AF.Exp, scale=-1.0), r=["pA", "pO"], w=["sg", "sgg"])
            yield
            for ti in range(4):
                for c in range(2):
                    o0 = ti * 256 + c * 128
                    P.add("dve", lambda e, o0=o0, ti=ti, c=c: e.scalar_tensor_tensor(
                        out=kdall[:, ti * 2 + c, :], in0=kT[:, c, ti * 128:(ti + 1) * 128], scalar=vav[:, o0 + 127:o0 + 128],
                        in1=sgs[:, o0:o0 + 128], op0=ALU.mult, op1=ALU.mult),
                        r=["kT", "va", "vtmp", "sg", "sgg"], w=["mixTa", "mixTb"])
                yield
            yield ("signal", "k_read")
            for q in range(8):
                P.add("pe", lambda e, q=q: e.transpose(pT[:, q * 128:(q + 1) * 128], kdall[:, q, :], ident[:, :]),
                      r=["mixTa", "mixTb", "ident"], w=["pT%d" % q])
            P.add("act", lambda e: e.activation(out=kdT_all, in_=pT[:, :], func=AF.Copy),
                  r=["pT%d" % q for q in range(8)], w=["uT"])
            yield
            for ti in range(4):
                vv, vr = vbuf(g, ti)
                psb, psr = (pS, "pS") if ti % 2 == 0 else (pZ, "pZ")
                for h in range(4):
                    c = h // 2
                    o0 = ti * 256 + c * 128
                    P.add("pe", lambda e, h=h, o0=o0, vv=vv, psb=psb: e.matmul(psb[:, h * 128:(h + 1) * 128], lhsT=kdT_all[:, o0:o0 + 128],
                                                                                rhs=vv[:, h * 128:(h + 1) * 128], start=True, stop=True),
                          r=["uT", vr], w=[psr])
                yield
                for h in range(4):
                    c, r0 = h // 2, (h % 2) * 64
                    o0 = ti * 256 + c * 128
                    P.add("dve", lambda e, h=h, c=c, r0=r0, o0=o0, psb=psb: e.scalar_tensor_tensor(
                        out=S[r0:r0 + 64, c, :], in0=S[r0:r0 + 64, c, :], scalar=vav[r0:r0 + 64, o0 + 127:o0 + 128],
                        in1=psb[r0:r0 + 64, h * 128:(h + 1) * 128], op0=ALU.mult, op1=ALU.add),
                        r=["S%d" % c, "va", "vtmp", psr], w=["S%d" % c])
                yield
            P.add("pool", lambda e: e.tensor_copy(out=Sb[:, :, :], in_=S[:, :, :]), r=["S0", "S1"], w=["Sb"])
            yield

        CSLOT = 3072
        ccnt = [0]

        conv_pending = []

        def conv_bufs(k):
            lo = k * CSLOT
            ar = area_res(lo, lo + CSLOT)
            stage = x1flat[:, lo:lo + 2048]
            bfv = x1flat[:, lo + 2048:lo + 3072].bitcast(BF16)
            return ar, stage, bfv

        def conv_load(u):
            k = ccnt[0] % 3
            ccnt[0] += 1
            ar, stage, bfv = conv_bufs(k)
            if u < NJ:
                src = dr["w_up_r"][u, :, :, :].rearrange("p k c -> p (k c)")
            else:
                src = dr["w_down_r"][:, 2 * (u - NJ):2 * (u - NJ) + 2, :].rearrange("p j d -> p (j d)")
            P.add(CONV_Q, lambda e: e.dma_start(out=stage, in_=src), w=ar, dma="cvl%d" % k)
            conv_pending.append((u, k))

        def conv_finish():
            u, k = conv_pending.pop(0)
            ar, stage, bfv = conv_bufs(k)
            if u < NJ:
                dst = dr["scr_up"][u, :, :, :].rearrange("p k c -> p (k c)")
            else:
                dst = dr["scr_dn"][u - NJ, :, :, :].rearrange("p j d -> p (j d)")
            cast("act", bfv[:, 0:1024], stage[:, 0:1024], ar, ["cv%d.a" % k])
            cast("pool", bfv[:, 1024:2048], stage[:, 1024:2048], ar, ["cv%d.b" % k])
            yield
            P.add(CONV_Q, lambda e: e.dma_start(out=dst, in_=bfv), r=["cv%d.a" % k, "cv%d.b" % k] + ar, w=["scrw%d" % u], dma="cvs%d" % k)
            yield

        def conv_thread(units, last=False):
            for u in units:
                conv_load(u)
                yield
                if len(conv_pending) > 2:
                    for _ in conv_finish():
                        yield
            if last:
                while conv_pending:
                    for _ in conv_finish():
                        yield

        def run_sched(gens, pre=()):
            sig = set(pre)
            live = [[x, None] for x in gens if x is not None]
            while live:
                progressed = False
                for item in list(live):
                    x, wk = item
                    if wk is not None:
                        if wk not in sig:
                            continue
                        item[1] = None
                    progressed = True
                    try:
                        r = next(x)
                    except StopIteration:
                        live.remove(item)
                        continue
                    if isinstance(r, tuple):
                        if r[0] == "signal":
                            sig.add(r[1])
                        elif r[0] == "wait" and r[1] not in sig:
                            item[1] = r[1]
                if not progressed:
                    raise RuntimeError("op-thread deadlock: %s" % [i[1] for i in live])

        n_pg = (NPT // 4 - 1) if BATCH_PREFIX else 0
        n_pre = max(1, NPT // 4 - 1)
        per_g = (8 + n_pre - 1) // n_pre
        if n_pg > 0:
            for kc in range(0, min(8, per_g)):
                load_wout(kc)
            run_sched([prefix_B(0)], pre=("k_read", "lr_read"))
            load_win_part("lo", 0, HSPLIT, do_cast=False)
            n_units = NJ + NJ // 2
            n_cg = max(1, n_pg - 1)
            upg = (n_units + n_cg - 1) // n_cg
            for g in range(n_pg):
                for kc in range((g + 1) * per_g, min(8, (g + 2) * per_g)):
                    load_wout(kc)
                gg = g - 1 if n_pg > 1 else g
                units = list(range(gg * upg, min(n_units, (gg + 1) * upg))) if gg >= 0 else []
                run_sched([prefix_A(g), prefix_B(g + 1) if g + 1 < n_pg else None, conv_thread(units, last=(g == n_pg - 1))])
                if g == 0:
                    load_win_part("lo", 0, HSPLIT, do_dma=False)

        def group_head(g):
            if n_pg == 0 or g >= n_pg:
                for kc in range(max(g, n_pg + 1) * per_g if n_pg > 0 else g * per_g, min(8, (g + 1) * per_g)):
                    load_wout(kc)
            for ti in range(4):
                tau = 4 * g + ti
                sl = xcnt[0] % NXS
                xcnt[0] += 1
                par = ti % 2
                P.add("sp", lambda e, tau=tau, sl=sl: e.dma_start(out=xs[sl][:, :], in_=dr["xp"][tau * 128:(tau + 1) * 128, :]),
                      w=["xs%d" % sl], dma="xs%d" % sl)
                _prenorm_a_nog(P, xs[sl][:, :], "xs%d" % sl, sth[par], hb, sfx="H%d" % par, hb_res="hb")
                yield
                if ti == 3:
                    yield ("wait", "h_gla")
                    yield ("wait", "h_sgu")
                _prenorm_b(P, hb, pT, ident, hT[:, :, ti * 128:(ti + 1) * 128], "hT%d" % ti, hb_res="hb")
                yield
            yield ("wait", "kq_read")
            for c in range(2):
                proj_fm(OFF_K + c * 128, 128,
                        lambda b, c=c: P.add("act", lambda e: e.activation(out=kT[:, c, :], in_=pP[:, b * 512:(b + 1) * 512], func=AF.Copy),
                                             r=["pP%d" % b], w=["kT"]))
                yield
            yield ("wait", "lr_read")
            proj_fm(OFF_LR, 16,
                    lambda b: P.add("act", lambda e: e.activation(out=lrT[0:16, :], in_=pP[0:16, b * 512:(b + 1) * 512], func=AF.Copy),
                                    r=["pP%d" % b], w=["lrT"]))
            yield
            for c in range(2):
                proj_fm(OFF_Q + c * 128, 128,
                        lambda b, c=c: P.add("dve", lambda e: e.tensor_copy(out=qT[:, c, :], in_=pP[:, b * 512:(b + 1) * 512]),
                                             r=["pP%d" % b], w=["qT"]))
                yield
            yield ("wait", "u_read")
            for c in range(4):
                proj_fm(OFF_U + c * 128, 128,
                        lambda b, c=c: P.add("act", lambda e: e.activation(out=uT[:, c, :], in_=pP[:, b * 512:(b + 1) * 512], func=AF.Gelu),
                                             r=["pP%d" % b], w=["uT"]))
                yield

        ALLSIG = ("h_gla", "h_sgu", "kq_read", "lr_read", "u_read")
        pending_out = [None]
        gate_done = set()
        for g in range(NG):
            own = g >= NPT // 4
            full = [own or (g == NPT // 4 - 1 and ti == 3) for ti in range(4)]
            if g < n_pg:
                continue
            if g == n_pg:
                run_sched([group_head(g)], pre=ALLSIG)
            EpS = [(Ep, "Ep"), (Ep2, "Ep2")]
            EmS = [(Em, "Em"), (Em2, "Em2")]

            def tile_gate(ti, inline=False):
                gc0, gc1 = ti * 128, (ti + 1) * 128
                Ept, Epn = EpS[ti % 2]
                Emt, Emn = EmS[ti % 2]
                if not inline:
                    yield ("wait", "gate_free")
                P.add("pe", lambda e: e.matmul(pG[:, 0:256], lhsT=lrT[0:17, gc0:gc1], rhs=wgk[0:17, :], start=True, stop=True),
                      r=["lrT", "wgk"], w=["pG.g"])
                P.add("act", lambda e: e.activation(out=el[:, :], in_=pG[:, 0:256], func=AF.Exp, scale=-1.0), r=["pG.g"], w=["el"])
                yield ("signal", "lr_read")
                P.add("act", lambda e: e.activation(out=lg[:, :], in_=el[:, :], func=AF.Ln, bias=1.0), r=["el"], w=["lg"])
                yield
                for c in range(2):
                    P.add("pe", lambda e, c=c: e.matmul(pG[:, 256 + c * 128:256 + (c + 1) * 128], lhsT=lg[:, c * 128:(c + 1) * 128],
                                                        rhs=Lcum[:, :], start=True, stop=True),
                          r=["lg", "Lcum"], w=["pG.c"])
                P.add("act", lambda e: e.activation(out=Ept[:, :, :].rearrange("p c i -> p (c i)"), in_=pG[:, 256:512], func=AF.Exp),
                      r=["pG.c"], w=[Epn])
                P.add("act", lambda e: e.activation(out=Emt[:, :, :].rearrange("p c i -> p (c i)"), in_=pG[:, 256:512], func=AF.Exp, scale=-1.0),
                      r=["pG.c"], w=[Emn])
                gate_done.add((g, ti))
                yield

            def tile_gla(ti, fl):
                tc0, tc1 = ti * 128, (ti + 1) * 128
                proj_tm(ti, OFF_BV,
                        lambda b: P.add("dve", lambda e: e.tensor_copy(out=vb[:, :], in_=pP[:, b * 512:(b + 1) * 512]),
                                        r=["pP%d" % b], w=["vb"]))
                if fl:
                    proj_tm(ti, OFF_G,
                            lambda b: P.add("act", lambda e: e.activation(out=sg[:, :], in_=pP[:, b * 512:(b + 1) * 512], func=AF.Silu),
                                            r=["pP%d" % b], w=["sg"]))
                yield ("signal", "h_gla")
                Ept, Epn = EpS[ti % 2]
                Emt, Emn = EmS[ti % 2]
                if (g, ti) in gate_done:
                    yield ("signal", "lr_read")
                else:
                    for r_ in tile_gate(ti, inline=True):
                        yield r_
                yield ("signal", "gate_free")
                for c in range(2):
                    P.add("dve", lambda e, c=c: e.scalar_tensor_tensor(
                        out=kdec[:, c, :], in0=kT[:, c, tc0:tc1], scalar=Ept[:, c, 127:128], in1=Emt[:, c, :],
                        op0=ALU.mult, op1=ALU.mult), r=["kT", Epn, Emn], w=["kdec"])
                yield
                if fl:
                    for h in range(4):
                        c, r0 = h // 2, (h % 2) * 64
                        P.add("dve", lambda e, h=h, c=c, r0=r0: e.scalar_tensor_tensor(
                            out=qd[r0:r0 + 64, h, :], in0=qT[r0:r0 + 64, c, tc0:tc1], scalar=0.125, in1=Ept[r0:r0 + 64, c, :],
                            op0=ALU.mult, op1=ALU.mult), r=["qT", Epn], w=["qd"])
                    P.add("pool", lambda e: e.tensor_tensor(
                        out=kd[:, :, :], in0=kT[:, :, tc0:tc1], in1=Emt[:, :, :], op=ALU.mult), r=["kT", Emn], w=["kd"])
                yield ("signal", "kq_read")
                for c in range(2):
                    P.add("pe", lambda e, c=c: e.transpose(pT[:, c * 128:(c + 1) * 128], kdec[:, c, :], ident[:, :]),
                          r=["kdec", "ident"], w=["pT%d" % c])
                P.add("dve", lambda e: e.tensor_copy(out=kdecT[:, :], in_=pT[:, 0:256]), r=["pT0", "pT1"], w=["kdecT"])
                yield
                if fl:
                    for h in range(4):
                        c = h // 2
                        P.add("pe", lambda e, h=h, c=c: e.matmul(pA[:, h * 128:(h + 1) * 128], lhsT=kd[:, c, :],
                                                                 rhs=qd[:, h, :], start=True, stop=True),
                              r=["kd", "qd"], w=["pA"])
                    P.add("dve", lambda e: e.tensor_tensor(out=attnT[:, :, :], in0=pA[:, :].rearrange("p (h c) -> p h c", h=4),
                                                           in1=maskU[:, :].unsqueeze(1).to_broadcast([128, 4, 128]), op=ALU.mult),
                          r=["pA", "maskU"], w=["attnT"])
                    yield
                    for h in range(4):
                        c = h // 2
                        P.add("pe", lambda e, h=h: e.matmul(pO[:, h * 128:(h + 1) * 128], lhsT=attnT[:, h, :],
                                                            rhs=vb[:, h * 128:(h + 1) * 128], start=True, stop=False),
                              r=["attnT", "vb"], w=["pO"])
                        P.add("pe", lambda e, h=h, c=c: e.matmul(pO[:, h * 128:(h + 1) * 128], lhsT=qd[:, h, :],
                                                                 rhs=Sb[:, c, :], start=False, stop=True),
                              r=["qd", "Sb"], w=["pO"])
                    yield
                for h in range(4):
                    c = h // 2
                    P.add("pe", lambda e, h=h, c=c: e.matmul(pS[:, h * 128:(h + 1) * 128], lhsT=kdecT[:, c * 128:(c + 1) * 128],
                                                             rhs=vb[:, h * 128:(h + 1) * 128], start=True, stop=True),
                          r=["kdecT", "vb"], w=["pS"])
                yield
                for h in range(4):
                    c, r0 = h // 2, (h % 2) * 64
                    P.add("dve", lambda e, h=h, c=c, r0=r0: e.scalar_tensor_tensor(
                        out=S[r0:r0 + 64, c, :], in0=S[r0:r0 + 64, c, :], scalar=Ept[r0:r0 + 64, c, 127:128],
                        in1=pS[r0:r0 + 64, h * 128:(h + 1) * 128], op0=ALU.mult, op1=ALU.add),
                        r=["S%d" % c, Epn, "pS"], w=["S%d" % c])
                P.add("pool", lambda e: e.tensor_copy(out=Sb[:, :, :], in_=S[:, :, :]), r=["S0", "S1"], w=["Sb"])
                yield
                if not fl:
                    return
                for h in range(4):
                    P.add("act", lambda e, h=h: e.activation(out=ob[:, h * 128:(h + 1) * 128], in_=pO[:, h * 128:(h + 1) * 128],
                                                             func=AF.Square, accum_out=st[:, 4 + h:5 + h]),
                          r=["pO"], w=["ob", "st4"])
                yield
                P.add("act", lambda e: e.activation(out=st[:, 8:12], in_=st[:, 4:8], func=AF.Ln, scale=1.0 / 128, bias=EPS),
                      r=["st4"], w=["st8"])
                P.add("act", lambda e: e.activation(out=st[:, 12:16], in_=st[:, 8:12], func=AF.Exp, scale=-0.5),
                      r=["st8"], w=["st12"])
                P.add("pool", lambda e: e.tensor_tensor(out=sgg[:, :], in0=sg[:, :], in1=gng[:, :], op=ALU.mult),
                      r=["sg", "gng"], w=["sgg"])
                yield
                for h in range(4):
                    P.add("dve", lambda e, h=h: e.scalar_tensor_tensor(
                        out=ob[:, h * 128:(h + 1) * 128], in0=pO[:, h * 128:(h + 1) * 128], scalar=st[:, 12 + h:13 + h],
                        in1=sgg[:, h * 128:(h + 1) * 128], op0=ALU.mult, op1=ALU.mult),
                        r=["pO", "st12", "sgg"], w=["ob"])
                yield
                for h in range(4):
                    P.add("pe", lambda e, h=h: e.transpose(pT[:, 256 + h * 128:256 + (h + 1) * 128], ob[:, h * 128:(h + 1) * 128], ident[:, :]),
                          r=["ob", "ident"], w=["pT%d" % (2 + h)])
                P.add("act", lambda e: e.activation(out=mixT[:, 4:8, :], in_=pT[:, 256:768].rearrange("p (h c) -> p h c", h=4), func=AF.Copy),
                      r=["pT2", "pT3", "pT4", "pT5"], w=["mixTb"])
                yield

            def tile_sgu(ti):
                tc0, tc1 = ti * 128, (ti + 1) * 128
                proj_tm(ti, OFF_V,
                        lambda b: P.add("act", lambda e: e.activation(out=va[:, :], in_=pP[:, b * 512:(b + 1) * 512], func=AF.Gelu),
                                        r=["pP%d" % b], w=["va"]))
                yield ("signal", "h_sgu")
                P.add("dve", lambda e: e.bn_stats(out=bst[:, 0:6], in_=va[:, :]), r=["va"], w=["bst"])
                P.add("dve", lambda e: e.bn_aggr(out=mv[:, 0:2], in_=bst[:, 0:6]), r=["bst"], w=["mv"])
                yield
                P.add("act", lambda e: e.activation(out=st[:, 3:4], in_=mv[:, 1:2], func=AF.Ln, bias=EPS), r=["mv"], w=["st3a"])
                P.add("act", lambda e: e.activation(out=st[:, 3:4], in_=st[:, 3:4], func=AF.Exp, scale=-0.5), r=["st3a"], w=["st3"])
                P.add("dve", lambda e: e.scalar_tensor_tensor(out=vtmp[:, :], in0=va[:, :], scalar=mv[:, 0:1], in1=lnG[:, :],
                                                              op0=ALU.subtract, op1=ALU.mult),
                      r=["va", "mv", "lnG"], w=["vtmp"])
                yield
                P.add("act", lambda e: e.activation(out=vng[:, :], in_=vtmp[:, :], func=AF.Copy, scale=st[:, 3:4]),
                      r=["vtmp", "st3"], w=["vng"])
                yield
                for h in range(4):
                    P.add("pe", lambda e, h=h: e.matmul(pZ[:, h * 128:(h + 1) * 128], lhsT=vng[:, h * 128:(h + 1) * 128],
                                                        rhs=WmT[:, h, :], start=True, stop=False),
                          r=["vng", "WmT"], w=["pZ"])
                    P.add("pe", lambda e, h=h: e.matmul(pZ[:, h * 128:(h + 1) * 128], lhsT=lhsT2[0:2, h * 128:(h + 1) * 128],
                                                        rhs=rhs2[0:2, h * 128:(h + 1) * 128], start=False, stop=True),
                          r=["lhsT2", "rhs2r0", "rhs2r1"], w=["pZ"])
                yield
                P.add("dve", lambda e: e.tensor_tensor(
                    out=mixT[:, 0:4, :], in0=pZ[:, :].rearrange("p (h c) -> p h c", h=4), in1=uT[:, :, tc0:tc1], op=ALU.mult),
                    r=["pZ", "uT"], w=["mixTa"])
                yield ("signal", "u_read")

            def tile_out(ti, g=g):
                tau = 4 * g + ti
                for half in range(2):
                    for kc in range(8):
                        P.add("pe", lambda e, kc=kc, half=half: e.matmul(pP[:, half * 512:(half + 1) * 512], lhsT=mixT[:, kc, :],
                                                                         rhs=wout[:, kc, half * 512:(half + 1) * 512],
                                                                         start=(kc == 0), stop=(kc == 7)),
                              r=["mixTa", "mixTb", "wout%d" % kc], w=["pP%d" % half])
                P.add("act", lambda e: e.activation(out=junk[:, :], in_=pP[:, :], func=AF.Square, accum_out=st[:, 0:1]),
                      r=["pP0", "pP1"], w=["junk", "st0"])
                P.add("act", lambda e: e.activation(out=st[:, 1:2], in_=st[:, 0:1], func=AF.Ln, scale=1.0 / D, bias=EPS),
                      r=["st0"], w=["st1"])
                P.add("act", lambda e: e.activation(out=st[:, 2:3], in_=st[:, 1:2], func=AF.Exp, scale=-0.5),
                      r=["st1"], w=["st2"])
                P.add("dve", lambda e: e.scalar_tensor_tensor(out=t1[:, :], in0=pP[:, :], scalar=st[:, 2:3], in1=gpost[:, :],
                                                              op0=ALU.mult, op1=ALU.mult),
                      r=["pP0", "pP1", "st2", "gpost"], w=["t1"])
                yield
                slot = tau - (NPT - 1)
                P.add("sp", lambda e: e.dma_start(out=x1all[:, slot, :], in_=dr["xp"][tau * 128:(tau + 1) * 128, :]),
                      w=["x1_%d" % slot], dma="x1ld%d" % slot)
                P.add("pool", lambda e: e.tensor_tensor(out=x1all[:, slot, :], in0=x1all[:, slot, :], in1=t1[:, :], op=ALU.add),
                      r=["x1_%d" % slot, "t1"], w=["x1_%d" % slot])
                yield

            def run_many(gens):
                gens = [x for x in gens if x is not None]
                while gens:
                    for x in list(gens):
                        try:
                            next(x)
                        except StopIteration:
                            gens.remove(x)

            for ti in range(4):
                fl = full[ti]
                th = [pending_out[0], tile_gla(ti, fl), tile_sgu(ti) if fl else None]
                if ti + 1 <= 3:
                    th.append(tile_gate(ti + 1))
                if ti == 3 and g + 1 < NG:
                    th.append(group_head(g + 1))
                run_sched(th)
                pending_out[0] = tile_out(ti) if fl else None
        run_sched([pending_out[0]])
        if n_pg > 0:
            P.add("sp", lambda e: e.nop(), r=["scrw%d" % u for u in range(NJ + NJ // 2)], w=[], force=True)
        if dbg is not None:
            for slot in range(NOT + 1):
                P.add("sp", lambda e, slot=slot: e.dma_start(out=dbg[slot * 128:(slot + 1) * 128, :], in_=x1all[:, slot, :]),
                      r=["x1_%d" % slot], w=["dbg%d" % slot], dma="dbg", force=True)
            P.add("sp", lambda e: e.nop(), r=["dbg%d" % s for s in range(NOT + 1)], w=[], force=True)
        build_block(nc, P)


def _phase_F(nc, dr, x1all, NOT, out):
    NGF = NOT // 4
    scr = dr["scr_up"]
    LIMIT[0] = int(os.environ.get("MK_LIMIT_F", "100000000"))
    P = Prog()
    with contextlib.ExitStack() as st_:
        def sb(name, shape, dt):
            return st_.enter_context(nc.sbuf_tensor("sb_" + name, shape, dt))

        def ps(name, shape, dt):
            return st_.enter_context(nc.psum_tensor("ps_" + name, shape, dt))

        wdn = sb("wdn", [128, NJ, D], BF16)
        NWU = 3
        wu = [sb("wu%d" % i, [128, 8, 256], BF16) for i in range(NWU)]
        GT = sb("GT", [128, NJ, 512], BF16)
        h2T = [sb("h2T%d" % i, [128, 8, 514], BF16) for i in range(2)]
        NHB = 2
        hb = [sb("hbF%d" % i, [128, D], BF16) for i in range(NHB)]
        NY = 3
        ya = [sb("ya%d" % i, [128, 512], F32) for i in range(NY)]
        yb = [sb("yb%d" % i, [128, 512], F32) for i in range(NY)]
        ga = [sb("ga%d" % i, [128, 512], F32) for i in range(NY)]
        t1 = [sb("t1F%d" % i, [128, D], F32) for i in range(2)]
        junk = sb("junkF", [128, D], BF16)
        st = [sb("stF%d" % i, [128, 8], F32) for i in range(NHB)]
        st2 = [sb("stG%d" % i, [128, 8], F32) for i in range(2)]
        Hs = [sb("Hs%d" % i, [128, 2 * NJ, 2], F32) for i in range(2)]
        corr = [sb("corr%d" % i, [128, 8], F32) for i in range(NY)]
        ident = sb("identF", [128, 128], BF16)
        cw = sb("cw", [128, 2 * NJ, 3], F32)
        cb = sb("cb", [128, 2 * NJ], F32)
        gffn = sb("gffn", [128, D], F32)
        gb = sb("gbF", [128, D], F32)

        pU = [ps("pU%d" % i, [128, 1024], F32) for i in range(2)]
        pF = ps("pF", [128, 1024], F32)
        pH = ps("pH", [128, 512], F32)
        pT = ps("pTF", [128, 1024], BF16)

        _consts(P, ident)
        P.add("sp", lambda e: e.dma_start(out=cw[:, :, :], in_=dr["cw"][:, :, :]), w=["cw"], dma="f0")
        P.add("sp", lambda e: e.dma_start(out=cb[:, :], in_=dr["cb"][:, :]), w=["cb"], dma="f1")
        P.add("sp", lambda e: e.dma_start(out=gffn[:, :], in_=dr["gffn"][0:1, :].partition_broadcast(128)), w=["gffn"], dma="f2")
        P.add("sp", lambda e: e.dma_start(out=gb[:, :], in_=dr["g2"][0:1, :].partition_broadcast(128)), w=["gb"], dma="f3")

        hcnt = [0]

        def prenorm_a(slot):
            k = hcnt[0] % NHB
            hcnt[0] += 1
            _prenorm_a(P, x1all[:, slot, :], "x1_%d" % slot, gb, st[k], hb[k], sfx="F%d" % k)
            return k

        def prenorm_b(k, dst3, dst_res):
            _prenorm_b(P, hb[k], pT, ident, dst3, dst_res, sfx="F%d" % k)

        k = prenorm_a(0)
        prenorm_b(k, h2T[0][:, :, 2:130], "h2T0_0")
        P.add("pool", lambda e: e.tensor_copy(out=h2T[0][:, :, 0:2], in_=h2T[0][:, :, 128:130]), r=["h2T0_0"], w=["h2Th"])
        for ti in range(4):
            k = prenorm_a(1 + ti)
            prenorm_b(k, h2T[0][:, :, 2 + ti * 128:2 + (ti + 1) * 128], "h2T0_%d" % ti)
        def load_wdn(jp):
            P.add("sp", lambda e: e.dma_start(out=wdn[:, 2 * jp:2 * jp + 2, :], in_=dr["scr_dn"][jp, :, :, :]),
                  w=["wdn%d" % (2 * jp), "wdn%d" % (2 * jp + 1)], dma="wdn%d" % jp)
        wcnt = [0]
        ycnt = [0]
        ecnt = [0]
        for g in range(NGF):
            hp = g % 2
            hT_ = h2T[hp]
            h2res = ["h2T%d_%d" % (hp, t) for t in range(4)]
            for j in range(NJ):
                ws = wcnt[0] % NWU
                wcnt[0] += 1
                pu = pU[j % 2]
                pur = "pU%d" % (j % 2)
                wur = ["wu%d.a" % ws, "wu%d.b" % ws]
                P.add("sp", lambda e, j=j, ws=ws: e.dma_start(out=wu[ws][:, :, :], in_=scr[j, :, :, :]),
                      w=wur, dma="wu%d" % ws)
                if g == 0 and 2 <= j < 2 + NJ // 2:
                    load_wdn(j - 2)
                for half in range(2):
                    for kc in range(8):
                        P.add("pe", lambda e, kc=kc, half=half, ws=ws, pu=pu, hT_=hT_: e.matmul(
                            pu[:, half * 512:(half + 1) * 512], lhsT=wu[ws][:, kc, half * 128:(half + 1) * 128],
                            rhs=hT_[:, kc, 2:514], start=(kc == 0), stop=(kc == 7)),
                            r=wur + h2res, w=[pur + ".%d" % half])
                    if g == 0:
                        for kc in range(8):
                            P.add("pe", lambda e, kc=kc, half=half, ws=ws, hT_=hT_: e.matmul(
                                pH[:, half * 2:half * 2 + 2], lhsT=wu[ws][:, kc, half * 128:(half + 1) * 128],
                                rhs=hT_[:, kc, 0:2], start=(kc == 0), stop=(kc == 7)),
                                r=wur + ["h2Th"], w=["pH.%d" % half])
                ys = ycnt[0] % NY
                ycnt[0] += 1
                for half, y in ((0, ya[ys]), (1, yb[ys])):
                    ci = half * NJ + j
                    yr = "y%d_%d" % (half, ys)
                    src = pu[:, half * 512:(half + 1) * 512]
                    sr = pur + ".%d" % half
                    if g == 0:
                        hsrc = pH[:, half * 2:half * 2 + 2]
                        hres = "pH.%d" % half
                    else:
                        hsrc = Hs[g % 2][:, ci, :]
                        hres = "Hs%d" % (g % 2)
                    P.add("act", lambda e, y=y, src=src, ci=ci: e.activation(out=y[:, :], in_=src, func=AF.Identity,
                                                                              scale=cw[:, ci, 2:3], bias=cb[:, ci:ci + 1]),
                          r=[sr, "cw", "cb"], w=[yr])
                    if g < NGF - 1:
                        P.add("act", lambda e, src=src, ci=ci, g=g: e.activation(out=Hs[(g + 1) % 2][:, ci, :], in_=src[:, 510:512], func=AF.Copy),
                              r=[sr], w=["Hs%d" % ((g + 1) % 2)])
                    P.add("dve", lambda e, y=y, src=src, ci=ci: e.scalar_tensor_tensor(
                        out=y[:, 1:512], in0=src[:, 0:511], scalar=cw[:, ci, 1:2], in1=y[:, 1:512], op0=ALU.mult, op1=ALU.add),
                        r=[sr, "cw", yr], w=[yr])
                    P.add("dve", lambda e, y=y, src=src, ci=ci: e.scalar_tensor_tensor(
                        out=y[:, 2:512], in0=src[:, 0:510], scalar=cw[:, ci, 0:1], in1=y[:, 2:512], op0=ALU.mult, op1=ALU.add),
                        r=[sr, "cw", yr], w=[yr])
                    if g == 0:
                        P.add("dve", lambda e, y=y, hsrc=hsrc, ci=ci: e.scalar_tensor_tensor(
                            out=y[:, 0:1], in0=hsrc[:, 1:2], scalar=cw[:, ci, 1:2], in1=y[:, 0:1], op0=ALU.mult, op1=ALU.add),
                            r=[hres, "cw", yr], w=[yr])
                        P.add("dve", lambda e, y=y, hsrc=hsrc, ci=ci: e.scalar_tensor_tensor(
                            out=y[:, 0:2], in0=hsrc[:, 0:2], scalar=cw[:, ci, 0:1], in1=y[:, 0:2], op0=ALU.mult, op1=ALU.add),
                            r=[hres, "cw", yr], w=[yr])
                    else:
                        cc = corr[ys][:, half * 2:half * 2 + 2]
                        c2 = corr[ys][:, 4 + half:5 + half]
                        cr = "corr%d.%d" % (ys, half)
                        P.add("pool", lambda e, cc=cc, hsrc=hsrc, ci=ci: e.tensor_scalar(out=cc, in0=hsrc[:, 0:2], scalar1=cw[:, ci, 0:1], scalar2=None, op0=ALU.mult),
                              r=[hres, "cw"], w=[cr])
                        P.add("pool", lambda e, c2=c2, hsrc=hsrc, ci=ci: e.tensor_scalar(out=c2, in0=hsrc[:, 1:2], scalar1=cw[:, ci, 1:2], scalar2=None, op0=ALU.mult),
                              r=[hres, "cw"], w=[cr + "b"])
                        P.add("pool", lambda e, cc=cc, c2=c2: e.tensor_tensor(out=cc[:, 0:1], in0=cc[:, 0:1], in1=c2, op=ALU.add),
                              r=[cr, cr + "b"], w=[cr])
                        P.add("dve", lambda e, y=y, cc=cc: e.tensor_tensor(out=y[:, 0:2], in0=y[:, 0:2], in1=cc, op=ALU.add),
                              r=[cr, yr], w=[yr])
                P.add("act", lambda e, ys=ys: e.activation(out=ga[ys][:, :], in_=ya[ys][:, :], func=AF.Gelu_apprx_tanh),
                      r=["y0_%d" % ys], w=["ga%d" % ys])
                P.add("pool", lambda e, j=j, ys=ys: e.tensor_tensor(out=GT[:, j, :], in0=ga[ys][:, :], in1=yb[ys][:, :], op=ALU.mult),
                      r=["ga%d" % ys, "y1_%d" % ys], w=["GT%d" % j])
            for ti in range(4):
                slot = 1 + 4 * g + ti
                nk = None
                if g + 1 < NGF:
                    nk = prenorm_a(1 + 4 * (g + 1) + ti)
                pf, pfr = [(pF, "pF"), (pU[0], "pU0."), (pU[1], "pU1.")][ti % 3]
                for half in range(2):
                    for j in range(NJ):
                        P.add("pe", lambda e, j=j, half=half, ti=ti, pf=pf: e.matmul(
                            pf[:, half * 512:(half + 1) * 512], lhsT=GT[:, j, ti * 128:(ti + 1) * 128],
                            rhs=wdn[:, j, half * 512:(half + 1) * 512], start=(j == 0), stop=(j == NJ - 1)),
                            r=["GT%d" % j, "wdn%d" % j], w=[pfr + "%d" % half])
                if nk is not None:
                    prenorm_b(nk, h2T[1 - hp][:, :, 2 + ti * 128:2 + (ti + 1) * 128], "h2T%d_%d" % (1 - hp, ti))
                es = ecnt[0] % 2
                sg_ = st2[es]
                ecnt[0] += 1
                P.add("act", lambda e, sg_=sg_, pf=pf: e.activation(out=junk[:, :], in_=pf[:, :], func=AF.Square, accum_out=sg_[:, 0:1]),
                      r=[pfr + "0", pfr + "1"], w=["junk", "sg0_%d" % es])
                P.add("act", lambda e, sg_=sg_: e.activation(out=sg_[:, 1:2], in_=sg_[:, 0:1], func=AF.Ln, scale=1.0 / D, bias=EPS),
                      r=["sg0_%d" % es], w=["sg1_%d" % es])
                P.add("act", lambda e, sg_=sg_: e.activation(out=sg_[:, 2:3], in_=sg_[:, 1:2], func=AF.Exp, scale=-0.5),
                      r=["sg1_%d" % es], w=["sg2_%d" % es])
                P.add("dve", lambda e, sg_=sg_, es=es, pf=pf: e.scalar_tensor_tensor(out=t1[es][:, :], in0=pf[:, :], scalar=sg_[:, 2:3], in1=gffn[:, :],
                                                                                     op0=ALU.mult, op1=ALU.mult),
                      r=[pfr + "0", pfr + "1", "sg2_%d" % es, "gffn"], w=["t1_%d" % es])
                P.add("pool", lambda e, slot=slot, es=es: e.tensor_tensor(out=x1all[:, slot, :], in0=x1all[:, slot, :], in1=t1[es][:, :], op=ALU.add),
                      r=["x1_%d" % slot, "t1_%d" % es], w=["x1_%d" % slot])
                P.add("sp", lambda e, slot=slot: e.dma_start(out=out[(slot - 1) * 128:slot * 128, :], in_=x1all[:, slot, :]),
                      r=["x1_%d" % slot], w=["out%d" % slot], dma="out", force=True)
        fin = P.add("sp", lambda e: e.nop(), r=["out%d" % s for s in range(1, NOT + 1)], w=[], force=True)
        build_block(nc, P)


def build_program(NOT=16, phase="all"):
    NT = 4 * NOT
    nc = bass.Bass("TRN2", target_bir_lowering=False)
    dr = {}

    def din(name, shape):
        dr[name] = nc.dram_tensor(name, shape, F32, kind="ExternalInput").ap()

    din("xp", [NT * 128, D])
    din("w_in_r", [128, 8, DIN])
    din("w_out_r", [128, 8, D])
    din("w_up_r", [NJ, 128, 8, 256])
    din("w_down_r", [128, NJ, D])
    for nm in ("gpre", "gpost", "g2", "gffn"):
        din(nm, [1, D])
    for nm in ("lng", "lnb4", "sgu_b4", "gng4"):
        din(nm, [1, 512])
    din("sgu_w", [4, 128, 128])
    din("wgk17", [17, 256])
    din("cw", [128, 2 * NJ, 3])
    din("cb", [128, 2 * NJ])
    din("gpre8", [128, 8])
    if phase == "M":
        out = nc.dram_tensor("out", [(NOT + 1) * 128, D], F32, kind="ExternalOutput").ap()
    else:
        out = nc.dram_tensor("out", [NOT * 128, D], F32, kind="ExternalOutput").ap()
    dr["scr_up"] = nc.dram_tensor("wup_bf16_scr", [NJ, 128, 8, 256], BF16, kind="Internal").ap()
    dr["scr_dn"] = nc.dram_tensor("wdn_bf16_scr", [NJ // 2, 128, 2, D], BF16, kind="Internal").ap()
    with nc.sbuf_tensor("x1all", [128, max(NOT + 1, 15), D], F32) as x1all:
        _phase_M(nc, dr, x1all, NOT, dbg=out if phase == "M" else None)
        if phase != "M":
            _phase_F(nc, dr, x1all, NOT, out)
    return nc


def make_in_maps(inp, n_seg=4):
    x = np.asarray(inp["x"], dtype=np.float32)
    B, S, _ = x.shape
    seg = S // n_seg
    f = lambda a: np.ascontiguousarray(np.asarray(a, dtype=np.float32))
    w_in = f(inp["w_in"])[0]
    w_out = f(inp["w_out"])[0]
    w_up = f(inp["w_up"])[0]
    w_down = f(inp["w_down"])[0]
    shared = {
        "w_in_r": f(w_in.reshape(8, 128, DIN).transpose(1, 0, 2)),
        "w_out_r": f(w_out.reshape(8, 128, D).transpose(1, 0, 2)),
        "w_up_r": f(w_up.reshape(8, 128, 2, NJ, 128).transpose(3, 1, 0, 2, 4).reshape(NJ, 128, 8, 256)),
        "w_down_r": f(w_down.reshape(NJ, 128, D).transpose(1, 0, 2)),
        "gpre": f(inp["norm_mix_pre"]).reshape(1, D),
        "gpre8": f(f(inp["norm_mix_pre"]).reshape(8, 128).T),
        "gpost": f(inp["norm_mix_post"]).reshape(1, D),
        "g2": f(inp["norm_ffn_pre"]).reshape(1, D),
        "gffn": f(inp["norm_ffn_post"]).reshape(1, D),
        "lng": f(inp["sgu_ln_g"]).reshape(1, 512),
        "lnb4": f(inp["sgu_ln_b"]).reshape(1, 512),
        "sgu_b4": f(inp["sgu_b"]).reshape(1, 512),
        "gng4": f(np.tile(f(inp["gla_norm_g"]).reshape(128), 4)).reshape(1, 512),
        "sgu_w": f(inp["sgu_w_s"])[0],
        "wgk17": f(np.concatenate([f(inp["gla_w_gk"])[0], f(inp["gla_b_gk"]).reshape(1, 256)], axis=0)),
        "cw": f(f(inp["conv_w"])[0].reshape(3, 2 * NJ, 128).transpose(2, 1, 0)),
        "cb": f(f(inp["conv_b"])[0].reshape(2 * NJ, 128).transpose(1, 0)),
    }
    maps = []
    for b in range(B):
        for s in range(n_seg):
            xp = np.zeros((n_seg * seg, D), np.float32)
            xp[(n_seg - 1 - s) * seg:] = x[b, :(s + 1) * seg]
            m = dict(shared)
            m["xp"] = xp
            maps.append(m)
    return maps, B, S, seg


def kernel(**inputs):
    maps, B, S, seg = make_in_maps(inputs)
    nc = build_program(NOT=seg // 128)
    res = run_bass_kernel_spmd(nc, maps, core_ids=list(range(len(maps))))
    out = np.zeros((B, S, D), np.float32)
    i = 0
    for b in range(B):
        for s in range(S // seg):
            out[b, s * seg:(s + 1) * seg] = res.results[i]["out"]
            i += 1
    return out
```

```python
import contextlib
import numpy as np
import concourse.bass as bass
import concourse.mybir as mybir
from concourse.bass_utils import run_bass_kernel_spmd

F32 = mybir.dt.float32
BF16 = mybir.dt.bfloat16
AF = mybir.ActivationFunctionType
ALU = mybir.AluOpType

D = 1024
DIN = 2576
DFF = 2816
NJ = DFF // 128
OFF_U, OFF_V, OFF_Q, OFF_K, OFF_BV, OFF_G, OFF_LR = 0, 512, 1024, 1280, 1536, 2048, 2560
EPS = 1e-6

ENGS = ["pe", "act", "dve", "pool", "sp"]
EPOCH = 16000
SAME_ENG_DIST = 2
BATCH_PREFIX = True
CONV_Q = "pool"
import os
LIMIT = [int(os.environ.get("MK_LIMIT", "100000000"))]


def _bank_of(res):
    if len(res) < 2 or res[0] != "p" or not res[1].isupper():
        return None
    for pre in ("pT", "pG", "pH"):
        if res.startswith(pre):
            return pre
    return res


class _Op:
    __slots__ = ("idx", "eng", "fn", "deps", "flag", "is_dma", "sem", "semval",
                 "eidx", "cnt", "waits")


class Prog:
    def __init__(self):
        self.ops = []
        self.last_w = {}
        self.readers = {}
        self.eng_n = {e: 0 for e in ENGS}
        self.dma_sems = {}
        self.bank_last = {}

    def add(self, eng, fn, r=(), w=(), dma=None, force=False):
        if len(self.ops) >= LIMIT[0] and not force:
            return None
        op = _Op()
        op.idx = len(self.ops)
        op.eng = eng
        op.fn = fn
        op.flag = False
        op.is_dma = dma is not None
        deps = set()
        for res in r:
            if res in self.last_w:
                deps.add(self.last_w[res])
        for res in w:
            if res in self.last_w:
                deps.add(self.last_w[res])
            for q in self.readers.get(res, ()):
                deps.add(q)
        for res in r:
            self.readers.setdefault(res, []).append(op.idx)
        for res in w:
            self.last_w[res] = op.idx
            self.readers[res] = []
        for res in list(r) + list(w):
            bk = _bank_of(res)
            if bk is None:
                continue
            lb = self.bank_last.setdefault(bk, {})
            for e2, oi in lb.items():
                if e2 != eng:
                    deps.add(oi)
            lb[eng] = op.idx
        deps.discard(op.idx)
        op.deps = sorted(deps)
        op.eidx = self.eng_n[eng]
        self.eng_n[eng] += 1
        op.sem = None
        op.semval = 0
        op.cnt = 0
        if op.is_dma:
            op.sem = dma
            self.dma_sems[dma] = self.dma_sems.get(dma, 0) + 16
            op.semval = self.dma_sems[dma]
        self.ops.append(op)
        return op

    def plan(self):
        ops = self.ops
        need = []
        for b in ops:
            nl = []
            for ai in b.deps:
                a = ops[ai]
                if a.is_dma:
                    nl.append(ai)
                elif a.eng == b.eng and not b.is_dma:
                    if a.eng == "pe":
                        continue
                    if b.eidx - a.eidx <= SAME_ENG_DIST:
                        a.flag = True
                        nl.append(ai)
                else:
                    a.flag = True
                    nl.append(ai)
            need.append(nl)
        cnt = {e: 0 for e in ENGS}
        for o in ops:
            if o.flag and not o.is_dma:
                cnt[o.eng] += 1
                o.cnt = cnt[o.eng]
        sem_names = []
        for e in ENGS:
            for k in range(max(1, (cnt[e] + EPOCH - 1) // EPOCH)):
                sem_names.append(("E", e, k))
        for d in self.dma_sems:
            sem_names.append(("D", d))
        waited = {e: {} for e in ENGS}
        for b, nl in zip(ops, need):
            ws = {}
            for ai in nl:
                a = ops[ai]
                if a.is_dma:
                    key = ("D", a.sem)
                    val = a.semval
                else:
                    c = a.cnt
                    key = ("E", a.eng, (c - 1) // EPOCH)
                    val = (c - 1) % EPOCH + 1
                if ws.get(key, 0) < val:
                    ws[key] = val
            out = []
            wd = waited[b.eng]
            for key, val in ws.items():
                if wd.get(key, 0) >= val:
                    continue
                wd[key] = val
                out.append((key, val))
            b.waits = out
        return sem_names

    def run_engine(self, eng, handle, sems):
        for o in self.ops:
            if o.eng != eng:
                continue
            for key, val in o.waits:
                handle.wait_ge(sems[key], val)
            ins = o.fn(handle)
            if o.is_dma:
                ins.then_inc(sems[("D", o.sem)], 16)
            elif o.flag:
                c = o.cnt
                ins.then_inc(sems[("E", o.eng, (c - 1) // EPOCH)], 1)


_BLK = [0]


def build_block(nc, prog):
    sem_names = prog.plan()
    _BLK[0] += 1
    blk = _BLK[0]
    with contextlib.ExitStack() as st:
        sems = {}
        for i, key in enumerate(sem_names):
            sems[key] = st.enter_context(nc.semaphore("sem%d_%d" % (blk, i)))
        block = st.enter_context(nc.Block())

        @block.tensor
        def _(e):
            prog.run_engine("pe", e, sems)

        @block.scalar
        def _(e):
            prog.run_engine("act", e, sems)

        @block.vector
        def _(e):
            prog.run_engine("dve", e, sems)

        @block.gpsimd
        def _(e):
            prog.run_engine("pool", e, sems)

        @block.sync
        def _(e):
            prog.run_engine("sp", e, sems)


def _prenorm_a(P, x_ap, x_res, gb, st, hb, sfx="", hb_res=None):
    hr = hb_res or ("hb" + sfx)
    P.add("act", lambda e: e.activation(out=hb[:, :], in_=x_ap, func=AF.Square, accum_out=st[:, 0:1]),
          r=[x_res], w=[hr, "st0" + sfx])
    P.add("act", lambda e: e.activation(out=st[:, 1:2], in_=st[:, 0:1], func=AF.Ln, scale=1.0 / D, bias=EPS),
          r=["st0" + sfx], w=["st1" + sfx])
    P.add("act", lambda e: e.activation(out=st[:, 2:3], in_=st[:, 1:2], func=AF.Exp, scale=-0.5),
          r=["st1" + sfx], w=["st2" + sfx])
    P.add("dve", lambda e: e.scalar_tensor_tensor(out=hb[:, :], in0=x_ap, scalar=st[:, 2:3], in1=gb[:, :],
                                                  op0=ALU.mult, op1=ALU.mult),
          r=[x_res, "st2" + sfx, "gb"], w=[hr])


def _prenorm_a_nog(P, x_ap, x_res, st, hb, sfx="", hb_res=None):
    hr = hb_res or ("hb" + sfx)
    P.add("act", lambda e: e.activation(out=hb[:, :], in_=x_ap, func=AF.Square, accum_out=st[:, 0:1]),
          r=[x_res], w=[hr, "st0" + sfx])
    P.add("act", lambda e: e.activation(out=st[:, 1:2], in_=st[:, 0:1], func=AF.Ln, scale=1.0 / D, bias=EPS),
          r=["st0" + sfx], w=["st1" + sfx])
    P.add("act", lambda e: e.activation(out=st[:, 2:3], in_=st[:, 1:2], func=AF.Exp, scale=-0.5),
          r=["st1" + sfx], w=["st2" + sfx])
    P.add("dve", lambda e: e.tensor_scalar(out=hb[:, :], in0=x_ap, scalar1=st[:, 2:3], scalar2=None, op0=ALU.mult),
          r=[x_res, "st2" + sfx], w=[hr])


def _prenorm_b(P, hb, pT, ident, dst3, dst_res, sfx="", evac_eng="act", hb_res=None, pres="pT"):
    hr = hb_res or ("hb" + sfx)
    for kc in range(8):
        P.add("pe", lambda e, kc=kc: e.transpose(pT[:, kc * 128:(kc + 1) * 128], hb[:, kc * 128:(kc + 1) * 128], ident[:, :]),
              r=[hr, "ident"], w=[pres + "%d" % kc])
    if evac_eng == "act":
        P.add("act", lambda e: e.activation(out=dst3, in_=pT[:, :].rearrange("p (k c) -> p k c", k=8), func=AF.Copy),
              r=[pres + "%d" % k for k in range(8)], w=[dst_res])
    else:
        P.add("dve", lambda e: e.tensor_copy(out=dst3, in_=pT[:, :].rearrange("p (k c) -> p k c", k=8)),
              r=[pres + "%d" % k for k in range(8)], w=[dst_res])


def _prenorm(P, tag, x_ap, x_res, gb, st, hb, junk, pT, ident, dst3, dst_res):
    _prenorm_a(P, x_ap, x_res, gb, st, hb)
    _prenorm_b(P, hb, pT, ident, dst3, dst_res)


def _consts(P, ident, dmaq="sp"):
    P.add("pool", lambda e: e.memset(ident[:, :], 0.0), w=["ident"])
    P.add("pool", lambda e: e.affine_select(out=ident[:, :], in_=ident[:, :], compare_op=ALU.not_equal, fill=1.0,
                                            base=0, pattern=[[-1, 128]], channel_multiplier=1),
          r=["ident"], w=["ident"])


def _phase_M(nc, dr, x1all, NOT, dbg=None):
    NPT = 3 * NOT
    NT = NPT + NOT
    NG = NT // 4
    P = Prog()
    with contextlib.ExitStack() as st_:
        def sb(name, shape, dt):
            return st_.enter_context(nc.sbuf_tensor("sb_" + name, shape, dt))

        def ps(name, shape, dt):
            return st_.enter_context(nc.psum_tensor("ps_" + name, shape, dt))

        win = sb("win", [128, 8, DIN], BF16)
        wout = sb("wout", [128, 8, D], BF16)
        NXS = 3
        xs = [sb("xs%d" % i, [128, D], F32) for i in range(NXS)]
        hb = sb("hb", [128, D], BF16)
        junk = sb("junk", [128, D], BF16)
        hT = sb("hT", [128, 8, 512], BF16)
        uT = sb("uT", [128, 4, 512], BF16)
        qT = sb("qT", [128, 2, 512], F32)
        kT = sb("kT", [128, 2, 512], F32)
        lrT = sb("lrT", [32, 512], F32)
        vav = sb("vav", [128, 1024], F32)
        va = vav[:, 0:512]
        vtmp = vav[:, 512:1024]
        vng = sb("vng", [128, 512], BF16)
        vb = sb("vb", [128, 512], BF16)
        sgs = sb("sgs", [128, 1024], F32)
        sg = sgs[:, 0:512]
        sgg = sgs[:, 512:1024]
        el = sb("el", [128, 256], F32)
        lg = sb("lg", [128, 256], F32)
        Ep = sb("Ep", [128, 2, 128], F32)
        Em = sb("Em", [128, 2, 128], F32)
        qd = sb("qd", [128, 4, 128], BF16)
        kd = sb("kd", [128, 2, 128], BF16)
        kdec = sb("kdec", [128, 2, 128], BF16)
        kdecT = sb("kdecT", [128, 256], BF16)
        attnT = sb("attnT", [128, 4, 128], BF16)
        ob = sb("ob", [128, 512], BF16)
        mixT = sb("mixT", [128, 8, 128], BF16)
        t1 = sb("t1", [128, D], F32)
        wraw = t1[:, 0:512].rearrange("p (h c) -> p h c", h=4)
        wrb = junk[:, 0:512].rearrange("p (h c) -> p h c", h=4)
        S = sb("S", [128, 2, 128], F32)
        Sb = sb("Sb", [128, 2, 128], BF16)
        st = sb("st", [128, 16], F32)
        stq = sb("stq", [128, 8], F32)
        sth = [sb("sth%d" % i, [128, 4], F32) for i in range(2)]
        bst = sb("bst", [128, 6], F32)
        mv = sb("mv", [128, 2], F32)
        ident = sb("ident", [128, 128], BF16)
        maskU = sb("maskU", [128, 128], F32)
        Lcum = sb("Lcum", [128, 128], F32)
        Ep2 = sb("Ep2", [128, 2, 128], F32)
        Em2 = sb("Em2", [128, 2, 128], F32)
        WmT = sb("WmT", [128, 4, 128], BF16)
        onesb = sb("onesb", [128, 2], BF16)
        lhsT2 = sb("lhsT2", [2, 512], F32)
        rhs2 = sb("rhs2", [2, 512], F32)
        lnG = sb("lnG", [128, 512], F32)
        gcol = sb("gcol", [128, 1], F32)
        gpost = sb("gpost", [128, D], F32)
        gp8 = sb("gp8", [128, 8], F32)
        wgk = sb("wgk", [32, 256], F32)

        pP = ps("pP", [128, 1024], F32)
        pG = ps("pG", [128, 512], F32)
        pAO = ps("pAO", [128, 1024], F32)
        pA = pAO[:, 0:512]
        pO = pAO[:, 512:1024]
        pS = ps("pS", [128, 512], F32)
        pZ = ps("pZ", [128, 512], F32)
        pT = ps("pT", [128, 1024], BF16)

        x1flat = x1all[:, :, :].rearrange("p s d -> p (s d)")

        def area_res(lo, hi):
            return ["x1_%d" % k for k in range(lo // D, (hi - 1) // D + 1)]

        def cast(eng, out_ap, in_ap, r, w):
            if eng == "act":
                P.add("act", lambda e: e.activation(out=out_ap, in_=in_ap, func=AF.Copy), r=r, w=w)
            else:
                P.add(eng, lambda e: e.tensor_copy(out=out_ap, in_=in_ap), r=r, w=w)

        HSPLIT = OFF_K
        AW = DIN - HSPLIT

        def win_res(kc, c0):
            part = "hi" if c0 >= HSPLIT else "lo"
            return ["win%d.%s.%s" % (kc, part, e) for e in ("act", "dve")]

        P.add("sp", lambda e: e.dma_start(out=gp8[:, :], in_=dr["gpre8"][:, :]), w=["gp8"], dma="c3")

        def cast_scaled(eng, out_ap, in_ap, sc, r, w):
            if eng == "act":
                P.add("act", lambda e: e.activation(out=out_ap, in_=in_ap, func=AF.Copy, scale=sc), r=r, w=w)
            else:
                P.add(eng, lambda e: e.tensor_scalar(out=out_ap, in0=in_ap, scalar1=sc, scalar2=None, op0=ALU.mult), r=r, w=w)

        def load_win_part(part, cb0, cb1, do_dma=True, do_cast=True):
            w_ = cb1 - cb0
            th = w_ // 2
            spl = [(0, th, "act"), (th, w_, "dve")]
            for kc in range(8 if do_dma else 0):
                lo = kc * AW
                ar = area_res(lo, lo + w_)
                P.add("sp", lambda e, kc=kc, lo=lo, w_=w_, cb0=cb0, cb1=cb1: e.dma_start(out=x1flat[:, lo:lo + w_], in_=dr["w_in_r"][:, kc, cb0:cb1]),
                      w=ar, dma="stgA%d" % kc)
            for kc in range(8 if do_cast else 0):
                lo = kc * AW
                ar = area_res(lo, lo + w_)
                for (c0, c1, eng) in spl:
                    cast_scaled(eng, win[:, kc, cb0 + c0:cb0 + c1], x1flat[:, lo + c0:lo + c1], gp8[:, kc:kc + 1], ar + ["gp8"],
                                ["win%d.%s.%s" % (kc, part, eng)])

        load_win_part("hi", HSPLIT, DIN)
        _consts(P, ident)
        P.add("pool", lambda e: e.memset(maskU[:, :], 1.0), w=["maskU"])
        P.add("pool", lambda e: e.affine_select(out=maskU[:, :], in_=maskU[:, :], compare_op=ALU.is_ge, fill=0.0,
                                                base=0, pattern=[[1, 128]], channel_multiplier=-1),
              r=["maskU"], w=["maskU"])
        P.add("pool", lambda e: e.memset(Lcum[:, :], -1.0 / 16.0), w=["Lcum"])
        P.add("pool", lambda e: e.affine_select(out=Lcum[:, :], in_=Lcum[:, :], compare_op=ALU.is_ge, fill=0.0,
                                                base=0, pattern=[[1, 128]], channel_multiplier=-1),
              r=["Lcum"], w=["Lcum"])
        P.add("pool", lambda e: e.memset(onesb[:, :], 1.0), w=["onesb"])
        P.add("pool", lambda e: e.memset(lrT[:, :], 1.0), w=["lrT"])
        P.add("pool", lambda e: e.memset(S[:, :, :], 0.0), w=["S0", "S1"])
        P.add("pool", lambda e: e.memset(Sb[:, :, :], 0.0), w=["Sb"])
        P.add("pool", lambda e: e.memset(qd[:, :, :], 0.0), w=["qd"])
        P.add("pool", lambda e: e.memset(lhsT2[:, :], 1.0), w=["lhsT2"])
        P.add("sp", lambda e: e.dma_start(out=lhsT2[0:1, :], in_=dr["lnb4"][0:1, :]), w=["lhsT2"], dma="c0")
        P.add("sp", lambda e: e.dma_start(out=rhs2[1:2, :], in_=dr["sgu_b4"][0:1, :]), w=["rhs2r1"], dma="c1")
        P.add("sp", lambda e: e.dma_start(out=wraw[:, :, :], in_=dr["sgu_w"].rearrange("h t s -> t h s")),
              w=["t1"], dma="c2")
        P.add("sp", lambda e: e.dma_start(out=wgk[0:17, :], in_=dr["wgk17"][:, :]), w=["wgk"], dma="c4")
        P.add("sp", lambda e: e.dma_start(out=lnG[:, :], in_=dr["lng"][0:1, :].partition_broadcast(128)), w=["lnG"], dma="c5")
        P.add("sp", lambda e: e.dma_start(out=gcol[:, :], in_=dr["gng1"][:, :]), w=["gcol"], dma="c6")
        P.add("sp", lambda e: e.dma_start(out=gpost[:, :], in_=dr["gpost"][0:1, :].partition_broadcast(128)), w=["gpost"], dma="c7")
        for h in range(4):
            P.add("pool", lambda e, h=h: e.affine_select(out=wraw[:, h, :], in_=wraw[:, h, :], compare_op=ALU.is_ge, fill=0.0,
                                                         base=0, pattern=[[-1, 128]], channel_multiplier=1),
                  r=["t1"], w=["t1"])
        P.add("pool", lambda e: e.tensor_copy(out=wrb[:, :, :], in_=wraw[:, :, :]), r=["t1"], w=["junk"])
        for h in range(4):
            P.add("pe", lambda e, h=h: e.transpose(pT[:, h * 128:(h + 1) * 128], wrb[:, h, :], ident[:, :]),
                  r=["junk", "ident"], w=["pT%d" % h])
        P.add("act", lambda e: e.activation(out=WmT[:, :, :], in_=pT[:, 0:512].rearrange("p (h c) -> p h c", h=4), func=AF.Copy),
              r=["pT0", "pT1", "pT2", "pT3"], w=["WmT"])
        P.add("pe", lambda e: e.matmul(pG[0:1, 0:512], lhsT=onesb[:, 0:1], rhs=WmT[:, :, :].rearrange("p h c -> p (h c)"),
                                       start=True, stop=True),
              r=["onesb", "WmT"], w=["pG.g", "pG.c"])
        P.add("act", lambda e: e.activation(out=rhs2[0:1, :], in_=pG[0:1, 0:512], func=AF.Copy),
              r=["pG.g", "pG.c"], w=["rhs2r0"])
        wo_loaded = [0]
        wo_cast = [0]

        def _wout_load(kc):
            a = kc % 4
            lo = 8 * (DIN - OFF_K) + a * D
            ar = area_res(lo, lo + D)
            P.add("sp", lambda e: e.dma_start(out=x1flat[:, lo:lo + D], in_=dr["w_out_r"][:, kc, :]), w=ar, dma="stgB%d" % a)

        def load_wout(kc):
            while wo_loaded[0] <= kc:
                _wout_load(wo_loaded[0])
                wo_loaded[0] += 1
            a = kc % 4
            lo = 8 * (DIN - OFF_K) + a * D
            ar = area_res(lo, lo + D)
            if kc >= 4:
                cast_scaled("pool" if kc % 2 == 0 else "act", wout[:, kc, :], x1flat[:, lo:lo + D], gcol[:, 0:1], ar + ["gcol"], ["wout%d" % kc])
            else:
                cast("pool" if kc % 2 == 0 else "act", wout[:, kc, :], x1flat[:, lo:lo + D], ar, ["wout%d" % kc])
            wo_cast[0] = kc + 1
            while wo_loaded[0] < min(8, wo_cast[0] + 3):
                _wout_load(wo_loaded[0])
                wo_loaded[0] += 1

        for kc0 in range(3):
            _wout_load(kc0)
            wo_loaded[0] += 1

        pbank = [0]

        def next_bank():
            b = pbank[0]
            pbank[0] ^= 1
            return b

        win_all = ["win%d" % k for k in range(8)]

        def proj_fm(c0, M, evac):
            b = next_bank()
            for kc in range(8):
                P.add("pe", lambda e, kc=kc, b=b: e.matmul(pP[0:M, b * 512:(b + 1) * 512], lhsT=win[:, kc, c0:c0 + M],
                                                           rhs=hT[:, kc, :], start=(kc == 0), stop=(kc == 7)),
                      r=win_res(kc, c0) + ["hT0", "hT1", "hT2", "hT3"], w=["pP%d" % b])
            evac(b)

        def proj_tm(ti, c0, evac):
            b = next_bank()
            for kc in range(8):
                P.add("pe", lambda e, kc=kc, b=b: e.matmul(pP[:, b * 512:(b + 1) * 512], lhsT=hT[:, kc, ti * 128:(ti + 1) * 128],
                                                           rhs=win[:, kc, c0:c0 + 512], start=(kc == 0), stop=(kc == 7)),
                      r=win_res(kc, c0) + ["hT%d" % ti], w=["pP%d" % b])
            evac(b)

        xcnt = [0]
        pG_bf = pG[:, :].bitcast(BF16)
        hbp = [hb, junk]
        hbr = ["hb", "junk"]
        stp = [st, stq]
        pTb = [pT, pG_bf]
        pTres = ["pT", "pG.T"]
        vbs = [vb[:, :], attnT[:, :, :].rearrange("p h c -> p (h c)")]
        vbr = ["vb", "attnT"]
        t1f = t1
        kdall = mixT
        kdT_all = uT[:, 0:2, :].rearrange("p a c -> p (a c)")

        vb_all = qT[:, :, :].rearrange("p a c -> p (a c)").bitcast(BF16)

        vb_alt = [(vb[:, :], "vb"), (vng[:, :], "vng"), (attnT[:, :, :].rearrange("p h c -> p (h c)"), "attnT"), (ob[:, :], "ob")]

        def vbuf(g, ti):
            if g % 2 == 0:
                return vb_all[:, ti * 512:(ti + 1) * 512], "qT"
            return vb_alt[ti]

        def prefix_B(g):
            for ti in range(4):
                tau = 4 * g + ti
                sl = xcnt[0] % NXS
                xcnt[0] += 1
                par = ti % 2
                P.add("sp", lambda e, tau=tau, sl=sl: e.dma_start(out=xs[sl][:, :], in_=dr["xp"][tau * 128:(tau + 1) * 128, :]),
                      w=["xs%d" % sl], dma="xs%d" % sl)
                _prenorm_a_nog(P, xs[sl][:, :], "xs%d" % sl, stp[par], hbp[par], sfx="M%d" % par, hb_res=hbr[par])
                yield
                _prenorm_b(P, hbp[par], pTb[par], ident, hT[:, :, ti * 128:(ti + 1) * 128], "hT%d" % ti,
                           evac_eng=("act" if par == 0 else "dve"), hb_res=hbr[par], pres=pTres[par])
                yield
            yield ("wait", "k_read")
            for c in range(2):
                proj_fm(OFF_K + c * 128, 128,
                        lambda b, c=c: P.add("act", lambda e: e.activation(out=kT[:, c, :], in_=pP[:, b * 512:(b + 1) * 512], func=AF.Copy),
                                             r=["pP%d" % b], w=["kT"]))
                yield
            yield ("wait", "lr_read")
            proj_fm(OFF_LR, 16,
                    lambda b: P.add("dve", lambda e: e.tensor_copy(out=lrT[0:16, :], in_=pP[0:16, b * 512:(b + 1) * 512]),
                                    r=["pP%d" % b], w=["lrT"]))
            yield
            for ti in range(4):
                vv, vr = vbuf(g, ti)
                if ti % 2 == 0:
                    proj_tm(ti, OFF_BV,
                            lambda b, vv=vv, vr=vr: P.add("act", lambda e: e.activation(out=vv, in_=pP[:, b * 512:(b + 1) * 512], func=AF.Copy),
                                                          r=["pP%d" % b], w=[vr]))
                else:
                    proj_tm(ti, OFF_BV,
                            lambda b, vv=vv, vr=vr: P.add("dve", lambda e: e.tensor_copy(out=vv, in_=pP[:, b * 512:(b + 1) * 512]),
                                                          r=["pP%d" % b], w=[vr]))
                yield

        def prefix_A(g):
            bk = ["pA", "pA", "pO", "pO"]
            for ti in range(4):
                P.add("pe", lambda e, ti=ti: e.matmul(pAO[:, ti * 256:(ti + 1) * 256], lhsT=lrT[0:17, ti * 128:(ti + 1) * 128],
                                                      rhs=wgk[0:17, :], start=True, stop=True),
                      r=["lrT", "wgk"], w=[bk[ti]])
            yield ("signal", "lr_read")
            P.add("act", lambda e: e.activation(out=t1f[:, :], in_=pAO[:, :], func=AF.Exp, scale=-1.0), r=["pA", "pO"], w=["t1"])
            yield
            P.add("act", lambda e: e.activation(out=t1f[:, :], in_=t1f[:, :], func=AF.Ln, bias=1.0), r=["t1"], w=["t1"])
            yield
            for ti in range(4):
                for c in range(2):
                    o0 = ti * 256 + c * 128
                    P.add("pe", lambda e, o0=o0: e.matmul(pAO[:, o0:o0 + 128], lhsT=t1f[:, o0:o0 + 128], rhs=Lcum[:, :],
                                                          start=True, stop=True),
                          r=["t1", "Lcum"], w=[bk[ti]])
            yield
            P.add("act", lambda e: e.activation(out=vav[:, :], in_=pAO[:, :], func=AF.Exp), r=["pA", "pO"], w=["va", "vtmp"])
            yield
            P.add("act", lambda e: e.activation(out=sgs[:, :], in_=pAO[:, :], func=AF.Exp, scale=-1.0), r=["pA", "pO"], w=["sg", "sgg"])
            yield
            for ti in range(4):
                for c in range(2):
                    o0 = ti * 256 + c * 128
                    P.add("dve", lambda e, o0=o0, ti=ti, c=c: e.scalar_tensor_tensor(
                        out=kdall[:, ti * 2 + c, :], in0=kT[:, c, ti * 128:(ti + 1) * 128], scalar=vav[:, o0 + 127:o0 + 128],
                        in1=sgs[:, o0:o0 + 128], op0=ALU.mult, op1=ALU.mult),
                        r=["kT", "va", "vtmp", "sg", "sgg"], w=["mixTa", "mixTb"])
                yield
            yield ("signal", "k_read")
            for q in range(8):
                P.add("pe", lambda e, q=q: e.transpose(pT[:, q * 128:(q + 1) * 128], kdall[:, q, :], ident[:, :]),
                      r=["mixTa", "mixTb", "ident"], w=["pT%d" % q])
            P.add("act", lambda e: e.activation(out=kdT_all, in_=pT[:, :], func=AF.Copy),
                  r=["pT%d" % q for q in range(8)], w=["uT"])
            yield
            for ti in range(4):
                vv, vr = vbuf(g, ti)
                psb, psr = (pS, "pS") if ti % 2 == 0 else (pZ, "pZ")
                for h in range(4):
                    c = h // 2
                    o0 = ti * 256 + c * 128
                    P.add("pe", lambda e, h=h, o0=o0, vv=vv, psb=psb: e.matmul(psb[:, h * 128:(h + 1) * 128], lhsT=kdT_all[:, o0:o0 + 128],
                                                                                rhs=vv[:, h * 128:(h + 1) * 128], start=True, stop=True),
                          r=["uT", vr], w=[psr])
                yield
                for h in range(4):
                    c, r0 = h // 2, (h % 2) * 64
                    o0 = ti * 256 + c * 128
                    P.add("dve", lambda e, h=h, c=c, r0=r0, o0=o0, psb=psb: e.scalar_tensor_tensor(
                        out=S[r0:r0 + 64, c, :], in0=S[r0:r0 + 64, c, :], scalar=vav[r0:r0 + 64, o0 + 127:o0 + 128],
                        in1=psb[r0:r0 + 64, h * 128:(h + 1) * 128], op0=ALU.mult, op1=ALU.add),
                        r=["S%d" % c, "va", "vtmp", psr], w=["S%d" % c])
                yield
            P.add("pool", lambda e: e.tensor_copy(out=Sb[:, :, :], in_=S[:, :, :]), r=["S0", "S1"], w=["Sb"])
            yield

        CSLOT = 3072
        ccnt = [0]

        conv_pending = []

        def conv_bufs(k):
            lo = k * CSLOT
            ar = area_res(lo, lo + CSLOT)
            stage = x1flat[:, lo:lo + 2048]
            bfv = x1flat[:, lo + 2048:lo + 3072].bitcast(BF16)
            return ar, stage, bfv

        def conv_load(u):
            k = ccnt[0] % 3
            ccnt[0] += 1
            ar, stage, bfv = conv_bufs(k)
            if u < NJ:
                src = dr["w_up_r"][u, :, :, :].rearrange("p k c -> p (k c)")
            else:
                src = dr["w_down_r"][:, 2 * (u - NJ):2 * (u - NJ) + 2, :].rearrange("p j d -> p (j d)")
            P.add(CONV_Q, lambda e: e.dma_start(out=stage, in_=src), w=ar, dma="cvl%d" % k)
            conv_pending.append((u, k))

        def conv_finish():
            u, k = conv_pending.pop(0)
            ar, stage, bfv = conv_bufs(k)
            if u < NJ:
                dst = dr["scr_up"][u, :, :, :].rearrange("p k c -> p (k c)")
            else:
                dst = dr["scr_dn"][u - NJ, :, :, :].rearrange("p j d -> p (j d)")
            cast("act", bfv[:, 0:1024], stage[:, 0:1024], ar, ["cv%d.a" % k])
            cast("pool", bfv[:, 1024:2048], stage[:, 1024:2048], ar, ["cv%d.b" % k])
            yield
            P.add(CONV_Q, lambda e: e.dma_start(out=dst, in_=bfv), r=["cv%d.a" % k, "cv%d.b" % k] + ar, w=["scrw%d" % u], dma="cvs%d" % k)
            yield

        def conv_thread(units, last=False):
            for u in units:
                conv_load(u)
                yield
                if len(conv_pending) > 2:
                    for _ in conv_finish():
                        yield
            if last:
                while conv_pending:
                    for _ in conv_finish():
                        yield

        def run_sched(gens, pre=()):
            sig = set(pre)
            live = [[x, None] for x in gens if x is not None]
            while live:
                progressed = False
                for item in list(live):
                    x, wk = item
                    if wk is not None:
                        if wk not in sig:
                            continue
                        item[1] = None
                    progressed = True
                    try:
                        r = next(x)
                    except StopIteration:
                        live.remove(item)
                        continue
                    if isinstance(r, tuple):
                        if r[0] == "signal":
                            sig.add(r[1])
                        elif r[0] == "wait" and r[1] not in sig:
                            item[1] = r[1]
                if not progressed:
                    raise RuntimeError("op-thread deadlock: %s" % [i[1] for i in live])

        n_pg = (NPT // 4 - 1) if BATCH_PREFIX else 0
        n_pre = max(1, NPT // 4 - 1)
        per_g = (8 + n_pre - 1) // n_pre
        if n_pg > 0:
            for kc in range(0, min(8, per_g)):
                load_wout(kc)
            run_sched([prefix_B(0)], pre=("k_read", "lr_read"))
            load_win_part("lo", 0, HSPLIT, do_cast=False)
            n_units = NJ + NJ // 2
            n_cg = max(1, n_pg - 1)
            upg = (n_units + n_cg - 1) // n_cg
            for g in range(n_pg):
                for kc in range((g + 1) * per_g, min(8, (g + 2) * per_g)):
                    load_wout(kc)
                gg = g - 1 if n_pg > 1 else g
                units = list(range(gg * upg, min(n_units, (gg + 1) * upg))) if gg >= 0 else []
                run_sched([prefix_A(g), prefix_B(g + 1) if g + 1 < n_pg else None, conv_thread(units, last=(g == n_pg - 1))])
                if g == 0:
                    load_win_part("lo", 0, HSPLIT, do_dma=False)

        def group_head(g):
            if n_pg == 0 or g >= n_pg:
                for kc in range(max(g, n_pg + 1) * per_g if n_pg > 0 else g * per_g, min(8, (g + 1) * per_g)):
                    load_wout(kc)
            for ti in range(4):
                tau = 4 * g + ti
                sl = xcnt[0] % NXS
                xcnt[0] += 1
                par = ti % 2
                P.add("sp", lambda e, tau=tau, sl=sl: e.dma_start(out=xs[sl][:, :], in_=dr["xp"][tau * 128:(tau + 1) * 128, :]),
                      w=["xs%d" % sl], dma="xs%d" % sl)
                _prenorm_a_nog(P, xs[sl][:, :], "xs%d" % sl, sth[par], hb, sfx="H%d" % par, hb_res="hb")
                yield
                if ti == 3:
                    yield ("wait", "h_gla")
                    yield ("wait", "h_sgu")
                _prenorm_b(P, hb, pT, ident, hT[:, :, ti * 128:(ti + 1) * 128], "hT%d" % ti, hb_res="hb")
                yield
            yield ("wait", "kq_read")
            for c in range(2):
                proj_fm(OFF_K + c * 128, 128,
                        lambda b, c=c: P.add("act", lambda e: e.activation(out=kT[:, c, :], in_=pP[:, b * 512:(b + 1) * 512], func=AF.Copy),
                                             r=["pP%d" % b], w=["kT"]))
                yield
            yield ("wait", "lr_read")
            proj_fm(OFF_LR, 16,
                    lambda b: P.add("act", lambda e: e.activation(out=lrT[0:16, :], in_=pP[0:16, b * 512:(b + 1) * 512], func=AF.Copy),
                                    r=["pP%d" % b], w=["lrT"]))
            yield
            for c in range(2):
                proj_fm(OFF_Q + c * 128, 128,
                        lambda b, c=c: P.add("dve", lambda e: e.tensor_copy(out=qT[:, c, :], in_=pP[:, b * 512:(b + 1) * 512]),
                                             r=["pP%d" % b], w=["qT"]))
                yield
            yield ("wait", "u_read")
            for c in range(4):
                proj_fm(OFF_U + c * 128, 128,
                        lambda b, c=c: P.add("act", lambda e: e.activation(out=uT[:, c, :], in_=pP[:, b * 512:(b + 1) * 512], func=AF.Gelu),
                                             r=["pP%d" % b], w=["uT"]))
                yield

        ALLSIG = ("h_gla", "h_sgu", "kq_read", "lr_read", "u_read")
        pending_out = [None]
        gate_done = set()
        for g in range(NG):
            own = g >= NPT // 4
            full = [own or (g == NPT // 4 - 1 and ti == 3) for ti in range(4)]
            if g < n_pg:
                continue
            if g == n_pg:
                run_sched([group_head(g)], pre=ALLSIG)
            EpS = [(Ep, "Ep"), (Ep2, "Ep2")]
            EmS = [(Em, "Em"), (Em2, "Em2")]

            def tile_gate(ti, inline=False):
                gc0, gc1 = ti * 128, (ti + 1) * 128
                Ept, Epn = EpS[ti % 2]
                Emt, Emn = EmS[ti % 2]
                if not inline:
                    yield ("wait", "gate_free")
                P.add("pe", lambda e: e.matmul(pG[:, 0:256], lhsT=lrT[0:17, gc0:gc1], rhs=wgk[0:17, :], start=True, stop=True),
                      r=["lrT", "wgk"], w=["pG.g"])
                P.add("act", lambda e: e.activation(out=el[:, :], in_=pG[:, 0:256], func=AF.Exp, scale=-1.0), r=["pG.g"], w=["el"])
                yield ("signal", "lr_read")
                P.add("act", lambda e: e.activation(out=lg[:, :], in_=el[:, :], func=AF.Ln, bias=1.0), r=["el"], w=["lg"])
                yield
                for c in range(2):
                    P.add("pe", lambda e, c=c: e.matmul(pG[:, 256 + c * 128:256 + (c + 1) * 128], lhsT=lg[:, c * 128:(c + 1) * 128],
                                                        rhs=Lcum[:, :], start=True, stop=True),
                          r=["lg", "Lcum"], w=["pG.c"])
                P.add("act", lambda e: e.activation(out=Ept[:, :, :].rearrange("p c i -> p (c i)"), in_=pG[:, 256:512], func=AF.Exp),
                      r=["pG.c"], w=[Epn])
                P.add("act", lambda e: e.activation(out=Emt[:, :, :].rearrange("p c i -> p (c i)"), in_=pG[:, 256:512], func=AF.Exp, scale=-1.0),
                      r=["pG.c"], w=[Emn])
                gate_done.add((g, ti))
                yield

            def tile_gla(ti, fl):
                tc0, tc1 = ti * 128, (ti + 1) * 128
                proj_tm(ti, OFF_BV,
                        lambda b: P.add("dve", lambda e: e.tensor_copy(out=vb[:, :], in_=pP[:, b * 512:(b + 1) * 512]),
                                        r=["pP%d" % b], w=["vb"]))
                if fl:
                    proj_tm(ti, OFF_G,
                            lambda b: P.add("act", lambda e: e.activation(out=sg[:, :], in_=pP[:, b * 512:(b + 1) * 512], func=AF.Silu),
                                            r=["pP%d" % b], w=["sg"]))
                yield ("signal", "h_gla")
                Ept, Epn = EpS[ti % 2]
                Emt, Emn = EmS[ti % 2]
                if (g, ti) in gate_done:
                    yield ("signal", "lr_read")
                else:
                    for r_ in tile_gate(ti, inline=True):
                        yield r_
                yield ("signal", "gate_free")
                for c in range(2):
                    P.add("dve", lambda e, c=c: e.scalar_tensor_tensor(
                        out=kdec[:, c, :], in0=kT[:, c, tc0:tc1], scalar=Ept[:, c, 127:128], in1=Emt[:, c, :],
                        op0=ALU.mult, op1=ALU.mult), r=["kT", Epn, Emn], w=["kdec"])
                yield
                if fl:
                    for h in range(4):
                        c, r0 = h // 2, (h % 2) * 64
                        P.add("dve", lambda e, h=h, c=c, r0=r0: e.scalar_tensor_tensor(
                            out=qd[r0:r0 + 64, h, :], in0=qT[r0:r0 + 64, c, tc0:tc1], scalar=0.125, in1=Ept[r0:r0 + 64, c, :],
                            op0=ALU.mult, op1=ALU.mult), r=["qT", Epn], w=["qd"])
                    P.add("pool", lambda e: e.tensor_tensor(
                        out=kd[:, :, :], in0=kT[:, :, tc0:tc1], in1=Emt[:, :, :], op=ALU.mult), r=["kT", Emn], w=["kd"])
                yield ("signal", "kq_read")
                for c in range(2):
                    P.add("pe", lambda e, c=c: e.transpose(pT[:, c * 128:(c + 1) * 128], kdec[:, c, :], ident[:, :]),
                          r=["kdec", "ident"], w=["pT%d" % c])
                P.add("dve", lambda e: e.tensor_copy(out=kdecT[:, :], in_=pT[:, 0:256]), r=["pT0", "pT1"], w=["kdecT"])
                yield
                if fl:
                    for h in range(4):
                        c = h // 2
                        P.add("pe", lambda e, h=h, c=c: e.matmul(pA[:, h * 128:(h + 1) * 128], lhsT=kd[:, c, :],
                                                                 rhs=qd[:, h, :], start=True, stop=True),
                              r=["kd", "qd"], w=["pA"])
                    P.add("dve", lambda e: e.tensor_tensor(out=attnT[:, :, :], in0=pA[:, :].rearrange("p (h c) -> p h c", h=4),
                                                           in1=maskU[:, :].unsqueeze(1).to_broadcast([128, 4, 128]), op=ALU.mult),
                          r=["pA", "maskU"], w=["attnT"])
                    yield
                    for h in range(4):
                        c = h // 2
                        P.add("pe", lambda e, h=h: e.matmul(pO[:, h * 128:(h + 1) * 128], lhsT=attnT[:, h, :],
                                                            rhs=vb[:, h * 128:(h + 1) * 128], start=True, stop=False),
                              r=["attnT", "vb"], w=["pO"])
                        P.add("pe", lambda e, h=h, c=c: e.matmul(pO[:, h * 128:(h + 1) * 128], lhsT=qd[:, h, :],
                                                                 rhs=Sb[:, c, :], start=False, stop=True),
                              r=["qd", "Sb"], w=["pO"])
                    yield
                for h in range(4):
                    c = h // 2
                    P.add("pe", lambda e, h=h, c=c: e.matmul(pS[:, h * 128:(h + 1) * 128], lhsT=kdecT[:, c * 128:(c + 1) * 128],
                                                             rhs=vb[:, h * 128:(h + 1) * 128], start=True, stop=True),
                          r=["kdecT", "vb"], w=["pS"])
                yield
                for h in range(4):
                    c, r0 = h // 2, (h % 2) * 64
                    P.add("dve", lambda e, h=h, c=c, r0=r0: e.scalar_tensor_tensor(
                        out=S[r0:r0 + 64, c, :], in0=S[r0:r0 + 64, c, :], scalar=Ept[r0:r0 + 64, c, 127:128],
                        in1=pS[r0:r0 + 64, h * 128:(h + 1) * 128], op0=ALU.mult, op1=ALU.add),
                        r=["S%d" % c, Epn, "pS"], w=["S%d" % c])
                P.add("pool", lambda e: e.tensor_copy(out=Sb[:, :, :], in_=S[:, :, :]), r=["S0", "S1"], w=["Sb"])
                yield
                if not fl:
                    return
                for h in range(4):
                    P.add("act", lambda e, h=h: e.activation(out=ob[:, h * 128:(h + 1) * 128], in_=pO[:, h * 128:(h + 1) * 128],
                                                             func=AF.Square, accum_out=st[:, 4 + h:5 + h]),
                          r=["pO"], w=["ob", "st4"])
                yield
                P.add("act", lambda e: e.activation(out=st[:, 8:12], in_=st[:, 4:8], func=AF.Ln, scale=1.0 / 128, bias=EPS),
                      r=["st4"], w=["st8"])
                P.add("act", lambda e: e.activation(out=st[:, 12:16], in_=st[:, 8:12], func=AF.Exp, scale=-0.5),
                      r=["st8"], w=["st12"])
                yield
                for h in range(4):
                    P.add("dve", lambda e, h=h: e.scalar_tensor_tensor(
                        out=ob[:, h * 128:(h + 1) * 128], in0=pO[:, h * 128:(h + 1) * 128], scalar=st[:, 12 + h:13 + h],
                        in1=sg[:, h * 128:(h + 1) * 128], op0=ALU.mult, op1=ALU.mult),
                        r=["pO", "st12", "sg"], w=["ob"])
                yield
                for h in range(4):
                    P.add("pe", lambda e, h=h: e.transpose(pT[:, 256 + h * 128:256 + (h + 1) * 128], ob[:, h * 128:(h + 1) * 128], ident[:, :]),
                          r=["ob", "ident"], w=["pT%d" % (2 + h)])
                P.add("act", lambda e: e.activation(out=mixT[:, 4:8, :], in_=pT[:, 256:768].rearrange("p (h c) -> p h c", h=4), func=AF.Copy),
                      r=["pT2", "pT3", "pT4", "pT5"], w=["mixTb"])
                yield

            def tile_sgu(ti):
                tc0, tc1 = ti * 128, (ti + 1) * 128
                proj_tm(ti, OFF_V,
                        lambda b: P.add("act", lambda e: e.activation(out=va[:, :], in_=pP[:, b * 512:(b + 1) * 512], func=AF.Gelu),
                                        r=["pP%d" % b], w=["va"]))
                yield ("signal", "h_sgu")
                P.add("dve", lambda e: e.bn_stats(out=bst[:, 0:6], in_=va[:, :]), r=["va"], w=["bst"])
                P.add("dve", lambda e: e.bn_aggr(out=mv[:, 0:2], in_=bst[:, 0:6]), r=["bst"], w=["mv"])
                yield
                P.add("act", lambda e: e.activation(out=st[:, 3:4], in_=mv[:, 1:2], func=AF.Ln, bias=EPS), r=["mv"], w=["st3a"])
                P.add("act", lambda e: e.activation(out=st[:, 3:4], in_=st[:, 3:4], func=AF.Exp, scale=-0.5), r=["st3a"], w=["st3"])
                P.add("dve", lambda e: e.scalar_tensor_tensor(out=vtmp[:, :], in0=va[:, :], scalar=mv[:, 0:1], in1=lnG[:, :],
                                                              op0=ALU.subtract, op1=ALU.mult),
                      r=["va", "mv", "lnG"], w=["vtmp"])
                yield
                P.add("act", lambda e: e.activation(out=vng[:, :], in_=vtmp[:, :], func=AF.Copy, scale=st[:, 3:4]),
                      r=["vtmp", "st3"], w=["vng"])
                yield
                for h in range(4):
                    P.add("pe", lambda e, h=h: e.matmul(pZ[:, h * 128:(h + 1) * 128], lhsT=vng[:, h * 128:(h + 1) * 128],
                                                        rhs=WmT[:, h, :], start=True, stop=False),
                          r=["vng", "WmT"], w=["pZ"])
                    P.add("pe", lambda e, h=h: e.matmul(pZ[:, h * 128:(h + 1) * 128], lhsT=lhsT2[0:2, h * 128:(h + 1) * 128],
                                                        rhs=rhs2[0:2, h * 128:(h + 1) * 128], start=False, stop=True),
                          r=["lhsT2", "rhs2r0", "rhs2r1"], w=["pZ"])
                yield
                P.add("dve", lambda e: e.tensor_tensor(
                    out=mixT[:, 0:4, :], in0=pZ[:, :].rearrange("p (h c) -> p h c", h=4), in1=uT[:, :, tc0:tc1], op=ALU.mult),
                    r=["pZ", "uT"], w=["mixTa"])
                yield ("signal", "u_read")

            def tile_out(ti, g=g):
                tau = 4 * g + ti
                for half in range(2):
                    for kc in range(8):
                        P.add("pe", lambda e, kc=kc, half=half: e.matmul(pP[:, half * 512:(half + 1) * 512], lhsT=mixT[:, kc, :],
                                                                         rhs=wout[:, kc, half * 512:(half + 1) * 512],
                                                                         start=(kc == 0), stop=(kc == 7)),
                              r=["mixTa", "mixTb", "wout%d" % kc], w=["pP%d" % half])
                P.add("act", lambda e: e.activation(out=junk[:, :], in_=pP[:, :], func=AF.Square, accum_out=st[:, 0:1]),
                      r=["pP0", "pP1"], w=["junk", "st0"])
                P.add("act", lambda e: e.activation(out=st[:, 1:2], in_=st[:, 0:1], func=AF.Ln, scale=1.0 / D, bias=EPS),
                      r=["st0"], w=["st1"])
                P.add("act", lambda e: e.activation(out=st[:, 2:3], in_=st[:, 1:2], func=AF.Exp, scale=-0.5),
                      r=["st1"], w=["st2"])
                P.add("dve", lambda e: e.scalar_tensor_tensor(out=t1[:, :], in0=pP[:, :], scalar=st[:, 2:3], in1=gpost[:, :],
                                                              op0=ALU.mult, op1=ALU.mult),
                      r=["pP0", "pP1", "st2", "gpost"], w=["t1"])
                yield
                slot = tau - (NPT - 1)
                P.add("sp", lambda e: e.dma_start(out=x1all[:, slot, :], in_=dr["xp"][tau * 128:(tau + 1) * 128, :]),
                      w=["x1_%d" % slot], dma="x1ld%d" % slot)
                P.add("pool", lambda e: e.tensor_tensor(out=x1all[:, slot, :], in0=x1all[:, slot, :], in1=t1[:, :], op=ALU.add),
                      r=["x1_%d" % slot, "t1"], w=["x1_%d" % slot])
                yield

            def run_many(gens):
                gens = [x for x in gens if x is not None]
                while gens:
                    for x in list(gens):
                        try:
                            next(x)
                        except StopIteration:
                            gens.remove(x)

            for ti in range(4):
                fl = full[ti]
                th = [pending_out[0], tile_gla(ti, fl), tile_sgu(ti) if fl else None]
                if ti + 1 <= 3:
                    th.append(tile_gate(ti + 1))
                if ti == 3 and g + 1 < NG:
                    th.append(group_head(g + 1))
                run_sched(th)
                pending_out[0] = tile_out(ti) if fl else None
        run_sched([pending_out[0]])
        if n_pg > 0:
            P.add("sp", lambda e: e.nop(), r=["scrw%d" % u for u in range(NJ + NJ // 2)], w=[], force=True)
        if dbg is not None:
            for slot in range(NOT + 1):
                P.add("sp", lambda e, slot=slot: e.dma_start(out=dbg[slot * 128:(slot + 1) * 128, :], in_=x1all[:, slot, :]),
                      r=["x1_%d" % slot], w=["dbg%d" % slot], dma="dbg", force=True)
            P.add("sp", lambda e: e.nop(), r=["dbg%d" % s for s in range(NOT + 1)], w=[], force=True)
        build_block(nc, P)


def _phase_F(nc, dr, x1all, NOT, out):
    NGF = NOT // 4
    scr = dr["scr_up"]
    LIMIT[0] = int(os.environ.get("MK_LIMIT_F", "100000000"))
    P = Prog()
    with contextlib.ExitStack() as st_:
        def sb(name, shape, dt):
            return st_.enter_context(nc.sbuf_tensor("sb_" + name, shape, dt))

        def ps(name, shape, dt):
            return st_.enter_context(nc.psum_tensor("ps_" + name, shape, dt))

        wdn = sb("wdn", [128, NJ, D], BF16)
        NWU = 3
        wu = [sb("wu%d" % i, [128, 8, 256], BF16) for i in range(NWU)]
        GT = sb("GT", [128, NJ, 512], BF16)
        h2T = [sb("h2T%d" % i, [128, 8, 514], BF16) for i in range(2)]
        NHB = 2
        hb = [sb("hbF%d" % i, [128, D], BF16) for i in range(NHB)]
        NY = 3
        ya = [sb("ya%d" % i, [128, 512], F32) for i in range(NY)]
        yb = [sb("yb%d" % i, [128, 512], F32) for i in range(NY)]
        ga = [sb("ga%d" % i, [128, 512], F32) for i in range(NY)]
        t1 = [sb("t1F%d" % i, [128, D], F32) for i in range(2)]
        junk = sb("junkF", [128, D], BF16)
        st = [sb("stF%d" % i, [128, 8], F32) for i in range(NHB)]
        st2 = [sb("stG%d" % i, [128, 8], F32) for i in range(2)]
        Hs = [sb("Hs%d" % i, [128, 2 * NJ, 2], F32) for i in range(2)]
        corr = [sb("corr%d" % i, [128, 8], F32) for i in range(NY)]
        ident = sb("identF", [128, 128], BF16)
        cw = sb("cw", [128, 2 * NJ, 3], F32)
        cb = sb("cb", [128, 2 * NJ], F32)
        gffn = sb("gffn", [128, D], F32)
        gb = sb("gbF", [128, D], F32)

        pU = [ps("pU%d" % i, [128, 1024], F32) for i in range(2)]
        pF = ps("pF", [128, 1024], F32)
        pH = ps("pH", [128, 512], F32)
        pT = ps("pTF", [128, 1024], BF16)

        _consts(P, ident)
        P.add("sp", lambda e: e.dma_start(out=cw[:, :, :], in_=dr["cw"][:, :, :]), w=["cw"], dma="f0")
        P.add("sp", lambda e: e.dma_start(out=cb[:, :], in_=dr["cb"][:, :]), w=["cb"], dma="f1")
        P.add("sp", lambda e: e.dma_start(out=gffn[:, :], in_=dr["gffn"][0:1, :].partition_broadcast(128)), w=["gffn"], dma="f2")
        P.add("sp", lambda e: e.dma_start(out=gb[:, :], in_=dr["g2"][0:1, :].partition_broadcast(128)), w=["gb"], dma="f3")

        hcnt = [0]

        def prenorm_a(slot):
            k = hcnt[0] % NHB
            hcnt[0] += 1
            _prenorm_a(P, x1all[:, slot, :], "x1_%d" % slot, gb, st[k], hb[k], sfx="F%d" % k)
            return k

        def prenorm_b(k, dst3, dst_res):
            _prenorm_b(P, hb[k], pT, ident, dst3, dst_res, sfx="F%d" % k)

        k = prenorm_a(0)
        prenorm_b(k, h2T[0][:, :, 2:130], "h2T0_0")
        P.add("pool", lambda e: e.tensor_copy(out=h2T[0][:, :, 0:2], in_=h2T[0][:, :, 128:130]), r=["h2T0_0"], w=["h2Th"])
        for ti in range(4):
            k = prenorm_a(1 + ti)
            prenorm_b(k, h2T[0][:, :, 2 + ti * 128:2 + (ti + 1) * 128], "h2T0_%d" % ti)
        def load_wdn(jp):
            P.add("sp", lambda e: e.dma_start(out=wdn[:, 2 * jp:2 * jp + 2, :], in_=dr["scr_dn"][jp, :, :, :]),
                  w=["wdn%d" % (2 * jp), "wdn%d" % (2 * jp + 1)], dma="wdn%d" % jp)
        wcnt = [0]
        ycnt = [0]
        ecnt = [0]
        for g in range(NGF):
            hp = g % 2
            hT_ = h2T[hp]
            h2res = ["h2T%d_%d" % (hp, t) for t in range(4)]
            for j in range(NJ):
                ws = wcnt[0] % NWU
                wcnt[0] += 1
                pu = pU[j % 2]
                pur = "pU%d" % (j % 2)
                wur = ["wu%d.a" % ws, "wu%d.b" % ws]
                P.add("sp", lambda e, j=j, ws=ws: e.dma_start(out=wu[ws][:, :, :], in_=scr[j, :, :, :]),
                      w=wur, dma="wu%d" % ws)
                if g == 0 and 2 <= j < 2 + NJ // 2:
                    load_wdn(j - 2)
                for half in range(2):
                    for kc in range(8):
                        P.add("pe", lambda e, kc=kc, half=half, ws=ws, pu=pu, hT_=hT_: e.matmul(
                            pu[:, half * 512:(half + 1) * 512], lhsT=wu[ws][:, kc, half * 128:(half + 1) * 128],
                            rhs=hT_[:, kc, 2:514], start=(kc == 0), stop=(kc == 7)),
                            r=wur + h2res, w=[pur + ".%d" % half])
                    if g == 0:
                        for kc in range(8):
                            P.add("pe", lambda e, kc=kc, half=half, ws=ws, hT_=hT_: e.matmul(
                                pH[:, half * 2:half * 2 + 2], lhsT=wu[ws][:, kc, half * 128:(half + 1) * 128],
                                rhs=hT_[:, kc, 0:2], start=(kc == 0), stop=(kc == 7)),
                                r=wur + ["h2Th"], w=["pH.%d" % half])
                ys = ycnt[0] % NY
                ycnt[0] += 1
                for half, y in ((0, ya[ys]), (1, yb[ys])):
                    ci = half * NJ + j
                    yr = "y%d_%d" % (half, ys)
                    src = pu[:, half * 512:(half + 1) * 512]
                    sr = pur + ".%d" % half
                    if g == 0:
                        hsrc = pH[:, half * 2:half * 2 + 2]
                        hres = "pH.%d" % half
                    else:
                        hsrc = Hs[g % 2][:, ci, :]
                        hres = "Hs%d" % (g % 2)
                    P.add("act", lambda e, y=y, src=src, ci=ci: e.activation(out=y[:, :], in_=src, func=AF.Identity,
                                                                              scale=cw[:, ci, 2:3], bias=cb[:, ci:ci + 1]),
                          r=[sr, "cw", "cb"], w=[yr])
                    if g < NGF - 1:
                        P.add("act", lambda e, src=src, ci=ci, g=g: e.activation(out=Hs[(g + 1) % 2][:, ci, :], in_=src[:, 510:512], func=AF.Copy),
                              r=[sr], w=["Hs%d" % ((g + 1) % 2)])
                    P.add("dve", lambda e, y=y, src=src, ci=ci: e.scalar_tensor_tensor(
                        out=y[:, 1:512], in0=src[:, 0:511], scalar=cw[:, ci, 1:2], in1=y[:, 1:512], op0=ALU.mult, op1=ALU.add),
                        r=[sr, "cw", yr], w=[yr])
                    P.add("dve", lambda e, y=y, src=src, ci=ci: e.scalar_tensor_tensor(
                        out=y[:, 2:512], in0=src[:, 0:510], scalar=cw[:, ci, 0:1], in1=y[:, 2:512], op0=ALU.mult, op1=ALU.add),
                        r=[sr, "cw", yr], w=[yr])
                    if g == 0:
                        P.add("dve", lambda e, y=y, hsrc=hsrc, ci=ci: e.scalar_tensor_tensor(
                            out=y[:, 0:1], in0=hsrc[:, 1:2], scalar=cw[:, ci, 1:2], in1=y[:, 0:1], op0=ALU.mult, op1=ALU.add),
                            r=[hres, "cw", yr], w=[yr])
                        P.add("dve", lambda e, y=y, hsrc=hsrc, ci=ci: e.scalar_tensor_tensor(
                            out=y[:, 0:2], in0=hsrc[:, 0:2], scalar=cw[:, ci, 0:1], in1=y[:, 0:2], op0=ALU.mult, op1=ALU.add),
                            r=[hres, "cw", yr], w=[yr])
                    else:
                        cc = corr[ys][:, half * 2:half * 2 + 2]
                        c2 = corr[ys][:, 4 + half:5 + half]
                        cr = "corr%d.%d" % (ys, half)
                        P.add("pool", lambda e, cc=cc, hsrc=hsrc, ci=ci: e.tensor_scalar(out=cc, in0=hsrc[:, 0:2], scalar1=cw[:, ci, 0:1], scalar2=None, op0=ALU.mult),
                              r=[hres, "cw"], w=[cr])
                        P.add("pool", lambda e, c2=c2, hsrc=hsrc, ci=ci: e.tensor_scalar(out=c2, in0=hsrc[:, 1:2], scalar1=cw[:, ci, 1:2], scalar2=None, op0=ALU.mult),
                              r=[hres, "cw"], w=[cr + "b"])
                        P.add("pool", lambda e, cc=cc, c2=c2: e.tensor_tensor(out=cc[:, 0:1], in0=cc[:, 0:1], in1=c2, op=ALU.add),
                              r=[cr, cr + "b"], w=[cr])
                        P.add("dve", lambda e, y=y, cc=cc: e.tensor_tensor(out=y[:, 0:2], in0=y[:, 0:2], in1=cc, op=ALU.add),
                              r=[cr, yr], w=[yr])
                P.add("act", lambda e, ys=ys: e.activation(out=ga[ys][:, :], in_=ya[ys][:, :], func=AF.Gelu_apprx_tanh),
                      r=["y0_%d" % ys], w=["ga%d" % ys])
                P.add("pool", lambda e, j=j, ys=ys: e.tensor_tensor(out=GT[:, j, :], in0=ga[ys][:, :], in1=yb[ys][:, :], op=ALU.mult),
                      r=["ga%d" % ys, "y1_%d" % ys], w=["GT%d" % j])
            for ti in range(4):
                slot = 1 + 4 * g + ti
                nk = None
                if g + 1 < NGF:
                    nk = prenorm_a(1 + 4 * (g + 1) + ti)
                pf, pfr = [(pF, "pF"), (pU[0], "pU0."), (pU[1], "pU1.")][ti % 3]
                for half in range(2):
                    for j in range(NJ):
                        P.add("pe", lambda e, j=j, half=half, ti=ti, pf=pf: e.matmul(
                            pf[:, half * 512:(half + 1) * 512], lhsT=GT[:, j, ti * 128:(ti + 1) * 128],
                            rhs=wdn[:, j, half * 512:(half + 1) * 512], start=(j == 0), stop=(j == NJ - 1)),
                            r=["GT%d" % j, "wdn%d" % j], w=[pfr + "%d" % half])
                if nk is not None:
                    prenorm_b(nk, h2T[1 - hp][:, :, 2 + ti * 128:2 + (ti + 1) * 128], "h2T%d_%d" % (1 - hp, ti))
                es = ecnt[0] % 2
                sg_ = st2[es]
                ecnt[0] += 1
                P.add("act", lambda e, sg_=sg_, pf=pf: e.activation(out=junk[:, :], in_=pf[:, :], func=AF.Square, accum_out=sg_[:, 0:1]),
                      r=[pfr + "0", pfr + "1"], w=["junk", "sg0_%d" % es])
                P.add("act", lambda e, sg_=sg_: e.activation(out=sg_[:, 1:2], in_=sg_[:, 0:1], func=AF.Ln, scale=1.0 / D, bias=EPS),
                      r=["sg0_%d" % es], w=["sg1_%d" % es])
                P.add("act", lambda e, sg_=sg_: e.activation(out=sg_[:, 2:3], in_=sg_[:, 1:2], func=AF.Exp, scale=-0.5),
                      r=["sg1_%d" % es], w=["sg2_%d" % es])
                P.add("dve", lambda e, sg_=sg_, es=es, pf=pf: e.scalar_tensor_tensor(out=t1[es][:, :], in0=pf[:, :], scalar=sg_[:, 2:3], in1=gffn[:, :],
                                                                                     op0=ALU.mult, op1=ALU.mult),
                      r=[pfr + "0", pfr + "1", "sg2_%d" % es, "gffn"], w=["t1_%d" % es])
                P.add("pool", lambda e, slot=slot, es=es: e.tensor_tensor(out=x1all[:, slot, :], in0=x1all[:, slot, :], in1=t1[es][:, :], op=ALU.add),
                      r=["x1_%d" % slot, "t1_%d" % es], w=["x1_%d" % slot])
                P.add("sp", lambda e, slot=slot: e.dma_start(out=out[(slot - 1) * 128:slot * 128, :], in_=x1all[:, slot, :]),
                      r=["x1_%d" % slot], w=["out%d" % slot], dma="out", force=True)
        fin = P.add("sp", lambda e: e.nop(), r=["out%d" % s for s in range(1, NOT + 1)], w=[], force=True)
        build_block(nc, P)


def build_program(NOT=16, phase="all"):
    NT = 4 * NOT
    nc = bass.Bass("TRN2", target_bir_lowering=False)
    dr = {}

    def din(name, shape):
        dr[name] = nc.dram_tensor(name, shape, F32, kind="ExternalInput").ap()

    din("xp", [NT * 128, D])
    din("w_in_r", [128, 8, DIN])
    din("w_out_r", [128, 8, D])
    din("w_up_r", [NJ, 128, 8, 256])
    din("w_down_r", [128, NJ, D])
    for nm in ("gpre", "gpost", "g2", "gffn"):
        din(nm, [1, D])
    for nm in ("lng", "lnb4", "sgu_b4", "gng4"):
        din(nm, [1, 512])
    din("sgu_w", [4, 128, 128])
    din("wgk17", [17, 256])
    din("cw", [128, 2 * NJ, 3])
    din("cb", [128, 2 * NJ])
    din("gpre8", [128, 8])
    din("gng1", [128, 1])
    if phase == "M":
        out = nc.dram_tensor("out", [(NOT + 1) * 128, D], F32, kind="ExternalOutput").ap()
    else:
        out = nc.dram_tensor("out", [NOT * 128, D], F32, kind="ExternalOutput").ap()
    dr["scr_up"] = nc.dram_tensor("wup_bf16_scr", [NJ, 128, 8, 256], BF16, kind="Internal").ap()
    dr["scr_dn"] = nc.dram_tensor("wdn_bf16_scr", [NJ // 2, 128, 2, D], BF16, kind="Internal").ap()
    with nc.sbuf_tensor("x1all", [128, max(NOT + 1, 15), D], F32) as x1all:
        _phase_M(nc, dr, x1all, NOT, dbg=out if phase == "M" else None)
        if phase != "M":
            _phase_F(nc, dr, x1all, NOT, out)
    return nc


def make_in_maps(inp, n_seg=4):
    x = np.asarray(inp["x"], dtype=np.float32)
    B, S, _ = x.shape
    seg = S // n_seg
    f = lambda a: np.ascontiguousarray(np.asarray(a, dtype=np.float32))
    w_in = f(inp["w_in"])[0]
    w_out = f(inp["w_out"])[0]
    w_up = f(inp["w_up"])[0]
    w_down = f(inp["w_down"])[0]
    shared = {
        "w_in_r": f(w_in.reshape(8, 128, DIN).transpose(1, 0, 2)),
        "w_out_r": f(w_out.reshape(8, 128, D).transpose(1, 0, 2)),
        "w_up_r": f(w_up.reshape(8, 128, 2, NJ, 128).transpose(3, 1, 0, 2, 4).reshape(NJ, 128, 8, 256)),
        "w_down_r": f(w_down.reshape(NJ, 128, D).transpose(1, 0, 2)),
        "gpre": f(inp["norm_mix_pre"]).reshape(1, D),
        "gpre8": f(f(inp["norm_mix_pre"]).reshape(8, 128).T),
        "gng1": f(f(inp["gla_norm_g"]).reshape(128, 1)),
        "gpost": f(inp["norm_mix_post"]).reshape(1, D),
        "g2": f(inp["norm_ffn_pre"]).reshape(1, D),
        "gffn": f(inp["norm_ffn_post"]).reshape(1, D),
        "lng": f(inp["sgu_ln_g"]).reshape(1, 512),
        "lnb4": f(inp["sgu_ln_b"]).reshape(1, 512),
        "sgu_b4": f(inp["sgu_b"]).reshape(1, 512),
        "gng4": f(np.tile(f(inp["gla_norm_g"]).reshape(128), 4)).reshape(1, 512),
        "sgu_w": f(inp["sgu_w_s"])[0],
        "wgk17": f(np.concatenate([f(inp["gla_w_gk"])[0], f(inp["gla_b_gk"]).reshape(1, 256)], axis=0)),
        "cw": f(f(inp["conv_w"])[0].reshape(3, 2 * NJ, 128).transpose(2, 1, 0)),
        "cb": f(f(inp["conv_b"])[0].reshape(2 * NJ, 128).transpose(1, 0)),
    }
    maps = []
    for b in range(B):
        for s in range(n_seg):
            xp = np.zeros((n_seg * seg, D), np.float32)
            xp[(n_seg - 1 - s) * seg:] = x[b, :(s + 1) * seg]
            m = dict(shared)
            m["xp"] = xp
            maps.append(m)
    return maps, B, S, seg


def kernel(**inputs):
    maps, B, S, seg = make_in_maps(inputs)
    nc = build_program(NOT=seg // 128)
    res = run_bass_kernel_spmd(nc, maps, core_ids=list(range(len(maps))))
    out = np.zeros((B, S, D), np.float32)
    i = 0
    for b in range(B):
        for s in range(S // seg):
            out[b, s * seg:(s + 1) * seg] = res.results[i]["out"]
            i += 1
    return out
```
